# Optimizing a Trainium2 kernel written in Bass

```python
import jax, jax.numpy as jnp
from jax import lax
import numpy as np

D_MODEL = 1024
BATCH = 8
SEQ = 8192
DEPTH = 2

RET_HEADS = 4
RET_HEAD_DIM = 128
RET_WIDTH = RET_HEADS * RET_HEAD_DIM
RET_CHUNK = 128
ROPE_BASE = 10000.0
HGRN_HEADS = 4
HGRN_HEAD_DIM = 128
HGRN_WIDTH = HGRN_HEADS * HGRN_HEAD_DIM
HGRN_CHUNK = 32
CONV_WIDTH = 512
CONV_KERNEL = 31
N_BRANCH = 3
D_FF = 2816
FFN_CONV_KERNEL = 3
ALPHA = (2.0 * DEPTH) ** 0.25
BETA = (8.0 * DEPTH) ** -0.25
LN_EPS = 1e-5
IN_SEGMENTS = (RET_WIDTH,) * 4 + (HGRN_WIDTH,) * 4 + (CONV_WIDTH,) * 2 + (N_BRANCH * D_MODEL,)
D_IN = sum(IN_SEGMENTS)
VALUE_SEGMENTS = (2, 6, 8)

kernel_name = "hybrid_retention_hgrn2_conformer_deepnorm"


def _layer_norm(x, g, b):
    xf = x.astype(jnp.float32)
    mu = xf.mean(-1, keepdims=True)
    var = jnp.square(xf - mu).mean(-1, keepdims=True)
    return ((xf - mu) * lax.rsqrt(var + LN_EPS)).astype(x.dtype) * g + b


def _causal_dwconv(u, w, b):
    k = w.shape[0]
    y = lax.conv_general_dilated(u, w[:, None, :].astype(u.dtype), window_strides=(1,),
                                 padding=[(k - 1, 0)], dimension_numbers=('NWC', 'WIO', 'NWC'),
                                 feature_group_count=u.shape[-1])
    return y + b


def _rope(t, positions):
    half = t.shape[-1] // 2
    inv = ROPE_BASE ** (-jnp.arange(half, dtype=jnp.float32) / half)
    ang = positions.astype(jnp.float32)[..., None] * inv
    cos, sin = jnp.cos(ang)[:, :, None, :], jnp.sin(ang)[:, :, None, :]
    t1, t2 = t[..., :half], t[..., half:]
    return jnp.concatenate([t1 * cos - t2 * sin, t1 * sin + t2 * cos], axis=-1)


def _to_chunks(t, c):
    b, s, h, d = t.shape
    return t.reshape(b, s // c, c, h, d).transpose(1, 0, 3, 2, 4)


def _from_chunks(t):
    n, b, h, c, d = t.shape
    return t.transpose(1, 0, 3, 2, 4).reshape(b, n * c, h, d)


def _retention(q, k, v):
    b, _, h, dk = q.shape
    dv = v.shape[-1]
    c = RET_CHUNK
    log_gamma = jnp.log1p(-jnp.exp2(-5.0 - jnp.arange(h, dtype=jnp.float32)))
    idx = jnp.arange(c, dtype=jnp.float32)
    rel = idx[:, None] - idx[None, :]
    decay_mask = jnp.where(rel >= 0, jnp.exp(log_gamma[:, None, None] * jnp.maximum(rel, 0.0)), 0.0)
    query_decay = jnp.exp(log_gamma[:, None] * (idx + 1.0))[:, :, None]
    key_decay = jnp.exp(log_gamma[:, None] * (c - 1.0 - idx))[:, :, None]
    chunk_decay = jnp.exp(log_gamma * c)[:, None, None]

    def step(state, inp):
        qc, kc, vc = inp
        scores = jnp.einsum('bhtd,bhsd->bhts', qc, kc) * decay_mask
        out = jnp.einsum('bhts,bhsv->bhtv', scores, vc)
        out = out + jnp.einsum('bhtd,bhdv->bhtv', qc, state) * query_decay
        state = chunk_decay * state + jnp.einsum('bhsd,bhsv->bhdv', kc * key_decay, vc)
        return state, out

    state0 = jnp.zeros((b, h, dk, dv), jnp.float32)
    _, out = lax.scan(step, state0, (_to_chunks(q, c), _to_chunks(k, c), _to_chunks(v, c)))
    return _from_chunks(out)


def _hgrn2(q, logf, k, v):
    b, _, h, dk = q.shape
    dv = v.shape[-1]
    c = HGRN_CHUNK
    causal = jnp.tril(jnp.ones((c, c), dtype=bool))[None, None, :, :, None]

    def step(state, inp):
        qc, gc, kc, vc = inp
        cum = jnp.cumsum(gc, axis=2)
        diff = cum[:, :, :, None, :] - cum[:, :, None, :, :]
        decay = jnp.exp(jnp.where(causal, diff, -jnp.inf))
        attn = jnp.einsum('bhtd,bhsd,bhtsd->bhts', qc, kc, decay)
        out = jnp.einsum('bhts,bhsv->bhtv', attn, vc)
        out = out + jnp.einsum('bhtd,bhdv->bhtv', qc * jnp.exp(cum), state)
        last = cum[:, :, -1:, :]
        state = jnp.exp(last)[:, :, 0, :, None] * state + \
            jnp.einsum('bhsd,bhsv->bhdv', kc * jnp.exp(last - cum), vc)
        return state, out

    state0 = jnp.zeros((b, h, dk, dv), jnp.float32)
    _, out = lax.scan(step, state0, (_to_chunks(q, c), _to_chunks(logf, c),
                                     _to_chunks(k, c), _to_chunks(v, c)))
    return _from_chunks(out)


def _token_mixer(h, positions, lb, w_in, ret_norm_g, hgrn_norm_g, conv_w, conv_b,
                 conv_ln_g, conv_ln_b, w_branch, w_out):
    bsz, s, _ = h.shape
    f32 = jnp.float32
    p = h @ w_in
    bounds = np.cumsum(IN_SEGMENTS)[:-1].tolist()
    rq, rk, rv, rg, gq, gf, gi, gg, ca, cb, gates = jnp.split(p, bounds, axis=-1)

    def heads(t, n):
        return t.reshape(bsz, s, n, -1).astype(f32)

    q = _rope(heads(rq, RET_HEADS), positions)
    k = _rope(heads(rk, RET_HEADS), positions) * (RET_HEAD_DIM ** -0.5)
    o = _retention(q, k, heads(rv, RET_HEADS))
    mu = o.mean(-1, keepdims=True)
    o = (o - mu) * lax.rsqrt(jnp.square(o - mu).mean(-1, keepdims=True) + LN_EPS)
    u_a = o.reshape(bsz, s, RET_WIDTH).astype(h.dtype) * ret_norm_g * jax.nn.silu(rg)

    lbh = lb.reshape(HGRN_HEADS, HGRN_HEAD_DIM)
    logf = jnp.logaddexp(jnp.log(lbh), jnp.log1p(-lbh) + jax.nn.log_sigmoid(heads(gf, HGRN_HEADS)))
    o = _hgrn2(jax.nn.silu(heads(gq, HGRN_HEADS)), logf, -jnp.expm1(logf), heads(gi, HGRN_HEADS))
    o = o * lax.rsqrt(jnp.square(o).mean(-1, keepdims=True) + LN_EPS)
    u_b = o.reshape(bsz, s, HGRN_WIDTH).astype(h.dtype) * hgrn_norm_g * jax.nn.silu(gg)

    u = _causal_dwconv(ca * jax.nn.sigmoid(cb), conv_w, conv_b)
    u_c = jax.nn.silu(_layer_norm(u, conv_ln_g, conv_ln_b))

    gates = jax.nn.sigmoid(gates.reshape(bsz, s, N_BRANCH, D_MODEL))
    y = gates[:, :, 0] * (u_a @ w_branch[0])
    y = y + gates[:, :, 1] * (u_b @ w_branch[1])
    y = y + gates[:, :, 2] * (u_c @ w_branch[2])
    return y @ w_out


def _conv_ffn(h, w_up, conv_w, conv_b, w_down):
    p = _causal_dwconv(h @ w_up, conv_w, conv_b)
    a, v = jnp.split(p, 2, axis=-1)
    return (jax.nn.silu(a) * v) @ w_down


def setup_inputs(seed: int = 0) -> dict:
    key = jax.random.key(seed)
    ks = jax.random.split(key, 24)
    f32 = jnp.float32
    nrm = lambda k, shape, s: jax.random.normal(k, shape, f32) * s
    col_scale = jnp.concatenate([jnp.full((n,), BETA if i in VALUE_SEGMENTS else 1.0, f32)
                                 for i, n in enumerate(IN_SEGMENTS)])
    positions = (jnp.arange(SEQ, dtype=jnp.int32)[None, :]
                 + jax.random.randint(ks[2], (BATCH, 1), 0, SEQ, dtype=jnp.int32))
    return {
        "x": nrm(ks[0], (BATCH, SEQ, D_MODEL), 1.0),
        "c": nrm(ks[1], (BATCH, D_MODEL), 1.0),
        "positions": positions,
        "w_ada": nrm(ks[3], (DEPTH, D_MODEL, 6 * D_MODEL), 0.5 * D_MODEL ** -0.5),
        "b_ada": nrm(ks[4], (DEPTH, 6 * D_MODEL), 0.01),
        "w_in": nrm(ks[5], (DEPTH, D_MODEL, D_IN), D_MODEL ** -0.5) * col_scale,
        "ret_norm_g": 1.0 + nrm(ks[6], (DEPTH, RET_WIDTH), 0.01),
        "hgrn_lb_logits": nrm(ks[7], (DEPTH, HGRN_WIDTH), 1.0),
        "hgrn_norm_g": 1.0 + nrm(ks[8], (DEPTH, HGRN_WIDTH), 0.01),
        "conv_w": nrm(ks[9], (DEPTH, CONV_KERNEL, CONV_WIDTH), CONV_KERNEL ** -0.5),
        "conv_b": nrm(ks[10], (DEPTH, CONV_WIDTH), 0.01),
        "conv_ln_g": 1.0 + nrm(ks[11], (DEPTH, CONV_WIDTH), 0.01),
        "conv_ln_b": nrm(ks[12], (DEPTH, CONV_WIDTH), 0.01),
        "w_branch": nrm(ks[13], (DEPTH, N_BRANCH, RET_WIDTH, D_MODEL), BETA * RET_WIDTH ** -0.5),
        "w_out": nrm(ks[14], (DEPTH, D_MODEL, D_MODEL), BETA * D_MODEL ** -0.5),
        "ln1_g": 1.0 + nrm(ks[15], (DEPTH, D_MODEL), 0.01),
        "ln1_b": nrm(ks[16], (DEPTH, D_MODEL), 0.01),
        "ffn_w_up": nrm(ks[17], (DEPTH, D_MODEL, 2 * D_FF), BETA * D_MODEL ** -0.5),
        "ffn_conv_w": nrm(ks[18], (DEPTH, FFN_CONV_KERNEL, 2 * D_FF), FFN_CONV_KERNEL ** -0.5),
        "ffn_conv_b": nrm(ks[19], (DEPTH, 2 * D_FF), 0.01),
        "ffn_w_down": nrm(ks[20], (DEPTH, D_FF, D_MODEL), BETA * D_FF ** -0.5),
        "ln2_g": 1.0 + nrm(ks[21], (DEPTH, D_MODEL), 0.01),
        "ln2_b": nrm(ks[22], (DEPTH, D_MODEL), 0.01),
    }


def reference(x, c, positions, w_ada, b_ada, w_in, ret_norm_g, hgrn_lb_logits, hgrn_norm_g,
              conv_w, conv_b, conv_ln_g, conv_ln_b, w_branch, w_out, ln1_g, ln1_b,
              ffn_w_up, ffn_conv_w, ffn_conv_b, ffn_w_down, ln2_g, ln2_b):
    lb_all = jnp.cumsum(jax.nn.softmax(hgrn_lb_logits.astype(jnp.float32), axis=0), axis=0)
    lb_all = lb_all - lb_all[0]
    cond = jax.nn.silu(c)
    for l in range(DEPTH):
        mod = cond @ w_ada[l] + b_ada[l]
        sh1, sc1, g1, sh2, sc2, g2 = [m[:, None, :] for m in jnp.split(mod, 6, axis=-1)]
        h = x * (1.0 + sc1) + sh1
        y = _token_mixer(h, positions, lb_all[l], w_in[l], ret_norm_g[l], hgrn_norm_g[l],
                         conv_w[l], conv_b[l], conv_ln_g[l], conv_ln_b[l], w_branch[l], w_out[l])
        x = _layer_norm(ALPHA * x + g1 * y, ln1_g[l], ln1_b[l])
        h = x * (1.0 + sc2) + sh2
        y = _conv_ffn(h, ffn_w_up[l], ffn_conv_w[l], ffn_conv_b[l], ffn_w_down[l])
        x = _layer_norm(ALPHA * x + g2 * y, ln2_g[l], ln2_b[l])
    return x
```

```python
import contextlib
import numpy as np
import concourse.bass as bass
import concourse.mybir as mybir
from concourse.bass_utils import run_bass_kernel_spmd

F32 = mybir.dt.float32
BF16 = mybir.dt.bfloat16
I32 = mybir.dt.int32
AF = mybir.ActivationFunctionType
ALU = mybir.AluOpType
AX = mybir.AxisListType


class _Rec:
    def __init__(self):
        self.call = None

    def __getattr__(self, name):
        def f(*a, **k):
            self.call = (name, a, k)
            return self
        return f


def _bind(fn):
    r = _Rec()
    fn(r)
    assert r.call is not None
    return r.call


class Sched:
    COMPUTE = ("pe", "act", "dve", "pool")
    ALL = ("pe", "act", "dve", "pool", "sp")

    def __init__(self, nc, stack, n_dma_sems=24, epoch=30000):
        self.nc = nc
        self.stack = stack
        self.epoch = epoch
        self.prog = {e: [] for e in self.ALL}
        self.sems = []
        self.sem_eng = {}
        self.cur = {}
        self.cnt = {}
        for e in self.COMPUTE:
            self.cur[e] = self._new_sem("c_" + e)
            self.cnt[e] = 0
        self.dma_sems = [self._new_sem("d%d" % i) for i in range(n_dma_sems)]
        self.dma_cnt = [0] * n_dma_sems
        self.dma_rr = 0
        self.seen = {e: {} for e in self.ALL}
        self.last_w = {}
        self.readers = {}
        self.n_wait = 0
        self.n_op = 0

    def _new_sem(self, name):
        h = self.stack.enter_context(self.nc.semaphore(name + "_%d" % len(self.sems)))
        self.sems.append(h)
        if name.startswith("c_"):
            self.sem_eng[len(self.sems) - 1] = name[2:]
        return len(self.sems) - 1

    def _deps(self, rd, wr):
        deps = {}
        def add(tok):
            if tok is None:
                return
            s, v = tok
            if deps.get(s, 0) < v:
                deps[s] = v
        for k in rd:
            add(self.last_w.get(k))
        for k in wr:
            add(self.last_w.get(k))
            for s, v in self.readers.get(k, {}).items():
                add((s, v))
        return deps

    def _emit_waits(self, eng, deps):
        for s, v in deps.items():
            if eng == "pe" and s == self.cur["pe"]:
                continue
            if self.seen[eng].get(s, 0) >= v:
                continue
            self.seen[eng][s] = v
            self.prog[eng].append(("wait", s, v))
            self.n_wait += 1

    def _record(self, tok, rd, wr):
        s, v = tok
        for k in wr:
            self.last_w[k] = tok
            self.readers[k] = {}
        for k in rd:
            r = self.readers.setdefault(k, {})
            if r.get(s, 0) < v:
                r[s] = v

    def op(self, eng, fn, rd=(), wr=(), sig=True):
        deps = self._deps(rd, wr)
        if eng != "pe":
            for k in rd:
                if isinstance(k, str) and k.startswith("ps") and k[2:].isdigit():
                    for s_, v_ in self.readers.get(k, {}).items():
                        if self.sem_eng.get(s_) != eng and deps.get(s_, 0) < v_:
                            deps[s_] = v_
        self._emit_waits(eng, deps)
        if sig and self.cnt[eng] >= self.epoch:
            self.cur[eng] = self._new_sem("c_" + eng)
            self.cnt[eng] = 0
        tok = (self.cur[eng], self.cnt[eng] + 1)
        if sig:
            self.cnt[eng] += 1
            self.prog[eng].append(("op", _bind(fn), tok[0], 1))
        else:
            self.prog[eng].append(("op", _bind(fn), None, 0))
        self._record(tok, rd, wr)
        self.n_op += 1
        return tok

    def dma(self, eng, fn, rd=(), wr=()):
        i = self.dma_rr
        self.dma_rr = (self.dma_rr + 1) % len(self.dma_sems)
        s = self.dma_sems[i]
        deps = self._deps(rd, wr)
        if self.dma_cnt[i] > 0:
            v = 16 * self.dma_cnt[i]
            if deps.get(s, 0) < v:
                deps[s] = v
        self._emit_waits(eng, deps)
        self.dma_cnt[i] += 1
        tok = (s, 16 * self.dma_cnt[i])
        self.prog[eng].append(("op", _bind(fn), s, 16))
        self._record(tok, rd, wr)
        return tok

    def wait_all(self, eng, toks):
        deps = {}
        for s, v in toks:
            if deps.get(s, 0) < v:
                deps[s] = v
        self._emit_waits(eng, deps)

    def final_tokens(self):
        toks = []
        for i, s in enumerate(self.dma_sems):
            if self.dma_cnt[i]:
                toks.append((s, 16 * self.dma_cnt[i]))
        return toks

    def emit(self):
        nc = self.nc
        prog = self.prog
        sems = self.sems

        def replay(eng_obj, lst):
            for it in lst:
                if it[0] == "wait":
                    eng_obj.wait_ge(sems[it[1]], it[2])
                else:
                    name, a, k = it[1]
                    ins = getattr(eng_obj, name)(*a, **k)
                    if it[2] is not None:
                        ins.then_inc(sems[it[2]], it[3])

        with nc.Block() as block:
            @block.sync
            def _(e):
                replay(e, prog["sp"])

            @block.scalar
            def _(e):
                replay(e, prog["act"])

            @block.vector
            def _(e):
                replay(e, prog["dve"])

            @block.gpsimd
            def _(e):
                replay(e, prog["pool"])

            @block.tensor
            def _(e):
                replay(e, prog["pe"])


D = 1024
DEPTH = 2
NH = 4
HD = 128
W = 512
D_IN = 8192
D_FF = 2816
NFC = 2 * D_FF // 128
CONV_K = 31
ALPHA = (2.0 * DEPTH) ** 0.25
LN_EPS = 1e-5
PI = float(np.pi)
TWO_PI = 2.0 * float(np.pi)
CW1 = float(np.float32(6.28125))
CW2 = float(np.float32(TWO_PI - 6.28125))

O_BADA, O_LN1G, O_LN1B, O_LN2G, O_LN2B = 0, 48, 56, 64, 72
O_CW, O_CB, O_CLG, O_CLB, O_LBL, O_FCW, O_FCB, NPP = 80, 204, 208, 212, 216, 224, 356, 400
C_MASK, C_QD, C_KD, C_BD, C_INV, C_SGN, C_D0, NCONST = 0, 512, 1024, 1028, 1156, 1157, 1158, 1158 + 512


def bcast_last(ap, n):
    return bass.AP(ap.tensor, ap.offset, [list(d) for d in ap.ap] + [[0, n]])


def bcast_mid(ap, n):
    d = [list(x) for x in ap.ap]
    return bass.AP(ap.tensor, ap.offset, [d[0], [0, n]] + d[1:])


def v3(ap, a):
    return ap.rearrange("p (a b) -> p a b", a=a)


class _StopBuild(Exception):
    pass


POOL_ENG = "pool"


def build(S_LEN, T=512, n_layers=DEPTH, dbg=(), upto=None):
    nc = bass.Bass("TRN2", target_bir_lowering=False)
    NT = S_LEN // T
    NB = T // 128
    L = n_layers

    def din(name, shape, dt=F32):
        return nc.dram_tensor(name, list(shape), dt, kind="ExternalInput").ap()

    x_d = din("x", [S_LEN, D])
    pos_d = din("pos", [1, S_LEN], I32)
    cvec_d = din("cvec", [128, 8])
    wada_d = din("w_ada", [DEPTH, D, 6 * D])
    win_d = din("w_in", [DEPTH, D, D_IN])
    wrot_d = din("w_rot", [DEPTH, D, 1024])
    wbr_d = din("w_branch", [DEPTH, 3 * W, D])
    wout_d = din("w_out", [DEPTH, D, D])
    wup_d = din("w_up", [DEPTH, D, 2 * D_FF])
    wdn_d = din("w_down", [DEPTH, D_FF, D])
    pp_d = din("pp", [DEPTH, 128, NPP])
    bc_d = din("bc", [DEPTH, 128, 1024])
    const_d = din("consts", [128, NCONST])
    out_d = nc.dram_tensor("out", [S_LEN, D], F32, kind="ExternalOutput").ap()

    winb = nc.dram_tensor("winb", [DEPTH, D, D_IN], BF16).ap()
    wrotb = nc.dram_tensor("wrotb", [DEPTH, D, 1024], BF16).ap()
    wbrb = nc.dram_tensor("wbrb", [DEPTH, 3 * W, D], BF16).ap()
    woutb = nc.dram_tensor("woutb", [DEPTH, D, D], BF16).ap()
    wupb = nc.dram_tensor("wupb", [DEPTH, D, 2 * D_FF], BF16).ap()
    wdnb = nc.dram_tensor("wdnb", [DEPTH, D_FF, D], BF16).ap()

    cdiag = nc.dram_tensor("cdiag", [DEPTH, 4, 128, CONV_K * 128], BF16).ap()

    dbg_outs = {}

    with contextlib.ExitStack() as st:
        S = Sched(nc, st)

        def sb(name, shape, dt=F32):
            return nc.alloc_sbuf_tensor("sb_" + name, list(shape), dt)

        xT = sb("xT", [128, 8, T])
        hT = sb("hT", [128, 8, T], BF16)
        xin = [sb("xin%d" % i, [128, D]) for i in range(2)]
        NSLOT = 4
        SLOT_E = 4096
        wslot = [sb("ws%d" % i, [128, SLOT_E], BF16) for i in range(NSLOT)]
        cosT = sb("cosT", [128, T])
        sinT = sb("sinT", [128, T])
        coskT = sb("coskT", [128, T])
        sinkT = sb("sinkT", [128, T])
        NTMP = 8
        tmps = [sb("tmp%d" % i, [128, T]) for i in range(NTMP)]
        Bb = [sb("B%d" % i, [128, 4, T], BF16) for i in range(6)]
        qmc = sb("qmc", [128, 4, 4, 128], BF16)
        kmc = sb("kmc", [128, 4, 4, 128], BF16)
        Y = [sb("Y%d" % i, [128, T]) for i in range(8)]
        uT = [sb("u%dT" % i, [128, 4, T], BF16) for i in range(3)]
        G = sb("G", [128, 4, T + 30], BF16)
        STm = sb("STm", [128, 4, 128], BF16)
        STm2 = sb("STm2", [128, 4, 128], BF16)
        ubuf = sb("ubuf", [128, 512], BF16)
        st4 = [sb("st4_%d" % i, [128, 4]) for i in range(6)]
        S_ret = [sb("Sret%d" % l, [128, 4, 128]) for l in range(L)]
        Sbf_ret = [sb("Sbfret%d" % l, [128, 4, 128], BF16) for l in range(L)]
        R_h = [sb("Rh%d" % l, [128, 4, 128]) for l in range(L)]
        Lprev = [sb("Lprev%d" % l, [128, 4]) for l in range(L)]
        Lc = sb("Lc", [128, 4, T // 32])
        halo31 = [sb("halo31_%d" % l, [128, 4, 30], BF16) for l in range(L)]
        halo3 = [sb("halo3_%d" % l, [128, NFC, 2]) for l in range(L)]
        pbuf = [sb("pbuf%d" % i, [128, T + 2]) for i in range(3)]
        pp = sb("pp", [128, DEPTH, NPP])
        bc = sb("bc", [128, 1, 1024])
        cst = sb("cst", [128, NCONST])
        cvec = sb("cvec_sb", [128, 8])
        cond = sb("cond", [128, 8])
        mod = sb("mod", [128, DEPTH, 48])
        onep = sb("onep", [128, DEPTH, 2, 8])
        lb = sb("lb", [128, DEPTH, 4])
        oml = sb("oml", [128, DEPTH, 4])
        identf = sb("identf", [128, 128])
        identb = sb("identb", [128, 128], BF16)
        onesA = sb("onesA", [128, 128])
        onesB = sb("onesB", [128, 128])
        PS = [nc.alloc_psum_tensor("ps%d" % i, [128, 512], F32) for i in range(8)]
        PSB = [p.bitcast(BF16) for p in PS]
        print("sbuf bytes remaining", nc.sbuf_bytes_remaining)

        state = {"tmp": 0, "pm": 0, "slot": 0, "pb": 0, "pm_excl": ()}

        def tmp():
            i = state["tmp"]
            state["tmp"] = (i + 1) % NTMP
            return tmps[i], "tmp%d" % i

        def pm():
            while True:
                i = state["pm"]
                state["pm"] = (i + 1) % 8
                if i not in state["pm_excl"]:
                    return i

        P_ST, P_DS, P_O, P_TP = 4, 5, 6, 7

        def psk(i):
            return "ps%d" % i

        def ACT(fn, rd, wr, sig=True):
            return S.op("act", fn, rd, wr, sig)

        def DVE(fn, rd, wr, sig=True):
            return S.op("dve", fn, rd, wr, sig)

        def POOL(fn, rd, wr, sig=True):
            return S.op(POOL_ENG, fn, rd, wr, sig)

        def PE(fn, rd, wr, sig=True):
            return S.op("pe", fn, rd, wr, sig)

        def mm(out_ap, out_key, terms):
            n = len(terms)
            for i, (l_, r_, keys) in enumerate(terms):
                PE(lambda e, l_=l_, r_=r_, i=i: e.matmul(out_ap, l_, r_, start=(i == 0), stop=(i == n - 1)),
                   rd=keys, wr=[out_key], sig=(i == n - 1))

        def load_w(src3, K, C, rdkeys, dt=BF16):
            i = state["slot"]
            state["slot"] = (i + 1) % NSLOT
            if dt == BF16:
                view = wslot[i][:, 0:K * C].rearrange("p (k c) -> p k c", k=K)
            else:
                view = wslot[i].bitcast(F32)[:, 0:K * C].rearrange("p (k c) -> p k c", k=K)
            key = "ws%d" % i
            S.dma("sp", lambda e: e.dma_start(out=view, in_=src3), rd=rdkeys, wr=[key])
            return view, key

        def dump(name, ap, shape, dt=F32, keys=()):
            if name not in dbg:
                return
            d = nc.dram_tensor("dbg_" + name, list(shape), dt, kind="ExternalOutput").ap()
            dbg_outs[name] = d
            S.dma("pool", lambda e: e.dma_start(out=d, in_=ap), rd=list(keys), wr=["dbg_" + name])

        def cast_w(dst, src, rows, name):
            ncol = dst.shape[-1]
            for l in range(L):
                for r0 in range(0, rows, 128):
                    for c0 in range(0, ncol, 512):
                        S.dma("pool", lambda e, l=l, r0=r0, c0=c0: e.dma_start(out=dst[l, r0:r0 + 128, c0:c0 + 512], in_=src[l, r0:r0 + 128, c0:c0 + 512]),
                              rd=[], wr=[(name, l, r0 // 128, c0 // 512)])

        def wkeys(name, l, rows, c0=0, c1=None):
            if c1 is None:
                c1 = c0 + 512
            return [(name, l, r, cc) for r in range(rows // 128) for cc in range(c0 // 512, (c1 + 511) // 512)]

        cast_w(winb, win_d, D, "winb")
        cast_w(wrotb, wrot_d, D, "wrotb")
        cast_w(wbrb, wbr_d, 3 * W, "wbrb")
        cast_w(woutb, wout_d, D, "woutb")
        cast_w(wupb, wup_d, D, "wupb")
        cast_w(wdnb, wdn_d, D_FF, "wdnb")

        S.dma("sp", lambda e: e.dma_start(out=cst[:], in_=const_d), wr=["cst"])
        S.dma("sp", lambda e: e.dma_start(out=cvec[:], in_=cvec_d), wr=["cvec"])
        for l in range(DEPTH):
            S.dma("sp", lambda e, l=l: e.dma_start(out=pp[:, l, :], in_=pp_d[l]), wr=["pp"], rd=["pp"])
        POOL(lambda e: e.memset(identf[:], 1.0), [], ["identf"])
        S.op("pool", lambda e: e.affine_select(out=identf[:], in_=identf[:], pattern=[[-1, 128]], compare_op=ALU.is_equal,
                                               fill=0.0, base=0, channel_multiplier=1), ["identf"], ["identf"])
        DVE(lambda e: e.tensor_copy(identb[:], identf[:]), ["identf"], ["identb"])
        POOL(lambda e: e.memset(onesA[:], 1.0 / D), [], ["onesA"])
        POOL(lambda e: e.memset(onesB[:], 1.0 / W), [], ["onesB"])
        POOL(lambda e: e.memset(qmc[:], 0.0), [], ["qmc"])
        POOL(lambda e: e.memset(kmc[:], 0.0), [], ["kmc"])
        for l in range(L):
            POOL(lambda e, l=l: e.memset(S_ret[l][:], 0.0), [], ["Sret%d" % l])
            POOL(lambda e, l=l: e.memset(Sbf_ret[l][:], 0.0), [], ["Sbfret%d" % l])
            POOL(lambda e, l=l: e.memset(R_h[l][:], 0.0), [], ["Rh%d" % l])
            POOL(lambda e, l=l: e.memset(Lprev[l][:], 1.0), [], ["Lprev%d" % l])
            POOL(lambda e, l=l: e.memset(halo31[l][:], 0.0), [], ["halo31_%d" % l])
            POOL(lambda e, l=l: e.memset(halo3[l][:], 0.0), [], ["halo3_%d" % l])

        ACT(lambda e: e.activation(cond[:], cvec[:], AF.Silu), ["cvec"], ["cond"])
        condr = sb("condr", [128, 8, 8])
        DVE(lambda e: e.tensor_copy(condr[:], bcast_last(cond[:], 8)), ["cond"], ["condr"])
        for l in range(L):
            pmi = pm()
            for g in range(24):
                src = wada_d[l].rearrange("(k p) c -> p k c", p=128)[:, :, g * 256:(g + 1) * 256]
                wv, wk = load_w(src, 8, 256, [], dt=F32)
                for j in range(2):
                    m = g * 2 + j
                    mm(PS[pmi][:, m * 8:(m + 1) * 8], psk(pmi),
                       [(wv[:, k, j * 128:(j + 1) * 128], condr[:, k, :], [wk, "condr"]) for k in range(8)])
            mt_, mtk = tmp()
            ACT(lambda e, pmi=pmi, mt_=mt_: e.copy(mt_[:, 0:384], PS[pmi][:, 0:384]), [psk(pmi)], [mtk])
            DVE(lambda e, l=l, mt_=mt_: e.tensor_tensor(mod[:, l, :], v3(mt_[:, 0:384], 48)[:, :, 0], pp[:, l, O_BADA:O_BADA + 48], ALU.add),
                [mtk, "pp"], ["mod"])
            DVE(lambda e, l=l: e.tensor_scalar(onep[:, l, 0, :], mod[:, l, 8:16], 1.0, None, ALU.add), ["mod"], ["onep"])
            DVE(lambda e, l=l: e.tensor_scalar(onep[:, l, 1, :], mod[:, l, 32:40], 1.0, None, ALU.add), ["mod", "onep"], ["onep"])
        DVE(lambda e: e.memset(lb[:], 0.0), [], ["lb"])
        if L > 1:
            DVE(lambda e: e.tensor_tensor(lb[:, 1, :], pp[:, 0, O_LBL + 4:O_LBL + 8], pp[:, 0, O_LBL:O_LBL + 4], ALU.subtract),
                ["pp", "lb"], ["lb"])
            ACT(lambda e: e.activation(lb[:, 1, :], lb[:, 1, :], AF.Sigmoid), ["lb"], ["lb"])
        DVE(lambda e: e.tensor_scalar(oml[:], lb[:], -1.0, 1.0, ALU.mult, ALU.add), ["lb"], ["oml"])
        for l in range(L):
            for j in range(4):
                i = state["slot"]
                state["slot"] = (i + 1) % NSLOT
                dv = wslot[i][:, 0:CONV_K * 128].rearrange("p (k c) -> p k c", k=CONV_K)
                DVE(lambda e, dv=dv, l=l, j=j: e.tensor_tensor(dv, bcast_mid(identf[:], CONV_K),
                                                               bcast_last(pp[:, l, O_CW + j * 31:O_CW + j * 31 + 31], 128), ALU.mult),
                    ["identf", "pp"], ["ws%d" % i])
                S.dma("pool", lambda e, i=i, l=l, j=j: e.dma_start(out=cdiag[l, j], in_=wslot[i][:, 0:CONV_K * 128]), rd=["ws%d" % i], wr=[("cdiag", l, j)])
        dump("mod", mod[:], [128, DEPTH, 48], keys=["mod"])
        dump("lb", lb[:], [128, DEPTH, 4], keys=["lb"])

        maskT = v3(cst[:, C_MASK:C_MASK + 512], 4)
        qdv = v3(cst[:, C_QD:C_QD + 512], 4)
        kdv = cst[:, C_KD:C_KD + 4]
        bdm = cst[:, C_BD:C_BD + 128]
        invf = cst[:, C_INV:C_INV + 1]
        sgn = cst[:, C_SGN:C_SGN + 1]
        d0m = cst[:, C_D0:C_D0 + 512]
        GAM = [1.0 - 2.0 ** (-5 - h) for h in range(NH)]
        CD = [g ** 128 for g in GAM]
        KSCALE = float(HD ** -0.5)

        def ln_stats_chunk(f):
            PE(lambda e: e.matmul(PS[P_ST][:, 0:T], onesA[:], xT[:, f, :], start=(f == 0), stop=(f == 7)),
               rd=["onesA", ("xT", f)], wr=[psk(P_ST)], sig=(f == 7))
            t_, tk = tmp()
            ACT(lambda e: e.activation(t_[:], xT[:, f, :], AF.Square), [("xT", f)], [tk])
            PE(lambda e: e.matmul(PS[P_DS][:, 0:T], onesA[:], t_[:], start=(f == 0), stop=(f == 7)),
               rd=["onesA", tk], wr=[psk(P_DS)], sig=(f == 7))

        def layer_norm_stream(l, gcol, bcol):
            pmean, pmsq = P_ST, P_DS
            mean_, mk = lnm, "lnm"
            rstd_, rk = lnr, "lnr"
            ACT(lambda e: e.copy(mean_[:], PS[pmean][:, 0:T]), [psk(pmean)], [mk])
            DVE(lambda e: e.tensor_tensor(rstd_[:], mean_[:], mean_[:], ALU.mult), [mk], [rk])
            DVE(lambda e: e.tensor_tensor(rstd_[:], PS[pmsq][:, 0:T], rstd_[:], ALU.subtract), [psk(pmsq), rk], [rk])
            ACT(lambda e: e.activation(rstd_[:], rstd_[:], AF.Sqrt, bias=eps_t[:, 0:1]), [rk, "eps"], [rk])
            DVE(lambda e: e.reciprocal(rstd_[:], rstd_[:]), [rk], [rk])
            for f in range(8):
                t_, tk = tmp()
                POOL(lambda e, f=f, t_=t_: e.tensor_tensor(t_[:], xT[:, f, :], mean_[:], ALU.subtract), [("xT", f), mk], [tk])
                DVE(lambda e, t_=t_: e.tensor_tensor(t_[:], t_[:], rstd_[:], ALU.mult), [tk, rk], [tk])
                ACT(lambda e, f=f, t_=t_: e.activation(xT[:, f, :], t_[:], AF.Identity,
                                                       bias=pp[:, l, bcol + f:bcol + f + 1], scale=pp[:, l, gcol + f:gcol + f + 1]),
                    [tk, "pp"], [("xT", f)])

        eps_t = sb("eps_t", [128, 1])
        DVE(lambda e: e.memset(eps_t[:], LN_EPS), [], ["eps"])

        def modulate(l, which):
            shc = 0 if which == 0 else 24
            for f in range(8):
                ACT(lambda e, f=f: e.activation(hT[:, f, :], xT[:, f, :], AF.Identity,
                                                bias=mod[:, l, shc + f:shc + f + 1], scale=onep[:, l, which, f:f + 1]),
                    [("xT", f), "mod", "onep"], [("hT", f)])

        HT_KEYS = [("hT", f) for f in range(8)]

        def residual(l, f, pbank, gcol):
            t_, tk = tmp()
            ACT(lambda e, t_=t_: e.activation(t_[:], PS[pbank][:, 0:T], AF.Copy, scale=mod[:, l, gcol + f:gcol + f + 1]),
                [psk(pbank), "mod"], [tk])
            DVE(lambda e, t_=t_: e.scalar_tensor_tensor(xT[:, f, :], xT[:, f, :], ALPHA, t_[:], ALU.mult, ALU.add),
                [("xT", f), tk], [("xT", f)])

        def proj_fm(wv, wk, j, pbank):
            mm(PS[pbank][:, 0:T], psk(pbank),
               [(wv[:, k, j * 128:(j + 1) * 128], hT[:, k, :], [wk, ("hT", k)]) for k in range(8)])

        def proj_tm(wv, wk, c, pbank):
            mm(PS[pbank][:, 0:512], psk(pbank),
               [(hT[:, k, c * 128:(c + 1) * 128], wv[:, k, :], [wk, ("hT", k)]) for k in range(8)])

        def win_group(l, g):
            src = winb[l].rearrange("(k p) c -> p k c", p=128)[:, :, g * 512:(g + 1) * 512]
            return load_w(src, 8, 512, wkeys("winb", l, D, g * 512))

        def wrot_group(l, g):
            src = wrotb[l].rearrange("(k p) c -> p k c", p=128)[:, :, g * 512:(g + 1) * 512]
            return load_w(src, 8, 512, wkeys("wrotb", l, D, g * 512))

        def rope_tables(ti):
            pt_, pk_ = tmp()
            posi = pt_.bitcast(I32)
            S.dma("sp", lambda e: e.dma_start(out=posi[:], in_=pos_d[:, ti * T:(ti + 1) * T].partition_broadcast(128)),
                  rd=[], wr=[pk_])
            ang, ak = tmp()
            kf, kk = tmp()
            r_, rk = tmp()
            m_, mk = tmp()
            ki = kf.bitcast(I32)
            DVE(lambda e: e.tensor_copy(ang[:], posi[:]), [pk_], [ak])
            DVE(lambda e: e.tensor_scalar(ang[:], ang[:], invf, None, ALU.mult), [ak, "cst"], [ak])
            DVE(lambda e: e.tensor_scalar(ki[:], ang[:], 1.0 / TWO_PI, None, ALU.mult), [ak], [kk])
            DVE(lambda e: e.tensor_copy(r_[:], ki[:]), [kk], [rk])
            DVE(lambda e: e.scalar_tensor_tensor(ang[:], r_[:], -CW1, ang[:], ALU.mult, ALU.add), [rk, ak], [ak])
            DVE(lambda e: e.scalar_tensor_tensor(ang[:], r_[:], -CW2, ang[:], ALU.mult, ALU.add), [rk, ak], [ak])

            def wrap(dst, dk, shift):
                DVE(lambda e: e.tensor_scalar(dst[:], ang[:], shift, None, ALU.add), [ak], [dk])
                DVE(lambda e: e.tensor_scalar(m_[:], dst[:], PI, -TWO_PI, ALU.is_gt, ALU.mult), [dk], [mk])
                DVE(lambda e: e.tensor_tensor(dst[:], dst[:], m_[:], ALU.add), [dk, mk], [dk])
                DVE(lambda e: e.tensor_scalar(m_[:], dst[:], -PI, TWO_PI, ALU.is_lt, ALU.mult), [dk], [mk])
                DVE(lambda e: e.tensor_tensor(dst[:], dst[:], m_[:], ALU.add), [dk, mk], [dk])

            wrap(kf, kk, 0.0)
            ACT(lambda e: e.activation(sinT[:], kf[:], AF.Sin, scale=sgn), [kk, "cst"], ["sinT"])
            wrap(kf, kk, PI / 2)
            ACT(lambda e: e.activation(cosT[:], kf[:], AF.Sin), [kk], ["cosT"])
            DVE(lambda e: e.tensor_scalar(coskT[:], cosT[:], KSCALE, None, ALU.mult), ["cosT"], ["coskT"])
            DVE(lambda e: e.tensor_scalar(sinkT[:], sinT[:], KSCALE, None, ALU.mult), ["sinT"], ["sinkT"])

        def retention(l, ti):
            qT, kT, qdT, vv, vkd, ktok = Bb
            for (g, grot, dst, dkey, ct, ck, st_, sk) in ((0, 0, qT, "B0", cosT, "cosT", sinT, "sinT"),
                                                          (1, 1, kT, "B1", coskT, "coskT", sinkT, "sinkT")):
                wv, wk = win_group(l, g)
                wr_, wrk = wrot_group(l, grot)
                for j in range(4):
                    pa, pb = pm(), pm()
                    proj_fm(wv, wk, j, pa)
                    proj_fm(wr_, wrk, j, pb)
                    t1, t1k = tmp()
                    t2, t2k = tmp()
                    DVE(lambda e, t1=t1, pa=pa, ct=ct: e.tensor_tensor(t1[:], PS[pa][:, 0:T], ct[:], ALU.mult), [psk(pa), ck], [t1k])
                    DVE(lambda e, t2=t2, pb=pb, st_=st_: e.tensor_tensor(t2[:], PS[pb][:, 0:T], st_[:], ALU.mult), [psk(pb), sk], [t2k])
                    POOL(lambda e, t1=t1, t2=t2, dst=dst, j=j: e.tensor_tensor(dst[:, j, :], t1[:], t2[:], ALU.add), [t1k, t2k], [(dkey, j)])
                    if g == 0:
                        POOL(lambda e, j=j: e.tensor_tensor(v3(qdT[:, j, :], NB), v3(qT[:, j, :], NB), bcast_mid(qdv[:, j, :], NB), ALU.mult),
                             [("B0", j), "cst"], [("B2", j)])
                    pump(4)
            wv, wk = win_group(l, 2)
            for c in range(NB):
                pa = pm()
                proj_tm(wv, wk, c, pa)
                ACT(lambda e, c=c, pa=pa: e.copy(vv[:, c, :], PS[pa][:, 0:512]), [psk(pa)], [("B3", c)])
                DVE(lambda e, c=c, pa=pa: e.tensor_tensor(v3(vkd[:, c, :], 4), v3(PS[pa][:, 0:512], 4), bcast_last(kdv, 128), ALU.mult),
                    [psk(pa), "cst"], [("B4", c)])
            wv, wk = win_group(l, 3)
            for c in range(NB):
                pa = pm()
                proj_tm(wv, wk, c, pa)
                ACT(lambda e, c=c, pa=pa: e.activation(Y[4 + c][:], PS[pa][:, 0:512], AF.Silu), [psk(pa)], ["Y%d" % (4 + c)])
                POOL(lambda e, c=c: e.tensor_tensor(Y[4 + c][:], Y[4 + c][:], bc[:, 0, 0:512], ALU.mult), ["Y%d" % (4 + c), "bc"], ["Y%d" % (4 + c)])
            if l == 0 and ti == 0:
                dump("q", qT[:], [128, 4, T], BF16, keys=[("B0", j) for j in range(4)])
                dump("k", kT[:], [128, 4, T], BF16, keys=[("B1", j) for j in range(4)])
                dump("qd", qdT[:], [128, 4, T], BF16, keys=[("B2", j) for j in range(4)])
                dump("v", vv[:], [128, 4, 512], BF16, keys=[("B3", j) for j in range(4)])
                dump("vkd", vkd[:], [128, 4, 512], BF16, keys=[("B4", j) for j in range(4)])
                dump("ga", Y[4][:], [128, 512], F32, keys=["Y4"])
            stmb = [STm, STm2]
            stmk = ["STm", "STm2"]

            def prep(c):
                cb = slice(c * 128, (c + 1) * 128)
                for h in range(4):
                    PE(lambda e, h=h: e.transpose(PSB[P_TP][:, h * 128:(h + 1) * 128], kT[:, h, cb], identb[:]),
                       [("B1", h), "identb"], [psk(P_TP)], sig=(h == 3))
                ACT(lambda e: e.copy(ktok[:, c, :], PSB[P_TP][:, 0:512]), [psk(P_TP)], [("B5", c)])
                for h in range(4):
                    PE(lambda e, h=h: e.matmul(PS[P_ST][:, h * 128:(h + 1) * 128], kT[:, h, cb], qT[:, h, cb], start=True, stop=True),
                       [("B1", h), ("B0", h)], [psk(P_ST)], sig=(h == 3))
                DVE(lambda e: e.tensor_tensor(stmb[c % 2][:], v3(PS[P_ST][:, 0:512], 4), maskT, ALU.mult), [psk(P_ST), "cst"], [stmk[c % 2]])
                for h in range(4):
                    hs = slice(h * 128, (h + 1) * 128)
                    PE(lambda e, h=h, hs=hs: e.matmul(PS[P_DS][:, hs], ktok[:, c, hs], vkd[:, c, hs], start=True, stop=True),
                       [("B5", c), ("B4", c)], [psk(P_DS)], sig=(h == 3))

            def out_update(c):
                cb = slice(c * 128, (c + 1) * 128)
                for h in range(4):
                    hs = slice(h * 128, (h + 1) * 128)
                    PE(lambda e, h=h, hs=hs: e.matmul(PS[P_O][:, hs], stmb[c % 2][:, h, :], vv[:, c, hs], start=True, stop=False),
                       [stmk[c % 2], ("B3", c)], [psk(P_O)], sig=False)
                    PE(lambda e, h=h, hs=hs: e.matmul(PS[P_O][:, hs], qdT[:, h, cb], Sbf_ret[l][:, h, :], start=False, stop=True),
                       [("B2", h), "Sbfret%d" % l], [psk(P_O)], sig=(h == 3))
                for h in range(4):
                    hs = slice(h * 128, (h + 1) * 128)
                    DVE(lambda e, h=h, hs=hs: e.scalar_tensor_tensor(S_ret[l][:, h, :], S_ret[l][:, h, :], CD[h], PS[P_DS][:, hs], ALU.mult, ALU.add),
                        ["Sret%d" % l, psk(P_DS)], ["Sret%d" % l])
                ACT(lambda e: e.copy(Sbf_ret[l][:], S_ret[l][:]), ["Sret%d" % l], ["Sbfret%d" % l])

            def norm(c):
                cb = slice(c * 128, (c + 1) * 128)
                s1, s2, mean, var, rstd, nmr = st4
                sq, sqk = tmp()
                DVE(lambda e: e.tensor_reduce(s1[:], v3(PS[P_O][:, 0:512], 4), AX.X, ALU.add), [psk(P_O)], ["s1"])
                ACT(lambda e: e.activation(sq[:, 0:512], PS[P_O][:, 0:512], AF.Square), [psk(P_O)], [sqk])
                DVE(lambda e: e.tensor_reduce(s2[:], v3(sq[:, 0:512], 4), AX.X, ALU.add), [sqk], ["s2"])
                DVE(lambda e: e.tensor_scalar(mean[:], s1[:], 1.0 / HD, None, ALU.mult), ["s1"], ["mean"])
                DVE(lambda e: e.tensor_tensor(var[:], mean[:], mean[:], ALU.mult), ["mean"], ["var"])
                DVE(lambda e: e.scalar_tensor_tensor(var[:], s2[:], 1.0 / HD, var[:], ALU.mult, ALU.subtract), ["s2", "var"], ["var"])
                ACT(lambda e: e.activation(rstd[:], var[:], AF.Sqrt, bias=eps_t[:, 0:1]), ["var", "eps"], ["rstd"])
                DVE(lambda e: e.reciprocal(rstd[:], rstd[:]), ["rstd"], ["rstd"])
                DVE(lambda e: e.scalar_tensor_tensor(nmr[:], mean[:], -1.0, rstd[:], ALU.mult, ALU.mult), ["mean", "rstd"], ["nmr"])
                onb, onk = tmp()
                for h in range(4):
                    hs = slice(h * 128, (h + 1) * 128)
                    ACT(lambda e, h=h, hs=hs: e.activation(onb[:, hs], PS[P_O][:, hs], AF.Identity, bias=nmr[:, h:h + 1], scale=rstd[:, h:h + 1]),
                        [psk(P_O), "nmr", "rstd"], [onk])
                POOL(lambda e: e.tensor_tensor(ubuf[:], onb[:], Y[4 + c][:], ALU.mult), [onk, "Y%d" % (4 + c)], ["ubuf"])
                for h in range(4):
                    hs = slice(h * 128, (h + 1) * 128)
                    PE(lambda e, h=h, hs=hs: e.transpose(PSB[P_TP][:, hs], ubuf[:, hs], identb[:]), ["ubuf", "identb"], [psk(P_TP)], sig=(h == 3))
                ACT(lambda e: e.copy(uT[0][:, :, cb], v3(PSB[P_TP][:, 0:512], 4)), [psk(P_TP)], [("u0T", c)])

            prep(0)
            conv_chunk(l, 0)
            for c in range(NB):
                out_update(c)
                if c + 1 < NB:
                    prep(c + 1)
                    conv_chunk(l, c + 1)
                norm(c)

        def hgrn(l, ti):
            qpT, kpT, vh, _kt0, Sb4f, _kt1 = Bb
            Sb4 = Sb4f
            NSC = T // 32
            wq, wqk = win_group(l, 4)
            wf, wfk = win_group(l, 5)
            for j in range(4):
                pq, pf = pm(), pm()
                proj_fm(wq, wqk, j, pq)
                proj_fm(wf, wfk, j, pf)
                ACT(lambda e, j=j, pq=pq: e.activation(Y[j][:], PS[pq][:, 0:T], AF.Silu), [psk(pq)], ["Y%d" % j])
                sig, sgk = tmp()
                sng, snk = tmp()
                cum, cuk = tmp()
                ACT(lambda e, sig=sig, pf=pf: e.activation(sig[:], PS[pf][:, 0:T], AF.Sigmoid), [psk(pf)], [sgk])
                ACT(lambda e, sng=sng, pf=pf: e.activation(sng[:], PS[pf][:, 0:T], AF.Sigmoid, scale=-1.0), [psk(pf)], [snk])
                ACT(lambda e, sig=sig, j=j: e.activation(sig[:], sig[:], AF.Ln, bias=lb[:, l, j:j + 1], scale=oml[:, l, j:j + 1]),
                    [sgk, "lb", "oml"], [sgk])
                DVE(lambda e, sig=sig, cum=cum: e.tensor_tensor_scan(cum[:], d0m[:, 0:T], sig[:], 0.0, ALU.mult, ALU.add), [sgk, "cst"], [cuk])
                ACT(lambda e, sig=sig, cum=cum: e.activation(sig[:], cum[:], AF.Exp), [cuk], [sgk])
                ACT(lambda e, cum=cum: e.activation(cum[:], cum[:], AF.Exp, scale=-1.0), [cuk], [cuk])
                POOL(lambda e, sig=sig, j=j: e.tensor_copy(Lc[:, j, :], sig[:, 31:T:32]), [sgk], [("Lc", j)])
                DVE(lambda e, sig=sig, j=j: e.tensor_tensor(qpT[:, j, :], Y[j][:], sig[:], ALU.mult), ["Y%d" % j, sgk], [("B0", j)])
                DVE(lambda e, sng=sng, cum=cum, j=j: e.scalar_tensor_tensor(kpT[:, j, :], sng[:], oml[:, l, j:j + 1], cum[:], ALU.mult, ALU.mult),
                    [snk, cuk, "oml"], [("B1", j)])
                pump(4)
            wv, wk = win_group(l, 6)
            for c in range(NB):
                pa = pm()
                proj_tm(wv, wk, c, pa)
                ACT(lambda e, c=c, pa=pa: e.copy(vh[:, c, :], PS[pa][:, 0:512]), [psk(pa)], [("B2", c)])
            wv, wk = win_group(l, 7)
            for c in range(NB):
                pa = pm()
                proj_tm(wv, wk, c, pa)
                ACT(lambda e, c=c, pa=pa: e.activation(Y[4 + c][:], PS[pa][:, 0:512], AF.Silu), [psk(pa)], ["Y%d" % (4 + c)])
                POOL(lambda e, c=c: e.tensor_tensor(Y[4 + c][:], Y[4 + c][:], bc[:, 0, 512:1024], ALU.mult), ["Y%d" % (4 + c), "bc"], ["Y%d" % (4 + c)])
            LC_KEYS = [("Lc", j) for j in range(4)]
            ktbuf = [Bb[3], Bb[5]]
            ktkey = ["B3", "B5"]
            stmb = [STm, STm2]
            stmk = ["STm", "STm2"]
            DSB = [0, 1, 2, 3]
            qdst = bass.AP(qmc, 0, [list(qmc[:].ap[0]), [512, 4], [160, 4], [1, 32]])
            kdst = bass.AP(kmc, 0, [list(kmc[:].ap[0]), [512, 4], [160, 4], [1, 32]])

            def prep(c):
                cb = slice(c * 128, (c + 1) * 128)
                kt = ktbuf[c % 2]
                POOL(lambda e: e.tensor_copy(kdst, kpT[:, :, cb].rearrange("p h (i t) -> p h i t", i=4)),
                     [("B1", j) for j in range(4)] + ["kmc"], ["kmc"])
                for h in range(4):
                    PE(lambda e, h=h: e.matmul(PS[P_ST][:, h * 128:(h + 1) * 128], kpT[:, h, cb], qpT[:, h, cb], start=True, stop=True),
                       [("B1", h), ("B0", h)], [psk(P_ST)], sig=(h == 3))
                DVE(lambda e: e.tensor_tensor(stmb[c % 2][:], v3(PS[P_ST][:, 0:512], 4), bcast_mid(bdm, 4), ALU.mult), [psk(P_ST), "cst"], [stmk[c % 2]])
                for half, bank in ((0, P_TP), (1, P_DS)):
                    for hh in range(2):
                        h = half * 2 + hh
                        for I in range(4):
                            col = (hh * 4 + I) * 128
                            PE(lambda e, h=h, I=I, col=col, bank=bank: e.transpose(PSB[bank][:, col:col + 128], kmc[:, h, I, :], identb[:]),
                               ["kmc", "identb"], [psk(bank)], sig=(hh == 1 and I == 3))
                    ACT(lambda e, half=half, bank=bank: e.copy(kt[:, half * 2:half * 2 + 2, :], v3(PSB[bank][:, 0:1024], 2)),
                        [psk(bank)], [(ktkey[c % 2], half)])

            def deltas(c):
                kt = ktbuf[c % 2]
                for h in range(4):
                    hs = slice(h * 128, (h + 1) * 128)
                    for I in range(4):
                        PE(lambda e, h=h, I=I, hs=hs: e.matmul(PS[DSB[h]][:, I * 128:(I + 1) * 128], kt[:, h, I * 128:(I + 1) * 128], vh[:, c, hs],
                                                               start=True, stop=True),
                           [(ktkey[c % 2], h // 2), ("B2", c)], [psk(DSB[h])], sig=(I == 3))

            def chain(c):
                for I in range(4):
                    n = c * 4 + I
                    for h in range(4):
                        if n == 0:
                            lsc = Lprev[l][:, h:h + 1]
                            lkeys = ["Lprev%d" % l]
                        else:
                            lsc = Lc[:, h, n - 1:n]
                            lkeys = [("Lc", h)]
                        ACT(lambda e, h=h, I=I, lsc=lsc: e.activation(Sb4[:, h, I * 128:(I + 1) * 128], R_h[l][:, h, :], AF.Copy, scale=lsc),
                            [("Rh%d" % l, h)] + lkeys, [("B4", h)])
                        DVE(lambda e, h=h, I=I, lsc=lsc: e.scalar_tensor_tensor(R_h[l][:, h, :], R_h[l][:, h, :], lsc, PS[DSB[h]][:, I * 128:(I + 1) * 128],
                                                                               ALU.mult, ALU.add),
                            [("Rh%d" % l, h), psk(DSB[h])] + lkeys, [("Rh%d" % l, h)])

            def outputs(c):
                cb = slice(c * 128, (c + 1) * 128)
                for h in range(4):
                    hs = slice(h * 128, (h + 1) * 128)
                    PE(lambda e, h=h, hs=hs: e.matmul(PS[P_O][:, hs], stmb[c % 2][:, h, :], vh[:, c, hs], start=True, stop=False),
                       [stmk[c % 2], ("B2", c)], [psk(P_O)], sig=False)
                    for I in range(4):
                        PE(lambda e, h=h, I=I, hs=hs: e.matmul(PS[P_O][:, hs], qmc[:, h, I, :], Sb4[:, h, I * 128:(I + 1) * 128], start=False, stop=(I == 3)),
                           ["qmc", ("B4", h)], [psk(P_O)], sig=(I == 3 and h == 3))
                s1, s2, mean, var, rstd, nmr = st4
                sq, sqk = tmp()
                ACT(lambda e, sq=sq: e.activation(sq[:, 0:512], PS[P_O][:, 0:512], AF.Square), [psk(P_O)], [sqk])
                DVE(lambda e, sq=sq: e.tensor_reduce(s2[:], v3(sq[:, 0:512], 4), AX.X, ALU.add), [sqk], ["s2"])
                ACT(lambda e: e.activation(rstd[:], s2[:], AF.Sqrt, bias=eps_t[:, 0:1], scale=1.0 / HD), ["s2", "eps"], ["rstd"])
                DVE(lambda e: e.reciprocal(rstd[:], rstd[:]), ["rstd"], ["rstd"])
                for h in range(4):
                    hs = slice(h * 128, (h + 1) * 128)
                    DVE(lambda e, h=h, hs=hs: e.scalar_tensor_tensor(ubuf[:, hs], PS[P_O][:, hs], rstd[:, h:h + 1], Y[4 + c][:, hs], ALU.mult, ALU.mult),
                        [psk(P_O), "rstd", "Y%d" % (4 + c)], ["ubuf"])
                for h in range(4):
                    hs = slice(h * 128, (h + 1) * 128)
                    PE(lambda e, h=h, hs=hs: e.transpose(PSB[P_TP][:, hs], ubuf[:, hs], identb[:]), ["ubuf", "identb"], [psk(P_TP)], sig=(h == 3))
                ACT(lambda e: e.copy(uT[1][:, :, cb], v3(PSB[P_TP][:, 0:512], 4)), [psk(P_TP)], [("u1T", c)])

            def qmask(c):
                cb = slice(c * 128, (c + 1) * 128)
                POOL(lambda e: e.tensor_copy(qdst, qpT[:, :, cb].rearrange("p h (i t) -> p h i t", i=4)),
                     [("B0", j) for j in range(4)] + ["qmc"], ["qmc"])

            prep(0)
            qmask(0)
            deltas(0)
            for c in range(NB):
                if c + 1 < NB:
                    prep(c + 1)
                pump(4)
                chain(c)
                pump(4)
                outputs(c)
                pump(4)
                if c + 1 < NB:
                    qmask(c + 1)
                    deltas(c + 1)
            POOL(lambda e: e.tensor_copy(Lprev[l][:], Lc[:, :, NSC - 1]), LC_KEYS, ["Lprev%d" % l])

        CA = [sb("cacc%d" % i, [128, T]) for i in range(4)]

        def conv_part1(l, ti):
            wa, wak = win_group(l, 8)
            wb_, wbk = win_group(l, 9)
            POOL(lambda e: e.tensor_copy(G[:, :, 0:30], halo31[l][:]), ["halo31_%d" % l], [("G", j) for j in range(4)])
            for j in range(4):
                pa, pb = pm(), pm()
                proj_fm(wa, wak, j, pa)
                proj_fm(wb_, wbk, j, pb)
                sg, sgk = tmp()
                ACT(lambda e, sg=sg, pb=pb: e.activation(sg[:], PS[pb][:, 0:T], AF.Sigmoid), [psk(pb)], [sgk])
                DVE(lambda e, sg=sg, pa=pa, j=j: e.tensor_tensor(G[:, j, 30:30 + T], PS[pa][:, 0:T], sg[:], ALU.mult), [psk(pa), sgk], [("G", j)])

        def conv_chunk(l, j):
            dg, dgk = load_w(cdiag[l, j].rearrange("p (k c) -> p k c", k=CONV_K), CONV_K, 128, [("cdiag", l, j)])
            pb_ = j
            mm(PS[pb_][:, 0:T], psk(pb_), [(dg[:, k, :], G[:, j, k:k + T], [dgk, ("G", j)]) for k in range(CONV_K)])
            ACT(lambda e: e.activation(CA[j][:], PS[pb_][:, 0:T], AF.Identity, bias=pp[:, l, O_CB + j:O_CB + j + 1]),
                [psk(pb_), "pp"], ["cacc%d" % j])
            if j == 3:
                POOL(lambda e: e.tensor_copy(halo31[l][:], G[:, :, T:T + 30]), [("G", jj) for jj in range(4)], ["halo31_%d" % l])

        def conv_part3(l, ti):
            for j in range(4):
                PE(lambda e, j=j: e.matmul(PS[P_ST][:, 0:T], onesB[:], CA[j][:], start=(j == 0), stop=(j == 3)), ["onesB", "cacc%d" % j], [psk(P_ST)], sig=(j == 3))
            for j in range(4):
                t_, tk = tmp()
                ACT(lambda e, j=j, t_=t_: e.activation(t_[:], CA[j][:], AF.Square), ["cacc%d" % j], [tk])
                PE(lambda e, j=j, t_=t_: e.matmul(PS[P_DS][:, 0:T], onesB[:], t_[:], start=(j == 0), stop=(j == 3)), ["onesB", tk], [psk(P_DS)], sig=(j == 3))
            ACT(lambda e: e.copy(lnm[:], PS[P_ST][:, 0:T]), [psk(P_ST)], ["lnm"])
            DVE(lambda e: e.tensor_tensor(lnr[:], lnm[:], lnm[:], ALU.mult), ["lnm"], ["lnr"])
            DVE(lambda e: e.tensor_tensor(lnr[:], PS[P_DS][:, 0:T], lnr[:], ALU.subtract), [psk(P_DS), "lnr"], ["lnr"])
            ACT(lambda e: e.activation(lnr[:], lnr[:], AF.Sqrt, bias=eps_t[:, 0:1]), ["lnr", "eps"], ["lnr"])
            DVE(lambda e: e.reciprocal(lnr[:], lnr[:]), ["lnr"], ["lnr"])
            for j in range(4):
                t_, tk = tmp()
                POOL(lambda e, j=j, t_=t_: e.tensor_tensor(t_[:], CA[j][:], lnm[:], ALU.subtract), ["cacc%d" % j, "lnm"], [tk])
                DVE(lambda e, t_=t_: e.tensor_tensor(t_[:], t_[:], lnr[:], ALU.mult), [tk, "lnr"], [tk])
                ACT(lambda e, j=j, t_=t_: e.activation(t_[:], t_[:], AF.Identity, bias=pp[:, l, O_CLB + j:O_CLB + j + 1],
                                                       scale=pp[:, l, O_CLG + j:O_CLG + j + 1]), [tk, "pp"], [tk])
                ACT(lambda e, j=j, t_=t_: e.activation(uT[2][:, j, :], t_[:], AF.Silu), [tk], [("u2T", j)])

        pump_state = {"gen": None}

        def pump(n):
            g = pump_state["gen"]
            if g is None:
                return
            for _ in range(n):
                try:
                    next(g)
                except StopIteration:
                    pump_state["gen"] = None
                    return

        lnm = sb("lnm", [128, T])
        lnr = sb("lnr", [128, T])

        def merge_and_out(l, ti):
            yT = [Bb[0], Bb[1]]
            for b in range(3):
                src = wbrb[l].rearrange("(k p) c -> p k c", p=128)[:, b * 4:(b + 1) * 4, :]
                wbv, wbk = load_w(src, 4, 1024, [("wbrb", l, r, cc) for r in range(b * 4, b * 4 + 4) for cc in range(2)])
                ukeys = [("u%dT" % b, c) for c in range(4)]
                for half in range(2):
                    wg, wgk = win_group(l, 10 + 2 * b + half)
                    for jj in range(4):
                        f = half * 4 + jj
                        pg, pb_ = pm(), pm()
                        proj_fm(wg, wgk, jj, pg)
                        mm(PS[pb_][:, 0:T], psk(pb_),
                           [(wbv[:, k, f * 128:(f + 1) * 128], uT[b][:, k, :], [wbk] + ukeys) for k in range(4)])
                        sg, sgk = tmp()
                        ACT(lambda e, sg=sg, pg=pg: e.activation(sg[:], PS[pg][:, 0:T], AF.Sigmoid), [psk(pg)], [sgk])
                        if b == 0:
                            DVE(lambda e, sg=sg, pb_=pb_, f=f: e.tensor_tensor(Y[f][:], PS[pb_][:, 0:T], sg[:], ALU.mult), [psk(pb_), sgk], ["Y%d" % f])
                        else:
                            DVE(lambda e, sg=sg, pb_=pb_: e.tensor_tensor(sg[:], PS[pb_][:, 0:T], sg[:], ALU.mult), [psk(pb_), sgk], [sgk])
                            if b == 1:
                                POOL(lambda e, sg=sg, f=f: e.tensor_tensor(Y[f][:], Y[f][:], sg[:], ALU.add), ["Y%d" % f, sgk], ["Y%d" % f])
                            else:
                                POOL(lambda e, sg=sg, f=f: e.tensor_tensor(yT[f // 4][:, f % 4, :], Y[f][:], sg[:], ALU.add),
                                     ["Y%d" % f, sgk], [("B%d" % (f // 4), f % 4)])
            dump("y_%d_%d" % (l, ti), Bb[0][:], [128, 4, T], BF16, keys=[("B0", j) for j in range(4)])
            YK = [("B%d" % (f // 4), f % 4) for f in range(8)]
            for half in range(2):
                src = woutb[l].rearrange("(k p) c -> p k c", p=128)[:, :, half * 512:(half + 1) * 512]
                wo, wok = load_w(src, 8, 512, wkeys("woutb", l, D, half * 512))
                for jj in range(4):
                    f = half * 4 + jj
                    pz = pm()
                    mm(PS[pz][:, 0:T], psk(pz),
                       [(wo[:, k, jj * 128:(jj + 1) * 128], yT[k // 4][:, k % 4, :], [wok, YK[k]]) for k in range(8)])
                    residual(l, f, pz, 16)
            for f in range(8):
                ln_stats_chunk(f)
            layer_norm_stream(l, O_LN1G, O_LN1B)

        def ffn(l, ti):
            def gt(j):
                return Bb[j // 4][:, j % 4, :], ("B%d" % (j // 4), j % 4)

            def conv3(pbank, m):
                i = state["pb"]
                state["pb"] = (i + 1) % 3
                pbf, pbk = pbuf[i], "pbuf%d" % i
                cw = O_FCW + m * 3
                POOL(lambda e: e.tensor_copy(pbf[:, 0:2], halo3[l][:, m, :]), [("halo3_%d" % l, m)], [pbk])
                ACT(lambda e: e.copy(pbf[:, 2:T + 2], PS[pbank][:, 0:T]), [psk(pbank), pbk], [pbk])
                POOL(lambda e: e.tensor_copy(halo3[l][:, m, :], pbf[:, T:T + 2]), [pbk], [("halo3_%d" % l, m)])
                r_, rk = tmp()
                ACT(lambda e: e.activation(r_[:], PS[pbank][:, 0:T], AF.Identity, bias=pp[:, l, O_FCB + m:O_FCB + m + 1],
                                           scale=pp[:, l, cw + 2:cw + 3]), [psk(pbank), "pp"], [rk])
                DVE(lambda e: e.scalar_tensor_tensor(r_[:], pbf[:, 1:T + 1], pp[:, l, cw + 1:cw + 2], r_[:], ALU.mult, ALU.add), [pbk, "pp", rk], [rk])
                DVE(lambda e: e.scalar_tensor_tensor(r_[:], pbf[:, 0:T], pp[:, l, cw:cw + 1], r_[:], ALU.mult, ALU.add), [pbk, "pp", rk], [rk])
                return r_, rk

            for m in range(11):
                src = wupb[l].rearrange("(k p) c -> p k c", p=128)[:, :, m * 512:(m + 1) * 512]
                wu, wuk = load_w(src, 8, 512, wkeys("wupb", l, D, m * 512))
                for jj in range(2):
                    j = 2 * m + jj
                    pa, pv = pm(), pm()
                    proj_fm(wu, wuk, jj, pa)
                    proj_fm(wu, wuk, 2 + jj, pv)
                    ra, rak = conv3(pa, j)
                    rv, rvk = conv3(pv, 22 + j)
                    ACT(lambda e, ra=ra: e.activation(ra[:], ra[:], AF.Silu), [rak], [rak])
                    gd, gk = gt(j)
                    POOL(lambda e, ra=ra, rv=rv, gd=gd: e.tensor_tensor(gd, ra[:], rv[:], ALU.mult), [rak, rvk], [gk])
            GK = [gt(j)[1] for j in range(22)]
            for f in range(8):
                src = wdnb[l].rearrange("(k p) c -> p k c", p=128)[:, :, f * 128:(f + 1) * 128]
                wd, wdk = load_w(src, 22, 128, wkeys("wdnb", l, D_FF, (f // 4) * 512))
                pz = pm()
                mm(PS[pz][:, 0:T], psk(pz),
                   [(wd[:, k, :], gt(k)[0], [wdk, GK[k]]) for k in range(22)])
                residual(l, f, pz, 40)
            for f in range(8):
                ln_stats_chunk(f)
            layer_norm_stream(l, O_LN2G, O_LN2B)

        XK = [("xT", f) for f in range(8)]

        def stage(name):
            if upto == name:
                raise _StopBuild()

        try:
          stage("setup")
          for ti in range(NT):
              for c in range(NB):
                  xi = xin[c % 2]
                  xk = "xin%d" % (c % 2)
                  r0 = ti * T + c * 128
                  S.dma("sp", lambda e, xi=xi, r0=r0: e.dma_start(out=xi[:], in_=x_d[r0:r0 + 128, :]), rd=[], wr=[xk])
                  for half in range(2):
                      pa = pm()
                      for jj in range(4):
                          f = half * 4 + jj
                          PE(lambda e, xi=xi, f=f, jj=jj, pa=pa: e.transpose(PS[pa][:, jj * 128:(jj + 1) * 128], xi[:, f * 128:(f + 1) * 128], identf[:]),
                             [xk, "identf"], [psk(pa)], sig=(jj == 3))
                      ACT(lambda e, half=half, pa=pa, c=c: e.copy(xT[:, half * 4:half * 4 + 4, c * 128:(c + 1) * 128], v3(PS[pa][:, 0:512], 4)),
                          [psk(pa)], [("xT", half * 4 + jj) for jj in range(4)])
              stage("load")
              rope_tables(ti)
              stage("rope")
              if ti == 0:
                  dump("cos", cosT[:], [128, T], keys=["cosT"])
                  dump("sin", sinT[:], [128, T], keys=["sinT"])
              for l in range(L):
                  modulate(l, 0)
                  stage("mod")
                  S.dma("sp", lambda e, l=l: e.dma_start(out=bc[:, 0, :], in_=bc_d[l]), rd=[], wr=["bc"])
                  conv_part1(l, ti)
                  retention(l, ti)
                  dump("ua_%d_%d" % (l, ti), uT[0][:], [128, 4, T], BF16, keys=[("u0T", c) for c in range(4)])
                  stage("ret")
                  hgrn(l, ti)
                  dump("ub_%d_%d" % (l, ti), uT[1][:], [128, 4, T], BF16, keys=[("u1T", c) for c in range(4)])
                  stage("hgrn")
                  pump(1000)
                  conv_part3(l, ti)
                  dump("uc_%d_%d" % (l, ti), uT[2][:], [128, 4, T], BF16, keys=[("u2T", c) for c in range(4)])
                  stage("conv")
                  merge_and_out(l, ti)
                  dump("x1_%d_%d" % (l, ti), xT[:], [128, 8, T], keys=XK)
                  stage("merge")
                  modulate(l, 1)
                  ffn(l, ti)
                  dump("x2_%d_%d" % (l, ti), xT[:], [128, 8, T], keys=XK)
              for c in range(NB):
                  xi = xin[c % 2]
                  xk = "xin%d" % (c % 2)
                  r0 = ti * T + c * 128
                  for half in range(2):
                      pa = pm()
                      for jj in range(4):
                          f = half * 4 + jj
                          PE(lambda e, f=f, jj=jj, pa=pa, c=c: e.transpose(PS[pa][:, jj * 128:(jj + 1) * 128], xT[:, f, c * 128:(c + 1) * 128], identf[:]),
                             [("xT", f), "identf"], [psk(pa)], sig=(jj == 3))
                      ACT(lambda e, xi=xi, half=half, pa=pa: e.copy(xi[:, half * 512:(half + 1) * 512], PS[pa][:, 0:512]), [psk(pa)], [xk])
                  S.dma("pool", lambda e, xi=xi, r0=r0: e.dma_start(out=out_d[r0:r0 + 128, :], in_=xi[:]), rd=[xk], wr=[("out", r0)])

        except _StopBuild:
            pass

        S.wait_all("sp", S.final_tokens())
        print("ops", S.n_op, "waits", S.n_wait, {e: len(v) for e, v in S.prog.items()}, "sems", len(S.sems))
        S.emit()
    return nc, dbg_outs


def _host_prep(inputs, S_LEN):
    f32 = np.float32
    w_in = np.asarray(inputs["w_in"], f32)
    perm = np.concatenate([h * 128 + (np.arange(128) + 64) % 128 for h in range(NH)])
    w_rot = np.concatenate([w_in[:, :, 0:512][:, :, perm], w_in[:, :, 512:1024][:, :, perm]], axis=2)
    w_up = np.asarray(inputs["ffn_w_up"], f32)
    cols = []
    for m in range(11):
        for jj in range(2):
            cols.append(np.arange((2 * m + jj) * 128, (2 * m + jj + 1) * 128))
        for jj in range(2):
            cols.append(D_FF + np.arange((2 * m + jj) * 128, (2 * m + jj + 1) * 128))
    w_up_p = w_up[:, :, np.concatenate(cols)]

    def pcol(v, n):
        return np.asarray(v, f32).reshape(n, 128).T

    pp = np.zeros((DEPTH, 128, NPP), f32)
    bc = np.zeros((DEPTH, 128, 1024), f32)
    for l in range(DEPTH):
        pp[l, :, O_BADA:O_BADA + 48] = pcol(inputs["b_ada"][l], 48)
        pp[l, :, O_LN1G:O_LN1G + 8] = pcol(inputs["ln1_g"][l], 8)
        pp[l, :, O_LN1B:O_LN1B + 8] = pcol(inputs["ln1_b"][l], 8)
        pp[l, :, O_LN2G:O_LN2G + 8] = pcol(inputs["ln2_g"][l], 8)
        pp[l, :, O_LN2B:O_LN2B + 8] = pcol(inputs["ln2_b"][l], 8)
        cw = np.asarray(inputs["conv_w"][l], f32).reshape(CONV_K, 4, 128).transpose(2, 1, 0)
        pp[l, :, O_CW:O_CW + 124] = cw.reshape(128, 124)
        pp[l, :, O_CB:O_CB + 4] = pcol(inputs["conv_b"][l], 4)
        pp[l, :, O_CLG:O_CLG + 4] = pcol(inputs["conv_ln_g"][l], 4)
        pp[l, :, O_CLB:O_CLB + 4] = pcol(inputs["conv_ln_b"][l], 4)
        for l2 in range(DEPTH):
            pp[l, :, O_LBL + 4 * l2:O_LBL + 4 * l2 + 4] = pcol(inputs["hgrn_lb_logits"][l2], 4)
        fw = np.asarray(inputs["ffn_conv_w"][l], f32).reshape(3, NFC, 128).transpose(2, 1, 0)
        pp[l, :, O_FCW:O_FCW + 132] = fw.reshape(128, 132)
        pp[l, :, O_FCB:O_FCB + NFC] = pcol(inputs["ffn_conv_b"][l], NFC)
        bc[l, :, 0:512] = np.broadcast_to(np.asarray(inputs["ret_norm_g"][l], f32)[None, :], (128, 512))
        bc[l, :, 512:1024] = np.broadcast_to(np.asarray(inputs["hgrn_norm_g"][l], f32)[None, :], (128, 512))

    cst = np.zeros((128, NCONST), np.float64)
    idx = np.arange(128)
    for h in range(NH):
        g = 1.0 - 2.0 ** (-5 - h)
        rel = idx[None, :] - idx[:, None]
        cst[:, C_MASK + h * 128:C_MASK + (h + 1) * 128] = np.where(rel >= 0, g ** np.maximum(rel, 0), 0.0)
        cst[:, C_QD + h * 128:C_QD + (h + 1) * 128] = (g ** (idx + 1.0))[None, :]
        cst[:, C_KD + h] = g ** (127.0 - idx)
    cst[:, C_BD:C_BD + 128] = ((idx[:, None] // 32 == idx[None, :] // 32) & (idx[None, :] >= idx[:, None]))
    half = 64
    inv = (np.float32(10000.0) ** (-np.arange(half, dtype=np.float32) / np.float32(half))).astype(np.float32)
    cst[:, C_INV] = inv[idx % 64]
    cst[:, C_SGN] = np.where(idx < 64, -1.0, 1.0)
    cst[:, C_D0:C_D0 + 512] = (np.arange(512) % 32 != 0)[None, :]
    shared = {
        "w_ada": np.ascontiguousarray(inputs["w_ada"], f32),
        "w_in": np.ascontiguousarray(w_in),
        "w_rot": np.ascontiguousarray(w_rot),
        "w_branch": np.ascontiguousarray(np.asarray(inputs["w_branch"], f32).reshape(DEPTH, 3 * W, D)),
        "w_out": np.ascontiguousarray(inputs["w_out"], f32),
        "w_up": np.ascontiguousarray(w_up_p),
        "w_down": np.ascontiguousarray(inputs["ffn_w_down"], f32),
        "pp": pp, "bc": bc, "consts": cst.astype(f32),
    }
    return shared


_NC_CACHE = {}


def kernel(x, c, positions, w_ada, b_ada, w_in, ret_norm_g, hgrn_lb_logits, hgrn_norm_g,
           conv_w, conv_b, conv_ln_g, conv_ln_b, w_branch, w_out, ln1_g, ln1_b,
           ffn_w_up, ffn_conv_w, ffn_conv_b, ffn_w_down, ln2_g, ln2_b):
    inputs = dict(x=x, c=c, positions=positions, w_ada=w_ada, b_ada=b_ada, w_in=w_in, ret_norm_g=ret_norm_g,
                  hgrn_lb_logits=hgrn_lb_logits, hgrn_norm_g=hgrn_norm_g, conv_w=conv_w, conv_b=conv_b,
                  conv_ln_g=conv_ln_g, conv_ln_b=conv_ln_b, w_branch=w_branch, w_out=w_out, ln1_g=ln1_g, ln1_b=ln1_b,
                  ffn_w_up=ffn_w_up, ffn_conv_w=ffn_conv_w, ffn_conv_b=ffn_conv_b, ffn_w_down=ffn_w_down,
                  ln2_g=ln2_g, ln2_b=ln2_b)
    inputs = {k: np.asarray(v) for k, v in inputs.items()}
    B, S_LEN, _ = inputs["x"].shape
    shared = _host_prep(inputs, S_LEN)
    if S_LEN not in _NC_CACHE:
        _NC_CACHE[S_LEN] = build(S_LEN)[0]
    nc = _NC_CACHE[S_LEN]
    in_maps = []
    for b in range(B):
        m = dict(shared)
        m["x"] = np.ascontiguousarray(inputs["x"][b], np.float32)
        m["pos"] = np.ascontiguousarray(inputs["positions"][b][None, :], np.int32)
        m["cvec"] = np.ascontiguousarray(np.asarray(inputs["c"][b], np.float32).reshape(8, 128).T)
        in_maps.append(m)
    res = run_bass_kernel_spmd(nc, in_maps, core_ids=list(range(B)))
    return np.stack([np.asarray(r["out"], np.float32) for r in res.results], axis=0)
```

```python
import contextlib
import numpy as np
import concourse.bass as bass
import concourse.mybir as mybir
from concourse.bass_utils import run_bass_kernel_spmd

F32 = mybir.dt.float32
BF16 = mybir.dt.bfloat16
I32 = mybir.dt.int32
AF = mybir.ActivationFunctionType
ALU = mybir.AluOpType
AX = mybir.AxisListType


class _Rec:
    def __init__(self):
        self.call = None

    def __getattr__(self, name):
        def f(*a, **k):
            self.call = (name, a, k)
            return self
        return f


def _bind(fn):
    r = _Rec()
    fn(r)
    assert r.call is not None
    return r.call


class Sched:
    COMPUTE = ("pe", "act", "dve", "pool")
    ALL = ("pe", "act", "dve", "pool", "sp")

    def __init__(self, nc, stack, n_dma_sems=24, epoch=30000):
        self.nc = nc
        self.stack = stack
        self.epoch = epoch
        self.prog = {e: [] for e in self.ALL}
        self.sems = []
        self.sem_eng = {}
        self.cur = {}
        self.cnt = {}
        for e in self.COMPUTE:
            self.cur[e] = self._new_sem("c_" + e)
            self.cnt[e] = 0
        self.dma_sems = [self._new_sem("d%d" % i) for i in range(n_dma_sems)]
        self.dma_cnt = [0] * n_dma_sems
        self.dma_rr = 0
        self.seen = {e: {} for e in self.ALL}
        self.last_w = {}
        self.readers = {}
        self.n_wait = 0
        self.n_op = 0

    def _new_sem(self, name):
        h = self.stack.enter_context(self.nc.semaphore(name + "_%d" % len(self.sems)))
        self.sems.append(h)
        if name.startswith("c_"):
            self.sem_eng[len(self.sems) - 1] = name[2:]
        return len(self.sems) - 1

    def _deps(self, rd, wr):
        deps = {}
        def add(tok):
            if tok is None:
                return
            s, v = tok
            if deps.get(s, 0) < v:
                deps[s] = v
        for k in rd:
            add(self.last_w.get(k))
        for k in wr:
            add(self.last_w.get(k))
            for s, v in self.readers.get(k, {}).items():
                add((s, v))
        return deps

    def _emit_waits(self, eng, deps):
        for s, v in deps.items():
            if eng == "pe" and s == self.cur["pe"]:
                continue
            if self.seen[eng].get(s, 0) >= v:
                continue
            self.seen[eng][s] = v
            self.prog[eng].append(("wait", s, v))
            self.n_wait += 1

    def _record(self, tok, rd, wr):
        s, v = tok
        for k in wr:
            self.last_w[k] = tok
            self.readers[k] = {}
        for k in rd:
            r = self.readers.setdefault(k, {})
            if r.get(s, 0) < v:
                r[s] = v

    def op(self, eng, fn, rd=(), wr=(), sig=True):
        deps = self._deps(rd, wr)
        if eng != "pe":
            for k in rd:
                if isinstance(k, str) and k.startswith("ps") and k[2:].isdigit():
                    for s_, v_ in self.readers.get(k, {}).items():
                        if self.sem_eng.get(s_) != eng and deps.get(s_, 0) < v_:
                            deps[s_] = v_
        self._emit_waits(eng, deps)
        if sig and self.cnt[eng] >= self.epoch:
            self.cur[eng] = self._new_sem("c_" + eng)
            self.cnt[eng] = 0
        tok = (self.cur[eng], self.cnt[eng] + 1)
        if sig:
            self.cnt[eng] += 1
            self.prog[eng].append(("op", _bind(fn), tok[0], 1))
        else:
            self.prog[eng].append(("op", _bind(fn), None, 0))
        self._record(tok, rd, wr)
        self.n_op += 1
        return tok

    def dma(self, eng, fn, rd=(), wr=()):
        i = self.dma_rr
        self.dma_rr = (self.dma_rr + 1) % len(self.dma_sems)
        s = self.dma_sems[i]
        deps = self._deps(rd, wr)
        if self.dma_cnt[i] > 0:
            v = 16 * self.dma_cnt[i]
            if deps.get(s, 0) < v:
                deps[s] = v
        self._emit_waits(eng, deps)
        self.dma_cnt[i] += 1
        tok = (s, 16 * self.dma_cnt[i])
        self.prog[eng].append(("op", _bind(fn), s, 16))
        self._record(tok, rd, wr)
        return tok

    def wait_all(self, eng, toks):
        deps = {}
        for s, v in toks:
            if deps.get(s, 0) < v:
                deps[s] = v
        self._emit_waits(eng, deps)

    def final_tokens(self):
        toks = []
        for i, s in enumerate(self.dma_sems):
            if self.dma_cnt[i]:
                toks.append((s, 16 * self.dma_cnt[i]))
        return toks

    def emit(self):
        nc = self.nc
        prog = self.prog
        sems = self.sems

        def replay(eng_obj, lst):
            for it in lst:
                if it[0] == "wait":
                    eng_obj.wait_ge(sems[it[1]], it[2])
                else:
                    name, a, k = it[1]
                    ins = getattr(eng_obj, name)(*a, **k)
                    if it[2] is not None:
                        ins.then_inc(sems[it[2]], it[3])

        with nc.Block() as block:
            @block.sync
            def _(e):
                replay(e, prog["sp"])

            @block.scalar
            def _(e):
                replay(e, prog["act"])

            @block.vector
            def _(e):
                replay(e, prog["dve"])

            @block.gpsimd
            def _(e):
                replay(e, prog["pool"])

            @block.tensor
            def _(e):
                replay(e, prog["pe"])


D = 1024
DEPTH = 2
NH = 4
HD = 128
W = 512
D_IN = 8192
D_FF = 2816
NFC = 2 * D_FF // 128
CONV_K = 31
ALPHA = (2.0 * DEPTH) ** 0.25
LN_EPS = 1e-5
PI = float(np.pi)
TWO_PI = 2.0 * float(np.pi)
CW1 = float(np.float32(6.28125))
CW2 = float(np.float32(TWO_PI - 6.28125))

O_BADA, O_LN1G, O_LN1B, O_LN2G, O_LN2B = 0, 48, 56, 64, 72
O_CW, O_CB, O_CLG, O_CLB, O_LBL, O_FCW, O_FCB, NPP = 80, 204, 208, 212, 216, 224, 356, 400
C_MASK, C_QD, C_KD, C_BD, C_INV, C_SGN, C_D0, NCONST = 0, 512, 1024, 1028, 1156, 1157, 1158, 1158 + 512


def bcast_last(ap, n):
    return bass.AP(ap.tensor, ap.offset, [list(d) for d in ap.ap] + [[0, n]])


def bcast_mid(ap, n):
    d = [list(x) for x in ap.ap]
    return bass.AP(ap.tensor, ap.offset, [d[0], [0, n]] + d[1:])


def v3(ap, a):
    return ap.rearrange("p (a b) -> p a b", a=a)


class _StopBuild(Exception):
    pass


POOL_ENG = "pool"


def build(S_LEN, T=512, n_layers=DEPTH, dbg=(), upto=None):
    nc = bass.Bass("TRN2", target_bir_lowering=False)
    NT = S_LEN // T
    NB = T // 128
    L = n_layers

    def din(name, shape, dt=F32):
        return nc.dram_tensor(name, list(shape), dt, kind="ExternalInput").ap()

    x_d = din("x", [S_LEN, D])
    pos_d = din("pos", [1, S_LEN], I32)
    cvec_d = din("cvec", [128, 8])
    wada_d = din("w_ada", [DEPTH, D, 6 * D])
    win_d = din("w_in", [DEPTH, D, D_IN])
    wrot_d = din("w_rot", [DEPTH, D, 1024])
    wbr_d = din("w_branch", [DEPTH, 3 * W, D])
    wout_d = din("w_out", [DEPTH, D, D])
    wup_d = din("w_up", [DEPTH, D, 2 * D_FF])
    wdn_d = din("w_down", [DEPTH, D_FF, D])
    pp_d = din("pp", [DEPTH, 128, NPP])
    bc_d = din("bc", [DEPTH, 128, 1024])
    const_d = din("consts", [128, NCONST])
    out_d = nc.dram_tensor("out", [S_LEN, D], F32, kind="ExternalOutput").ap()

    winb = nc.dram_tensor("winb", [DEPTH, D, D_IN], BF16).ap()
    wrotb = nc.dram_tensor("wrotb", [DEPTH, D, 1024], BF16).ap()
    wbrb = nc.dram_tensor("wbrb", [DEPTH, 3 * W, D], BF16).ap()
    woutb = nc.dram_tensor("woutb", [DEPTH, D, D], BF16).ap()
    wupb = nc.dram_tensor("wupb", [DEPTH, D, 2 * D_FF], BF16).ap()
    wdnb = nc.dram_tensor("wdnb", [DEPTH, D_FF, D], BF16).ap()

    cdiag = nc.dram_tensor("cdiag", [DEPTH, 4, 128, CONV_K * 128], BF16).ap()

    dbg_outs = {}

    with contextlib.ExitStack() as st:
        S = Sched(nc, st)

        def sb(name, shape, dt=F32):
            return nc.alloc_sbuf_tensor("sb_" + name, list(shape), dt)

        xT = sb("xT", [128, 8, T])
        hT = sb("hT", [128, 8, T], BF16)
        xin = [sb("xin%d" % i, [128, D]) for i in range(2)]
        NSLOT = 4
        SLOT_E = 4096
        wslot = [sb("ws%d" % i, [128, SLOT_E], BF16) for i in range(NSLOT)]
        cosT = sb("cosT", [128, T])
        sinT = sb("sinT", [128, T])
        coskT = sb("coskT", [128, T])
        sinkT = sb("sinkT", [128, T])
        NTMP = 8
        tmps = [sb("tmp%d" % i, [128, T]) for i in range(NTMP)]
        Bb = [sb("B%d" % i, [128, 4, T], BF16) for i in range(6)]
        qmc = sb("qmc", [128, 4, 4, 128], BF16)
        kmc = sb("kmc", [128, 4, 4, 128], BF16)
        Y = [sb("Y%d" % i, [128, T]) for i in range(8)]
        uT = [sb("u%dT" % i, [128, 4, T], BF16) for i in range(3)]
        G = sb("G", [128, 4, T + 30], BF16)
        STm = sb("STm", [128, 4, 128], BF16)
        STm2 = sb("STm2", [128, 4, 128], BF16)
        ubuf = sb("ubuf", [128, 512], BF16)
        st4 = [sb("st4_%d" % i, [128, 4]) for i in range(6)]
        S_ret = [sb("Sret%d" % l, [128, 4, 128]) for l in range(L)]
        Sbf_ret = [sb("Sbfret%d" % l, [128, 4, 128], BF16) for l in range(L)]
        R_h = [sb("Rh%d" % l, [128, 4, 128]) for l in range(L)]
        Lprev = [sb("Lprev%d" % l, [128, 4]) for l in range(L)]
        Lc = sb("Lc", [128, 4, T // 32])
        halo31 = [sb("halo31_%d" % l, [128, 4, 30], BF16) for l in range(L)]
        halo3 = [sb("halo3_%d" % l, [128, NFC, 2]) for l in range(L)]
        pbuf = [sb("pbuf%d" % i, [128, T + 2]) for i in range(3)]
        pp = sb("pp", [128, DEPTH, NPP])
        bc = sb("bc", [128, 1, 1024])
        cst = sb("cst", [128, NCONST])
        cvec = sb("cvec_sb", [128, 8])
        cond = sb("cond", [128, 8])
        mod = sb("mod", [128, DEPTH, 48])
        onep = sb("onep", [128, DEPTH, 2, 8])
        lb = sb("lb", [128, DEPTH, 4])
        oml = sb("oml", [128, DEPTH, 4])
        identf = sb("identf", [128, 128])
        identb = sb("identb", [128, 128], BF16)
        onesA = sb("onesA", [128, 128])
        onesB = sb("onesB", [128, 128])
        PS = [nc.alloc_psum_tensor("ps%d" % i, [128, 512], F32) for i in range(8)]
        PSB = [p.bitcast(BF16) for p in PS]
        print("sbuf bytes remaining", nc.sbuf_bytes_remaining)

        state = {"tmp": 0, "pm": 0, "slot": 0, "pb": 0, "pm_excl": ()}

        def tmp():
            i = state["tmp"]
            state["tmp"] = (i + 1) % NTMP
            return tmps[i], "tmp%d" % i

        def pm():
            while True:
                i = state["pm"]
                state["pm"] = (i + 1) % 8
                if i not in state["pm_excl"]:
                    return i

        P_ST, P_DS, P_O, P_TP = 4, 5, 6, 7

        def psk(i):
            return "ps%d" % i

        def ACT(fn, rd, wr, sig=True):
            return S.op("act", fn, rd, wr, sig)

        def DVE(fn, rd, wr, sig=True):
            return S.op("dve", fn, rd, wr, sig)

        def POOL(fn, rd, wr, sig=True):
            return S.op(POOL_ENG, fn, rd, wr, sig)

        def PE(fn, rd, wr, sig=True):
            return S.op("pe", fn, rd, wr, sig)

        def mm(out_ap, out_key, terms):
            n = len(terms)
            for i, (l_, r_, keys) in enumerate(terms):
                PE(lambda e, l_=l_, r_=r_, i=i: e.matmul(out_ap, l_, r_, start=(i == 0), stop=(i == n - 1)),
                   rd=keys, wr=[out_key], sig=(i == n - 1))

        def load_w(src3, K, C, rdkeys, dt=BF16):
            i = state["slot"]
            state["slot"] = (i + 1) % NSLOT
            if dt == BF16:
                view = wslot[i][:, 0:K * C].rearrange("p (k c) -> p k c", k=K)
            else:
                view = wslot[i].bitcast(F32)[:, 0:K * C].rearrange("p (k c) -> p k c", k=K)
            key = "ws%d" % i
            S.dma("sp", lambda e: e.dma_start(out=view, in_=src3), rd=rdkeys, wr=[key])
            return view, key

        def dump(name, ap, shape, dt=F32, keys=()):
            if name not in dbg:
                return
            d = nc.dram_tensor("dbg_" + name, list(shape), dt, kind="ExternalOutput").ap()
            dbg_outs[name] = d
            S.dma("pool", lambda e: e.dma_start(out=d, in_=ap), rd=list(keys), wr=["dbg_" + name])

        def cast_w(dst, src, rows, name):
            ncol = dst.shape[-1]
            for l in range(L):
                for r0 in range(0, rows, 128):
                    for c0 in range(0, ncol, 512):
                        S.dma("pool", lambda e, l=l, r0=r0, c0=c0: e.dma_start(out=dst[l, r0:r0 + 128, c0:c0 + 512], in_=src[l, r0:r0 + 128, c0:c0 + 512]),
                              rd=[], wr=[(name, l, r0 // 128, c0 // 512)])

        def wkeys(name, l, rows, c0=0, c1=None):
            if c1 is None:
                c1 = c0 + 512
            return [(name, l, r, cc) for r in range(rows // 128) for cc in range(c0 // 512, (c1 + 511) // 512)]

        cast_w(winb, win_d, D, "winb")
        cast_w(wrotb, wrot_d, D, "wrotb")
        cast_w(wbrb, wbr_d, 3 * W, "wbrb")
        cast_w(woutb, wout_d, D, "woutb")
        cast_w(wupb, wup_d, D, "wupb")
        cast_w(wdnb, wdn_d, D_FF, "wdnb")

        S.dma("sp", lambda e: e.dma_start(out=cst[:], in_=const_d), wr=["cst"])
        S.dma("sp", lambda e: e.dma_start(out=cvec[:], in_=cvec_d), wr=["cvec"])
        for l in range(DEPTH):
            S.dma("sp", lambda e, l=l: e.dma_start(out=pp[:, l, :], in_=pp_d[l]), wr=["pp"], rd=["pp"])
        POOL(lambda e: e.memset(identf[:], 1.0), [], ["identf"])
        S.op("pool", lambda e: e.affine_select(out=identf[:], in_=identf[:], pattern=[[-1, 128]], compare_op=ALU.is_equal,
                                               fill=0.0, base=0, channel_multiplier=1), ["identf"], ["identf"])
        DVE(lambda e: e.tensor_copy(identb[:], identf[:]), ["identf"], ["identb"])
        POOL(lambda e: e.memset(onesA[:], 1.0 / D), [], ["onesA"])
        POOL(lambda e: e.memset(onesB[:], 1.0 / W), [], ["onesB"])
        POOL(lambda e: e.memset(qmc[:], 0.0), [], ["qmc"])
        POOL(lambda e: e.memset(kmc[:], 0.0), [], ["kmc"])
        for l in range(L):
            POOL(lambda e, l=l: e.memset(S_ret[l][:], 0.0), [], ["Sret%d" % l])
            POOL(lambda e, l=l: e.memset(Sbf_ret[l][:], 0.0), [], ["Sbfret%d" % l])
            POOL(lambda e, l=l: e.memset(R_h[l][:], 0.0), [], ["Rh%d" % l])
            POOL(lambda e, l=l: e.memset(Lprev[l][:], 1.0), [], ["Lprev%d" % l])
            POOL(lambda e, l=l: e.memset(halo31[l][:], 0.0), [], ["halo31_%d" % l])
            POOL(lambda e, l=l: e.memset(halo3[l][:], 0.0), [], ["halo3_%d" % l])

        ACT(lambda e: e.activation(cond[:], cvec[:], AF.Silu), ["cvec"], ["cond"])
        condr = sb("condr", [128, 8, 8])
        DVE(lambda e: e.tensor_copy(condr[:], bcast_last(cond[:], 8)), ["cond"], ["condr"])
        for l in range(L):
            pmi = pm()
            for g in range(24):
                src = wada_d[l].rearrange("(k p) c -> p k c", p=128)[:, :, g * 256:(g + 1) * 256]
                wv, wk = load_w(src, 8, 256, [], dt=F32)
                for j in range(2):
                    m = g * 2 + j
                    mm(PS[pmi][:, m * 8:(m + 1) * 8], psk(pmi),
                       [(wv[:, k, j * 128:(j + 1) * 128], condr[:, k, :], [wk, "condr"]) for k in range(8)])
            mt_, mtk = tmp()
            ACT(lambda e, pmi=pmi, mt_=mt_: e.copy(mt_[:, 0:384], PS[pmi][:, 0:384]), [psk(pmi)], [mtk])
            DVE(lambda e, l=l, mt_=mt_: e.tensor_tensor(mod[:, l, :], v3(mt_[:, 0:384], 48)[:, :, 0], pp[:, l, O_BADA:O_BADA + 48], ALU.add),
                [mtk, "pp"], ["mod"])
            DVE(lambda e, l=l: e.tensor_scalar(onep[:, l, 0, :], mod[:, l, 8:16], 1.0, None, ALU.add), ["mod"], ["onep"])
            DVE(lambda e, l=l: e.tensor_scalar(onep[:, l, 1, :], mod[:, l, 32:40], 1.0, None, ALU.add), ["mod", "onep"], ["onep"])
        DVE(lambda e: e.memset(lb[:], 0.0), [], ["lb"])
        if L > 1:
            DVE(lambda e: e.tensor_tensor(lb[:, 1, :], pp[:, 0, O_LBL + 4:O_LBL + 8], pp[:, 0, O_LBL:O_LBL + 4], ALU.subtract),
                ["pp", "lb"], ["lb"])
            ACT(lambda e: e.activation(lb[:, 1, :], lb[:, 1, :], AF.Sigmoid), ["lb"], ["lb"])
        DVE(lambda e: e.tensor_scalar(oml[:], lb[:], -1.0, 1.0, ALU.mult, ALU.add), ["lb"], ["oml"])
        for l in range(L):
            for j in range(4):
                i = state["slot"]
                state["slot"] = (i + 1) % NSLOT
                dv = wslot[i][:, 0:CONV_K * 128].rearrange("p (k c) -> p k c", k=CONV_K)
                DVE(lambda e, dv=dv, l=l, j=j: e.tensor_tensor(dv, bcast_mid(identf[:], CONV_K),
                                                               bcast_last(pp[:, l, O_CW + j * 31:O_CW + j * 31 + 31], 128), ALU.mult),
                    ["identf", "pp"], ["ws%d" % i])
                S.dma("pool", lambda e, i=i, l=l, j=j: e.dma_start(out=cdiag[l, j], in_=wslot[i][:, 0:CONV_K * 128]), rd=["ws%d" % i], wr=[("cdiag", l, j)])
        dump("mod", mod[:], [128, DEPTH, 48], keys=["mod"])
        dump("lb", lb[:], [128, DEPTH, 4], keys=["lb"])

        maskT = v3(cst[:, C_MASK:C_MASK + 512], 4)
        qdv = v3(cst[:, C_QD:C_QD + 512], 4)
        kdv = cst[:, C_KD:C_KD + 4]
        bdm = cst[:, C_BD:C_BD + 128]
        invf = cst[:, C_INV:C_INV + 1]
        sgn = cst[:, C_SGN:C_SGN + 1]
        d0m = cst[:, C_D0:C_D0 + 512]
        GAM = [1.0 - 2.0 ** (-5 - h) for h in range(NH)]
        CD = [g ** 128 for g in GAM]
        KSCALE = float(HD ** -0.5)

        def ln_stats_chunk(f):
            PE(lambda e: e.matmul(PS[P_ST][:, 0:T], onesA[:], xT[:, f, :], start=(f == 0), stop=(f == 7)),
               rd=["onesA", ("xT", f)], wr=[psk(P_ST)], sig=(f == 7))
            t_, tk = tmp()
            ACT(lambda e: e.activation(t_[:], xT[:, f, :], AF.Square), [("xT", f)], [tk])
            PE(lambda e: e.matmul(PS[P_DS][:, 0:T], onesA[:], t_[:], start=(f == 0), stop=(f == 7)),
               rd=["onesA", tk], wr=[psk(P_DS)], sig=(f == 7))

        def layer_norm_stream(l, gcol, bcol):
            pmean, pmsq = P_ST, P_DS
            mean_, mk = lnm, "lnm"
            rstd_, rk = lnr, "lnr"
            ACT(lambda e: e.copy(mean_[:], PS[pmean][:, 0:T]), [psk(pmean)], [mk])
            DVE(lambda e: e.tensor_tensor(rstd_[:], mean_[:], mean_[:], ALU.mult), [mk], [rk])
            DVE(lambda e: e.tensor_tensor(rstd_[:], PS[pmsq][:, 0:T], rstd_[:], ALU.subtract), [psk(pmsq), rk], [rk])
            ACT(lambda e: e.activation(rstd_[:], rstd_[:], AF.Sqrt, bias=eps_t[:, 0:1]), [rk, "eps"], [rk])
            DVE(lambda e: e.reciprocal(rstd_[:], rstd_[:]), [rk], [rk])
            for f in range(8):
                t_, tk = tmp()
                POOL(lambda e, f=f, t_=t_: e.tensor_tensor(t_[:], xT[:, f, :], mean_[:], ALU.subtract), [("xT", f), mk], [tk])
                DVE(lambda e, t_=t_: e.tensor_tensor(t_[:], t_[:], rstd_[:], ALU.mult), [tk, rk], [tk])
                ACT(lambda e, f=f, t_=t_: e.activation(xT[:, f, :], t_[:], AF.Identity,
                                                       bias=pp[:, l, bcol + f:bcol + f + 1], scale=pp[:, l, gcol + f:gcol + f + 1]),
                    [tk, "pp"], [("xT", f)])

        eps_t = sb("eps_t", [128, 1])
        DVE(lambda e: e.memset(eps_t[:], LN_EPS), [], ["eps"])

        def modulate(l, which):
            shc = 0 if which == 0 else 24
            for f in range(8):
                ACT(lambda e, f=f: e.activation(hT[:, f, :], xT[:, f, :], AF.Identity,
                                                bias=mod[:, l, shc + f:shc + f + 1], scale=onep[:, l, which, f:f + 1]),
                    [("xT", f), "mod", "onep"], [("hT", f)])

        HT_KEYS = [("hT", f) for f in range(8)]

        def residual(l, f, pbank, gcol):
            t_, tk = tmp()
            ACT(lambda e, t_=t_: e.activation(t_[:], PS[pbank][:, 0:T], AF.Copy, scale=mod[:, l, gcol + f:gcol + f + 1]),
                [psk(pbank), "mod"], [tk])
            DVE(lambda e, t_=t_: e.scalar_tensor_tensor(xT[:, f, :], xT[:, f, :], ALPHA, t_[:], ALU.mult, ALU.add),
                [("xT", f), tk], [("xT", f)])

        def proj_fm(wv, wk, j, pbank):
            mm(PS[pbank][:, 0:T], psk(pbank),
               [(wv[:, k, j * 128:(j + 1) * 128], hT[:, k, :], [wk, ("hT", k)]) for k in range(8)])

        def proj_tm(wv, wk, c, pbank):
            mm(PS[pbank][:, 0:512], psk(pbank),
               [(hT[:, k, c * 128:(c + 1) * 128], wv[:, k, :], [wk, ("hT", k)]) for k in range(8)])

        def win_group(l, g):
            src = winb[l].rearrange("(k p) c -> p k c", p=128)[:, :, g * 512:(g + 1) * 512]
            return load_w(src, 8, 512, wkeys("winb", l, D, g * 512))

        def wrot_group(l, g):
            src = wrotb[l].rearrange("(k p) c -> p k c", p=128)[:, :, g * 512:(g + 1) * 512]
            return load_w(src, 8, 512, wkeys("wrotb", l, D, g * 512))

        def rope_tables(ti):
            pt_, pk_ = tmp()
            posi = pt_.bitcast(I32)
            S.dma("sp", lambda e: e.dma_start(out=posi[:], in_=pos_d[:, ti * T:(ti + 1) * T].partition_broadcast(128)),
                  rd=[], wr=[pk_])
            ang, ak = tmp()
            kf, kk = tmp()
            r_, rk = tmp()
            m_, mk = tmp()
            ki = kf.bitcast(I32)
            DVE(lambda e: e.tensor_copy(ang[:], posi[:]), [pk_], [ak])
            DVE(lambda e: e.tensor_scalar(ang[:], ang[:], invf, None, ALU.mult), [ak, "cst"], [ak])
            DVE(lambda e: e.tensor_scalar(ki[:], ang[:], 1.0 / TWO_PI, None, ALU.mult), [ak], [kk])
            DVE(lambda e: e.tensor_copy(r_[:], ki[:]), [kk], [rk])
            DVE(lambda e: e.scalar_tensor_tensor(ang[:], r_[:], -CW1, ang[:], ALU.mult, ALU.add), [rk, ak], [ak])
            DVE(lambda e: e.scalar_tensor_tensor(ang[:], r_[:], -CW2, ang[:], ALU.mult, ALU.add), [rk, ak], [ak])

            def wrap(dst, dk, shift):
                DVE(lambda e: e.tensor_scalar(dst[:], ang[:], shift, None, ALU.add), [ak], [dk])
                DVE(lambda e: e.tensor_scalar(m_[:], dst[:], PI, -TWO_PI, ALU.is_gt, ALU.mult), [dk], [mk])
                DVE(lambda e: e.tensor_tensor(dst[:], dst[:], m_[:], ALU.add), [dk, mk], [dk])
                DVE(lambda e: e.tensor_scalar(m_[:], dst[:], -PI, TWO_PI, ALU.is_lt, ALU.mult), [dk], [mk])
                DVE(lambda e: e.tensor_tensor(dst[:], dst[:], m_[:], ALU.add), [dk, mk], [dk])

            wrap(kf, kk, 0.0)
            ACT(lambda e: e.activation(sinT[:], kf[:], AF.Sin, scale=sgn), [kk, "cst"], ["sinT"])
            wrap(kf, kk, PI / 2)
            ACT(lambda e: e.activation(cosT[:], kf[:], AF.Sin), [kk], ["cosT"])
            DVE(lambda e: e.tensor_scalar(coskT[:], cosT[:], KSCALE, None, ALU.mult), ["cosT"], ["coskT"])
            DVE(lambda e: e.tensor_scalar(sinkT[:], sinT[:], KSCALE, None, ALU.mult), ["sinT"], ["sinkT"])

        def retention(l, ti):
            qT, kT, qdT, vv, vkd, ktok = Bb
            for (g, grot, dst, dkey, ct, ck, st_, sk) in ((0, 0, qT, "B0", cosT, "cosT", sinT, "sinT"),
                                                          (1, 1, kT, "B1", coskT, "coskT", sinkT, "sinkT")):
                wv, wk = win_group(l, g)
                wr_, wrk = wrot_group(l, grot)
                for j in range(4):
                    pa, pb = pm(), pm()
                    proj_fm(wv, wk, j, pa)
                    proj_fm(wr_, wrk, j, pb)
                    t1, t1k = tmp()
                    t2, t2k = tmp()
                    DVE(lambda e, t1=t1, pa=pa, ct=ct: e.tensor_tensor(t1[:], PS[pa][:, 0:T], ct[:], ALU.mult), [psk(pa), ck], [t1k])
                    DVE(lambda e, t2=t2, pb=pb, st_=st_: e.tensor_tensor(t2[:], PS[pb][:, 0:T], st_[:], ALU.mult), [psk(pb), sk], [t2k])
                    POOL(lambda e, t1=t1, t2=t2, dst=dst, j=j: e.tensor_tensor(dst[:, j, :], t1[:], t2[:], ALU.add), [t1k, t2k], [(dkey, j)])
                    if g == 0:
                        POOL(lambda e, j=j: e.tensor_tensor(v3(qdT[:, j, :], NB), v3(qT[:, j, :], NB), bcast_mid(qdv[:, j, :], NB), ALU.mult),
                             [("B0", j), "cst"], [("B2", j)])
                    pump(4)
            wv, wk = win_group(l, 2)
            for c in range(NB):
                pa = pm()
                proj_tm(wv, wk, c, pa)
                ACT(lambda e, c=c, pa=pa: e.copy(vv[:, c, :], PS[pa][:, 0:512]), [psk(pa)], [("B3", c)])
                DVE(lambda e, c=c, pa=pa: e.tensor_tensor(v3(vkd[:, c, :], 4), v3(PS[pa][:, 0:512], 4), bcast_last(kdv, 128), ALU.mult),
                    [psk(pa), "cst"], [("B4", c)])
            wv, wk = win_group(l, 3)
            for c in range(NB):
                pa = pm()
                proj_tm(wv, wk, c, pa)
                ACT(lambda e, c=c, pa=pa: e.activation(Y[4 + c][:], PS[pa][:, 0:512], AF.Silu), [psk(pa)], ["Y%d" % (4 + c)])
                POOL(lambda e, c=c: e.tensor_tensor(Y[4 + c][:], Y[4 + c][:], bc[:, 0, 0:512], ALU.mult), ["Y%d" % (4 + c), "bc"], ["Y%d" % (4 + c)])
            if l == 0 and ti == 0:
                dump("q", qT[:], [128, 4, T], BF16, keys=[("B0", j) for j in range(4)])
                dump("k", kT[:], [128, 4, T], BF16, keys=[("B1", j) for j in range(4)])
                dump("qd", qdT[:], [128, 4, T], BF16, keys=[("B2", j) for j in range(4)])
                dump("v", vv[:], [128, 4, 512], BF16, keys=[("B3", j) for j in range(4)])
                dump("vkd", vkd[:], [128, 4, 512], BF16, keys=[("B4", j) for j in range(4)])
                dump("ga", Y[4][:], [128, 512], F32, keys=["Y4"])
            stmb = [STm, STm2]
            stmk = ["STm", "STm2"]

            def prep(c):
                cb = slice(c * 128, (c + 1) * 128)
                for h in range(4):
                    PE(lambda e, h=h: e.transpose(PSB[P_TP][:, h * 128:(h + 1) * 128], kT[:, h, cb], identb[:]),
                       [("B1", h), "identb"], [psk(P_TP)], sig=(h == 3))
                ACT(lambda e: e.copy(ktok[:, c, :], PSB[P_TP][:, 0:512]), [psk(P_TP)], [("B5", c)])
                for h in range(4):
                    PE(lambda e, h=h: e.matmul(PS[P_ST][:, h * 128:(h + 1) * 128], kT[:, h, cb], qT[:, h, cb], start=True, stop=True),
                       [("B1", h), ("B0", h)], [psk(P_ST)], sig=(h == 3))
                DVE(lambda e: e.tensor_tensor(stmb[c % 2][:], v3(PS[P_ST][:, 0:512], 4), maskT, ALU.mult), [psk(P_ST), "cst"], [stmk[c % 2]])
                for h in range(4):
                    hs = slice(h * 128, (h + 1) * 128)
                    PE(lambda e, h=h, hs=hs: e.matmul(PS[P_DS][:, hs], ktok[:, c, hs], vkd[:, c, hs], start=True, stop=True),
                       [("B5", c), ("B4", c)], [psk(P_DS)], sig=(h == 3))

            def out_update(c):
                cb = slice(c * 128, (c + 1) * 128)
                for h in range(4):
                    hs = slice(h * 128, (h + 1) * 128)
                    PE(lambda e, h=h, hs=hs: e.matmul(PS[P_O][:, hs], stmb[c % 2][:, h, :], vv[:, c, hs], start=True, stop=False),
                       [stmk[c % 2], ("B3", c)], [psk(P_O)], sig=False)
                    PE(lambda e, h=h, hs=hs: e.matmul(PS[P_O][:, hs], qdT[:, h, cb], Sbf_ret[l][:, h, :], start=False, stop=True),
                       [("B2", h), "Sbfret%d" % l], [psk(P_O)], sig=(h == 3))
                for h in range(4):
                    hs = slice(h * 128, (h + 1) * 128)
                    DVE(lambda e, h=h, hs=hs: e.scalar_tensor_tensor(S_ret[l][:, h, :], S_ret[l][:, h, :], CD[h], PS[P_DS][:, hs], ALU.mult, ALU.add),
                        ["Sret%d" % l, psk(P_DS)], ["Sret%d" % l])
                ACT(lambda e: e.copy(Sbf_ret[l][:], S_ret[l][:]), ["Sret%d" % l], ["Sbfret%d" % l])

            def norm(c):
                cb = slice(c * 128, (c + 1) * 128)
                s1, s2, mean, var, rstd, nmr = st4
                sq, sqk = tmp()
                DVE(lambda e: e.tensor_reduce(s1[:], v3(PS[P_O][:, 0:512], 4), AX.X, ALU.add), [psk(P_O)], ["s1"])
                ACT(lambda e: e.activation(sq[:, 0:512], PS[P_O][:, 0:512], AF.Square), [psk(P_O)], [sqk])
                DVE(lambda e: e.tensor_reduce(s2[:], v3(sq[:, 0:512], 4), AX.X, ALU.add), [sqk], ["s2"])
                DVE(lambda e: e.tensor_scalar(mean[:], s1[:], 1.0 / HD, None, ALU.mult), ["s1"], ["mean"])
                DVE(lambda e: e.tensor_tensor(var[:], mean[:], mean[:], ALU.mult), ["mean"], ["var"])
                DVE(lambda e: e.scalar_tensor_tensor(var[:], s2[:], 1.0 / HD, var[:], ALU.mult, ALU.subtract), ["s2", "var"], ["var"])
                ACT(lambda e: e.activation(rstd[:], var[:], AF.Sqrt, bias=eps_t[:, 0:1]), ["var", "eps"], ["rstd"])
                DVE(lambda e: e.reciprocal(rstd[:], rstd[:]), ["rstd"], ["rstd"])
                DVE(lambda e: e.scalar_tensor_tensor(nmr[:], mean[:], -1.0, rstd[:], ALU.mult, ALU.mult), ["mean", "rstd"], ["nmr"])
                onb, onk = tmp()
                for h in range(4):
                    hs = slice(h * 128, (h + 1) * 128)
                    ACT(lambda e, h=h, hs=hs: e.activation(onb[:, hs], PS[P_O][:, hs], AF.Identity, bias=nmr[:, h:h + 1], scale=rstd[:, h:h + 1]),
                        [psk(P_O), "nmr", "rstd"], [onk])
                POOL(lambda e: e.tensor_tensor(ubuf[:], onb[:], Y[4 + c][:], ALU.mult), [onk, "Y%d" % (4 + c)], ["ubuf"])
                for h in range(4):
                    hs = slice(h * 128, (h + 1) * 128)
                    PE(lambda e, h=h, hs=hs: e.transpose(PSB[P_TP][:, hs], ubuf[:, hs], identb[:]), ["ubuf", "identb"], [psk(P_TP)], sig=(h == 3))
                ACT(lambda e: e.copy(uT[0][:, :, cb], v3(PSB[P_TP][:, 0:512], 4)), [psk(P_TP)], [("u0T", c)])

            prep(0)
            conv_chunk(l, 0)
            conv_evac(l, 0)
            for c in range(NB):
                out_update(c)
                if c + 1 < NB:
                    prep(c + 1)
                    conv_chunk(l, c + 1)
                norm(c)
                if c + 1 < NB:
                    conv_evac(l, c + 1)

        def hgrn(l, ti):
            qpT, kpT, vh, _kt0, Sb4f, _kt1 = Bb
            Sb4 = Sb4f
            NSC = T // 32
            wq, wqk = win_group(l, 4)
            wf, wfk = win_group(l, 5)
            for j in range(4):
                pq, pf = pm(), pm()
                proj_fm(wq, wqk, j, pq)
                proj_fm(wf, wfk, j, pf)
                ACT(lambda e, j=j, pq=pq: e.activation(Y[j][:], PS[pq][:, 0:T], AF.Silu), [psk(pq)], ["Y%d" % j])
                sig, sgk = tmp()
                sng, snk = tmp()
                cum, cuk = tmp()
                ACT(lambda e, sig=sig, pf=pf: e.activation(sig[:], PS[pf][:, 0:T], AF.Sigmoid), [psk(pf)], [sgk])
                ACT(lambda e, sng=sng, pf=pf: e.activation(sng[:], PS[pf][:, 0:T], AF.Sigmoid, scale=-1.0), [psk(pf)], [snk])
                ACT(lambda e, sig=sig, j=j: e.activation(sig[:], sig[:], AF.Ln, bias=lb[:, l, j:j + 1], scale=oml[:, l, j:j + 1]),
                    [sgk, "lb", "oml"], [sgk])
                DVE(lambda e, sig=sig, cum=cum: e.tensor_tensor_scan(cum[:], d0m[:, 0:T], sig[:], 0.0, ALU.mult, ALU.add), [sgk, "cst"], [cuk])
                ACT(lambda e, sig=sig, cum=cum: e.activation(sig[:], cum[:], AF.Exp), [cuk], [sgk])
                ACT(lambda e, cum=cum: e.activation(cum[:], cum[:], AF.Exp, scale=-1.0), [cuk], [cuk])
                POOL(lambda e, sig=sig, j=j: e.tensor_copy(Lc[:, j, :], sig[:, 31:T:32]), [sgk], [("Lc", j)])
                DVE(lambda e, sig=sig, j=j: e.tensor_tensor(qpT[:, j, :], Y[j][:], sig[:], ALU.mult), ["Y%d" % j, sgk], [("B0", j)])
                DVE(lambda e, sng=sng, cum=cum, j=j: e.scalar_tensor_tensor(kpT[:, j, :], sng[:], oml[:, l, j:j + 1], cum[:], ALU.mult, ALU.mult),
                    [snk, cuk, "oml"], [("B1", j)])
                pump(4)
            wv, wk = win_group(l, 6)
            for c in range(NB):
                pa = pm()
                proj_tm(wv, wk, c, pa)
                ACT(lambda e, c=c, pa=pa: e.copy(vh[:, c, :], PS[pa][:, 0:512]), [psk(pa)], [("B2", c)])
            wv, wk = win_group(l, 7)
            for c in range(NB):
                pa = pm()
                proj_tm(wv, wk, c, pa)
                ACT(lambda e, c=c, pa=pa: e.activation(Y[4 + c][:], PS[pa][:, 0:512], AF.Silu), [psk(pa)], ["Y%d" % (4 + c)])
                POOL(lambda e, c=c: e.tensor_tensor(Y[4 + c][:], Y[4 + c][:], bc[:, 0, 512:1024], ALU.mult), ["Y%d" % (4 + c), "bc"], ["Y%d" % (4 + c)])
            LC_KEYS = [("Lc", j) for j in range(4)]
            ktbuf = [Bb[3], Bb[5]]
            ktkey = ["B3", "B5"]
            stmb = [STm, STm2]
            stmk = ["STm", "STm2"]
            DSB = [0, 1, 2, 3]
            qdst = bass.AP(qmc, 0, [list(qmc[:].ap[0]), [512, 4], [160, 4], [1, 32]])
            kdst = bass.AP(kmc, 0, [list(kmc[:].ap[0]), [512, 4], [160, 4], [1, 32]])

            def prep(c):
                cb = slice(c * 128, (c + 1) * 128)
                kt = ktbuf[c % 2]
                POOL(lambda e: e.tensor_copy(kdst, kpT[:, :, cb].rearrange("p h (i t) -> p h i t", i=4)),
                     [("B1", j) for j in range(4)] + ["kmc"], ["kmc"])
                for h in range(4):
                    PE(lambda e, h=h: e.matmul(PS[P_ST][:, h * 128:(h + 1) * 128], kpT[:, h, cb], qpT[:, h, cb], start=True, stop=True),
                       [("B1", h), ("B0", h)], [psk(P_ST)], sig=(h == 3))
                DVE(lambda e: e.tensor_tensor(stmb[c % 2][:], v3(PS[P_ST][:, 0:512], 4), bcast_mid(bdm, 4), ALU.mult), [psk(P_ST), "cst"], [stmk[c % 2]])
                for half, bank in ((0, P_TP), (1, P_DS)):
                    for hh in range(2):
                        h = half * 2 + hh
                        for I in range(4):
                            col = (hh * 4 + I) * 128
                            PE(lambda e, h=h, I=I, col=col, bank=bank: e.transpose(PSB[bank][:, col:col + 128], kmc[:, h, I, :], identb[:]),
                               ["kmc", "identb"], [psk(bank)], sig=(hh == 1 and I == 3))
                    ACT(lambda e, half=half, bank=bank: e.copy(kt[:, half * 2:half * 2 + 2, :], v3(PSB[bank][:, 0:1024], 2)),
                        [psk(bank)], [(ktkey[c % 2], half)])

            def deltas(c):
                kt = ktbuf[c % 2]
                for h in range(4):
                    hs = slice(h * 128, (h + 1) * 128)
                    for I in range(4):
                        PE(lambda e, h=h, I=I, hs=hs: e.matmul(PS[DSB[h]][:, I * 128:(I + 1) * 128], kt[:, h, I * 128:(I + 1) * 128], vh[:, c, hs],
                                                               start=True, stop=True),
                           [(ktkey[c % 2], h // 2), ("B2", c)], [psk(DSB[h])], sig=(I == 3))

            def chain(c):
                for I in range(4):
                    n = c * 4 + I
                    for h in range(4):
                        if n == 0:
                            lsc = Lprev[l][:, h:h + 1]
                            lkeys = ["Lprev%d" % l]
                        else:
                            lsc = Lc[:, h, n - 1:n]
                            lkeys = [("Lc", h)]
                        ACT(lambda e, h=h, I=I, lsc=lsc: e.activation(Sb4[:, h, I * 128:(I + 1) * 128], R_h[l][:, h, :], AF.Copy, scale=lsc),
                            [("Rh%d" % l, h)] + lkeys, [("B4", h)])
                        DVE(lambda e, h=h, I=I, lsc=lsc: e.scalar_tensor_tensor(R_h[l][:, h, :], R_h[l][:, h, :], lsc, PS[DSB[h]][:, I * 128:(I + 1) * 128],
                                                                               ALU.mult, ALU.add),
                            [("Rh%d" % l, h), psk(DSB[h])] + lkeys, [("Rh%d" % l, h)])

            def outputs(c):
                cb = slice(c * 128, (c + 1) * 128)
                for h in range(4):
                    hs = slice(h * 128, (h + 1) * 128)
                    PE(lambda e, h=h, hs=hs: e.matmul(PS[P_O][:, hs], stmb[c % 2][:, h, :], vh[:, c, hs], start=True, stop=False),
                       [stmk[c % 2], ("B2", c)], [psk(P_O)], sig=False)
                    for I in range(4):
                        PE(lambda e, h=h, I=I, hs=hs: e.matmul(PS[P_O][:, hs], qmc[:, h, I, :], Sb4[:, h, I * 128:(I + 1) * 128], start=False, stop=(I == 3)),
                           ["qmc", ("B4", h)], [psk(P_O)], sig=(I == 3 and h == 3))
                s1, s2, mean, var, rstd, nmr = st4
                sq, sqk = tmp()
                ACT(lambda e, sq=sq: e.activation(sq[:, 0:512], PS[P_O][:, 0:512], AF.Square), [psk(P_O)], [sqk])
                DVE(lambda e, sq=sq: e.tensor_reduce(s2[:], v3(sq[:, 0:512], 4), AX.X, ALU.add), [sqk], ["s2"])
                ACT(lambda e: e.activation(rstd[:], s2[:], AF.Sqrt, bias=eps_t[:, 0:1], scale=1.0 / HD), ["s2", "eps"], ["rstd"])
                DVE(lambda e: e.reciprocal(rstd[:], rstd[:]), ["rstd"], ["rstd"])
                for h in range(4):
                    hs = slice(h * 128, (h + 1) * 128)
                    DVE(lambda e, h=h, hs=hs: e.scalar_tensor_tensor(ubuf[:, hs], PS[P_O][:, hs], rstd[:, h:h + 1], Y[4 + c][:, hs], ALU.mult, ALU.mult),
                        [psk(P_O), "rstd", "Y%d" % (4 + c)], ["ubuf"])
                for h in range(4):
                    hs = slice(h * 128, (h + 1) * 128)
                    PE(lambda e, h=h, hs=hs: e.transpose(PSB[P_TP][:, hs], ubuf[:, hs], identb[:]), ["ubuf", "identb"], [psk(P_TP)], sig=(h == 3))
                ACT(lambda e: e.copy(uT[1][:, :, cb], v3(PSB[P_TP][:, 0:512], 4)), [psk(P_TP)], [("u1T", c)])

            def qmask(c):
                cb = slice(c * 128, (c + 1) * 128)
                POOL(lambda e: e.tensor_copy(qdst, qpT[:, :, cb].rearrange("p h (i t) -> p h i t", i=4)),
                     [("B0", j) for j in range(4)] + ["qmc"], ["qmc"])

            prep(0)
            qmask(0)
            deltas(0)
            for c in range(NB):
                chain(c)
                if c + 1 < NB:
                    prep(c + 1)
                outputs(c)
                if c + 1 < NB:
                    qmask(c + 1)
                    deltas(c + 1)
            POOL(lambda e: e.tensor_copy(Lprev[l][:], Lc[:, :, NSC - 1]), LC_KEYS, ["Lprev%d" % l])

        CA = [sb("cacc%d" % i, [128, T]) for i in range(4)]

        def conv_part1(l, ti):
            wa, wak = win_group(l, 8)
            wb_, wbk = win_group(l, 9)
            POOL(lambda e: e.tensor_copy(G[:, :, 0:30], halo31[l][:]), ["halo31_%d" % l], [("G", j) for j in range(4)])
            for j in range(4):
                pa, pb = pm(), pm()
                proj_fm(wa, wak, j, pa)
                proj_fm(wb_, wbk, j, pb)
                sg, sgk = tmp()
                ACT(lambda e, sg=sg, pb=pb: e.activation(sg[:], PS[pb][:, 0:T], AF.Sigmoid), [psk(pb)], [sgk])
                DVE(lambda e, sg=sg, pa=pa, j=j: e.tensor_tensor(G[:, j, 30:30 + T], PS[pa][:, 0:T], sg[:], ALU.mult), [psk(pa), sgk], [("G", j)])

        def conv_chunk(l, j):
            dg, dgk = load_w(cdiag[l, j].rearrange("p (k c) -> p k c", k=CONV_K), CONV_K, 128, [("cdiag", l, j)])
            pb_ = j
            mm(PS[pb_][:, 0:T], psk(pb_), [(dg[:, k, :], G[:, j, k:k + T], [dgk, ("G", j)]) for k in range(CONV_K)])

        def conv_evac(l, j):
            pb_ = j
            ACT(lambda e: e.activation(CA[j][:], PS[pb_][:, 0:T], AF.Identity, bias=pp[:, l, O_CB + j:O_CB + j + 1]),
                [psk(pb_), "pp"], ["cacc%d" % j])
            if j == 3:
                POOL(lambda e: e.tensor_copy(halo31[l][:], G[:, :, T:T + 30]), [("G", jj) for jj in range(4)], ["halo31_%d" % l])

        def conv_part3(l, ti):
            for j in range(4):
                PE(lambda e, j=j: e.matmul(PS[P_ST][:, 0:T], onesB[:], CA[j][:], start=(j == 0), stop=(j == 3)), ["onesB", "cacc%d" % j], [psk(P_ST)], sig=(j == 3))
            for j in range(4):
                t_, tk = tmp()
                ACT(lambda e, j=j, t_=t_: e.activation(t_[:], CA[j][:], AF.Square), ["cacc%d" % j], [tk])
                PE(lambda e, j=j, t_=t_: e.matmul(PS[P_DS][:, 0:T], onesB[:], t_[:], start=(j == 0), stop=(j == 3)), ["onesB", tk], [psk(P_DS)], sig=(j == 3))
            ACT(lambda e: e.copy(lnm[:], PS[P_ST][:, 0:T]), [psk(P_ST)], ["lnm"])
            DVE(lambda e: e.tensor_tensor(lnr[:], lnm[:], lnm[:], ALU.mult), ["lnm"], ["lnr"])
            DVE(lambda e: e.tensor_tensor(lnr[:], PS[P_DS][:, 0:T], lnr[:], ALU.subtract), [psk(P_DS), "lnr"], ["lnr"])
            ACT(lambda e: e.activation(lnr[:], lnr[:], AF.Sqrt, bias=eps_t[:, 0:1]), ["lnr", "eps"], ["lnr"])
            DVE(lambda e: e.reciprocal(lnr[:], lnr[:]), ["lnr"], ["lnr"])
            for j in range(4):
                t_, tk = tmp()
                POOL(lambda e, j=j, t_=t_: e.tensor_tensor(t_[:], CA[j][:], lnm[:], ALU.subtract), ["cacc%d" % j, "lnm"], [tk])
                DVE(lambda e, t_=t_: e.tensor_tensor(t_[:], t_[:], lnr[:], ALU.mult), [tk, "lnr"], [tk])
                ACT(lambda e, j=j, t_=t_: e.activation(t_[:], t_[:], AF.Identity, bias=pp[:, l, O_CLB + j:O_CLB + j + 1],
                                                       scale=pp[:, l, O_CLG + j:O_CLG + j + 1]), [tk, "pp"], [tk])
                ACT(lambda e, j=j, t_=t_: e.activation(uT[2][:, j, :], t_[:], AF.Silu), [tk], [("u2T", j)])

        pump_state = {"gen": None}

        def pump(n):
            g = pump_state["gen"]
            if g is None:
                return
            for _ in range(n):
                try:
                    next(g)
                except StopIteration:
                    pump_state["gen"] = None
                    return

        lnm = sb("lnm", [128, T])
        lnr = sb("lnr", [128, T])

        def merge_and_out(l, ti):
            yT = [Bb[0], Bb[1]]
            for b in range(3):
                src = wbrb[l].rearrange("(k p) c -> p k c", p=128)[:, b * 4:(b + 1) * 4, :]
                wbv, wbk = load_w(src, 4, 1024, [("wbrb", l, r, cc) for r in range(b * 4, b * 4 + 4) for cc in range(2)])
                ukeys = [("u%dT" % b, c) for c in range(4)]
                for half in range(2):
                    wg, wgk = win_group(l, 10 + 2 * b + half)
                    for jj in range(4):
                        f = half * 4 + jj
                        pg, pb_ = pm(), pm()
                        proj_fm(wg, wgk, jj, pg)
                        mm(PS[pb_][:, 0:T], psk(pb_),
                           [(wbv[:, k, f * 128:(f + 1) * 128], uT[b][:, k, :], [wbk] + ukeys) for k in range(4)])
                        sg, sgk = tmp()
                        ACT(lambda e, sg=sg, pg=pg: e.activation(sg[:], PS[pg][:, 0:T], AF.Sigmoid), [psk(pg)], [sgk])
                        if b == 0:
                            DVE(lambda e, sg=sg, pb_=pb_, f=f: e.tensor_tensor(Y[f][:], PS[pb_][:, 0:T], sg[:], ALU.mult), [psk(pb_), sgk], ["Y%d" % f])
                        else:
                            DVE(lambda e, sg=sg, pb_=pb_: e.tensor_tensor(sg[:], PS[pb_][:, 0:T], sg[:], ALU.mult), [psk(pb_), sgk], [sgk])
                            if b == 1:
                                POOL(lambda e, sg=sg, f=f: e.tensor_tensor(Y[f][:], Y[f][:], sg[:], ALU.add), ["Y%d" % f, sgk], ["Y%d" % f])
                            else:
                                POOL(lambda e, sg=sg, f=f: e.tensor_tensor(yT[f // 4][:, f % 4, :], Y[f][:], sg[:], ALU.add),
                                     ["Y%d" % f, sgk], [("B%d" % (f // 4), f % 4)])
            dump("y_%d_%d" % (l, ti), Bb[0][:], [128, 4, T], BF16, keys=[("B0", j) for j in range(4)])
            YK = [("B%d" % (f // 4), f % 4) for f in range(8)]
            for half in range(2):
                src = woutb[l].rearrange("(k p) c -> p k c", p=128)[:, :, half * 512:(half + 1) * 512]
                wo, wok = load_w(src, 8, 512, wkeys("woutb", l, D, half * 512))
                for jj in range(4):
                    f = half * 4 + jj
                    pz = pm()
                    mm(PS[pz][:, 0:T], psk(pz),
                       [(wo[:, k, jj * 128:(jj + 1) * 128], yT[k // 4][:, k % 4, :], [wok, YK[k]]) for k in range(8)])
                    residual(l, f, pz, 16)
            for f in range(8):
                ln_stats_chunk(f)
            layer_norm_stream(l, O_LN1G, O_LN1B)

        def ffn(l, ti):
            def gt(j):
                return Bb[j // 4][:, j % 4, :], ("B%d" % (j // 4), j % 4)

            def conv3(pbank, m):
                i = state["pb"]
                state["pb"] = (i + 1) % 3
                pbf, pbk = pbuf[i], "pbuf%d" % i
                cw = O_FCW + m * 3
                POOL(lambda e: e.tensor_copy(pbf[:, 0:2], halo3[l][:, m, :]), [("halo3_%d" % l, m)], [pbk])
                ACT(lambda e: e.copy(pbf[:, 2:T + 2], PS[pbank][:, 0:T]), [psk(pbank), pbk], [pbk])
                POOL(lambda e: e.tensor_copy(halo3[l][:, m, :], pbf[:, T:T + 2]), [pbk], [("halo3_%d" % l, m)])
                r_, rk = tmp()
                ACT(lambda e: e.activation(r_[:], PS[pbank][:, 0:T], AF.Identity, bias=pp[:, l, O_FCB + m:O_FCB + m + 1],
                                           scale=pp[:, l, cw + 2:cw + 3]), [psk(pbank), "pp"], [rk])
                DVE(lambda e: e.scalar_tensor_tensor(r_[:], pbf[:, 1:T + 1], pp[:, l, cw + 1:cw + 2], r_[:], ALU.mult, ALU.add), [pbk, "pp", rk], [rk])
                DVE(lambda e: e.scalar_tensor_tensor(r_[:], pbf[:, 0:T], pp[:, l, cw:cw + 1], r_[:], ALU.mult, ALU.add), [pbk, "pp", rk], [rk])
                return r_, rk

            for m in range(11):
                src = wupb[l].rearrange("(k p) c -> p k c", p=128)[:, :, m * 512:(m + 1) * 512]
                wu, wuk = load_w(src, 8, 512, wkeys("wupb", l, D, m * 512))
                for jj in range(2):
                    j = 2 * m + jj
                    pa, pv = pm(), pm()
                    proj_fm(wu, wuk, jj, pa)
                    proj_fm(wu, wuk, 2 + jj, pv)
                    ra, rak = conv3(pa, j)
                    rv, rvk = conv3(pv, 22 + j)
                    ACT(lambda e, ra=ra: e.activation(ra[:], ra[:], AF.Silu), [rak], [rak])
                    gd, gk = gt(j)
                    POOL(lambda e, ra=ra, rv=rv, gd=gd: e.tensor_tensor(gd, ra[:], rv[:], ALU.mult), [rak, rvk], [gk])
            GK = [gt(j)[1] for j in range(22)]
            for f in range(8):
                src = wdnb[l].rearrange("(k p) c -> p k c", p=128)[:, :, f * 128:(f + 1) * 128]
                wd, wdk = load_w(src, 22, 128, wkeys("wdnb", l, D_FF, (f // 4) * 512))
                pz = pm()
                mm(PS[pz][:, 0:T], psk(pz),
                   [(wd[:, k, :], gt(k)[0], [wdk, GK[k]]) for k in range(22)])
                residual(l, f, pz, 40)
            for f in range(8):
                ln_stats_chunk(f)
            layer_norm_stream(l, O_LN2G, O_LN2B)

        XK = [("xT", f) for f in range(8)]

        def stage(name):
            if upto == name:
                raise _StopBuild()

        try:
          stage("setup")
          for ti in range(NT):
              for c in range(NB):
                  xi = xin[c % 2]
                  xk = "xin%d" % (c % 2)
                  r0 = ti * T + c * 128
                  S.dma("sp", lambda e, xi=xi, r0=r0: e.dma_start(out=xi[:], in_=x_d[r0:r0 + 128, :]), rd=[], wr=[xk])
                  for half in range(2):
                      pa = pm()
                      for jj in range(4):
                          f = half * 4 + jj
                          PE(lambda e, xi=xi, f=f, jj=jj, pa=pa: e.transpose(PS[pa][:, jj * 128:(jj + 1) * 128], xi[:, f * 128:(f + 1) * 128], identf[:]),
                             [xk, "identf"], [psk(pa)], sig=(jj == 3))
                      ACT(lambda e, half=half, pa=pa, c=c: e.copy(xT[:, half * 4:half * 4 + 4, c * 128:(c + 1) * 128], v3(PS[pa][:, 0:512], 4)),
                          [psk(pa)], [("xT", half * 4 + jj) for jj in range(4)])
              stage("load")
              rope_tables(ti)
              stage("rope")
              if ti == 0:
                  dump("cos", cosT[:], [128, T], keys=["cosT"])
                  dump("sin", sinT[:], [128, T], keys=["sinT"])
              for l in range(L):
                  modulate(l, 0)
                  stage("mod")
                  S.dma("sp", lambda e, l=l: e.dma_start(out=bc[:, 0, :], in_=bc_d[l]), rd=[], wr=["bc"])
                  conv_part1(l, ti)
                  retention(l, ti)
                  dump("ua_%d_%d" % (l, ti), uT[0][:], [128, 4, T], BF16, keys=[("u0T", c) for c in range(4)])
                  stage("ret")
                  hgrn(l, ti)
                  dump("ub_%d_%d" % (l, ti), uT[1][:], [128, 4, T], BF16, keys=[("u1T", c) for c in range(4)])
                  stage("hgrn")
                  pump(1000)
                  conv_part3(l, ti)
                  dump("uc_%d_%d" % (l, ti), uT[2][:], [128, 4, T], BF16, keys=[("u2T", c) for c in range(4)])
                  stage("conv")
                  merge_and_out(l, ti)
                  dump("x1_%d_%d" % (l, ti), xT[:], [128, 8, T], keys=XK)
                  stage("merge")
                  modulate(l, 1)
                  ffn(l, ti)
                  dump("x2_%d_%d" % (l, ti), xT[:], [128, 8, T], keys=XK)
              for c in range(NB):
                  xi = xin[c % 2]
                  xk = "xin%d" % (c % 2)
                  r0 = ti * T + c * 128
                  for half in range(2):
                      pa = pm()
                      for jj in range(4):
                          f = half * 4 + jj
                          PE(lambda e, f=f, jj=jj, pa=pa, c=c: e.transpose(PS[pa][:, jj * 128:(jj + 1) * 128], xT[:, f, c * 128:(c + 1) * 128], identf[:]),
                             [("xT", f), "identf"], [psk(pa)], sig=(jj == 3))
                      ACT(lambda e, xi=xi, half=half, pa=pa: e.copy(xi[:, half * 512:(half + 1) * 512], PS[pa][:, 0:512]), [psk(pa)], [xk])
                  S.dma("pool", lambda e, xi=xi, r0=r0: e.dma_start(out=out_d[r0:r0 + 128, :], in_=xi[:]), rd=[xk], wr=[("out", r0)])

        except _StopBuild:
            pass

        S.wait_all("sp", S.final_tokens())
        print("ops", S.n_op, "waits", S.n_wait, {e: len(v) for e, v in S.prog.items()}, "sems", len(S.sems))
        S.emit()
    return nc, dbg_outs


def _host_prep(inputs, S_LEN):
    f32 = np.float32
    w_in = np.asarray(inputs["w_in"], f32)
    perm = np.concatenate([h * 128 + (np.arange(128) + 64) % 128 for h in range(NH)])
    w_rot = np.concatenate([w_in[:, :, 0:512][:, :, perm], w_in[:, :, 512:1024][:, :, perm]], axis=2)
    w_up = np.asarray(inputs["ffn_w_up"], f32)
    cols = []
    for m in range(11):
        for jj in range(2):
            cols.append(np.arange((2 * m + jj) * 128, (2 * m + jj + 1) * 128))
        for jj in range(2):
            cols.append(D_FF + np.arange((2 * m + jj) * 128, (2 * m + jj + 1) * 128))
    w_up_p = w_up[:, :, np.concatenate(cols)]

    def pcol(v, n):
        return np.asarray(v, f32).reshape(n, 128).T

    pp = np.zeros((DEPTH, 128, NPP), f32)
    bc = np.zeros((DEPTH, 128, 1024), f32)
    for l in range(DEPTH):
        pp[l, :, O_BADA:O_BADA + 48] = pcol(inputs["b_ada"][l], 48)
        pp[l, :, O_LN1G:O_LN1G + 8] = pcol(inputs["ln1_g"][l], 8)
        pp[l, :, O_LN1B:O_LN1B + 8] = pcol(inputs["ln1_b"][l], 8)
        pp[l, :, O_LN2G:O_LN2G + 8] = pcol(inputs["ln2_g"][l], 8)
        pp[l, :, O_LN2B:O_LN2B + 8] = pcol(inputs["ln2_b"][l], 8)
        cw = np.asarray(inputs["conv_w"][l], f32).reshape(CONV_K, 4, 128).transpose(2, 1, 0)
        pp[l, :, O_CW:O_CW + 124] = cw.reshape(128, 124)
        pp[l, :, O_CB:O_CB + 4] = pcol(inputs["conv_b"][l], 4)
        pp[l, :, O_CLG:O_CLG + 4] = pcol(inputs["conv_ln_g"][l], 4)
        pp[l, :, O_CLB:O_CLB + 4] = pcol(inputs["conv_ln_b"][l], 4)
        for l2 in range(DEPTH):
            pp[l, :, O_LBL + 4 * l2:O_LBL + 4 * l2 + 4] = pcol(inputs["hgrn_lb_logits"][l2], 4)
        fw = np.asarray(inputs["ffn_conv_w"][l], f32).reshape(3, NFC, 128).transpose(2, 1, 0)
        pp[l, :, O_FCW:O_FCW + 132] = fw.reshape(128, 132)
        pp[l, :, O_FCB:O_FCB + NFC] = pcol(inputs["ffn_conv_b"][l], NFC)
        bc[l, :, 0:512] = np.broadcast_to(np.asarray(inputs["ret_norm_g"][l], f32)[None, :], (128, 512))
        bc[l, :, 512:1024] = np.broadcast_to(np.asarray(inputs["hgrn_norm_g"][l], f32)[None, :], (128, 512))

    cst = np.zeros((128, NCONST), np.float64)
    idx = np.arange(128)
    for h in range(NH):
        g = 1.0 - 2.0 ** (-5 - h)
        rel = idx[None, :] - idx[:, None]
        cst[:, C_MASK + h * 128:C_MASK + (h + 1) * 128] = np.where(rel >= 0, g ** np.maximum(rel, 0), 0.0)
        cst[:, C_QD + h * 128:C_QD + (h + 1) * 128] = (g ** (idx + 1.0))[None, :]
        cst[:, C_KD + h] = g ** (127.0 - idx)
    cst[:, C_BD:C_BD + 128] = ((idx[:, None] // 32 == idx[None, :] // 32) & (idx[None, :] >= idx[:, None]))
    half = 64
    inv = (np.float32(10000.0) ** (-np.arange(half, dtype=np.float32) / np.float32(half))).astype(np.float32)
    cst[:, C_INV] = inv[idx % 64]
    cst[:, C_SGN] = np.where(idx < 64, -1.0, 1.0)
    cst[:, C_D0:C_D0 + 512] = (np.arange(512) % 32 != 0)[None, :]
    shared = {
        "w_ada": np.ascontiguousarray(inputs["w_ada"], f32),
        "w_in": np.ascontiguousarray(w_in),
        "w_rot": np.ascontiguousarray(w_rot),
        "w_branch": np.ascontiguousarray(np.asarray(inputs["w_branch"], f32).reshape(DEPTH, 3 * W, D)),
        "w_out": np.ascontiguousarray(inputs["w_out"], f32),
        "w_up": np.ascontiguousarray(w_up_p),
        "w_down": np.ascontiguousarray(inputs["ffn_w_down"], f32),
        "pp": pp, "bc": bc, "consts": cst.astype(f32),
    }
    return shared


_NC_CACHE = {}


def kernel(x, c, positions, w_ada, b_ada, w_in, ret_norm_g, hgrn_lb_logits, hgrn_norm_g,
           conv_w, conv_b, conv_ln_g, conv_ln_b, w_branch, w_out, ln1_g, ln1_b,
           ffn_w_up, ffn_conv_w, ffn_conv_b, ffn_w_down, ln2_g, ln2_b):
    inputs = dict(x=x, c=c, positions=positions, w_ada=w_ada, b_ada=b_ada, w_in=w_in, ret_norm_g=ret_norm_g,
                  hgrn_lb_logits=hgrn_lb_logits, hgrn_norm_g=hgrn_norm_g, conv_w=conv_w, conv_b=conv_b,
                  conv_ln_g=conv_ln_g, conv_ln_b=conv_ln_b, w_branch=w_branch, w_out=w_out, ln1_g=ln1_g, ln1_b=ln1_b,
                  ffn_w_up=ffn_w_up, ffn_conv_w=ffn_conv_w, ffn_conv_b=ffn_conv_b, ffn_w_down=ffn_w_down,
                  ln2_g=ln2_g, ln2_b=ln2_b)
    inputs = {k: np.asarray(v) for k, v in inputs.items()}
    B, S_LEN, _ = inputs["x"].shape
    shared = _host_prep(inputs, S_LEN)
    if S_LEN not in _NC_CACHE:
        _NC_CACHE[S_LEN] = build(S_LEN)[0]
    nc = _NC_CACHE[S_LEN]
    in_maps = []
    for b in range(B):
        m = dict(shared)
        m["x"] = np.ascontiguousarray(inputs["x"][b], np.float32)
        m["pos"] = np.ascontiguousarray(inputs["positions"][b][None, :], np.int32)
        m["cvec"] = np.ascontiguousarray(np.asarray(inputs["c"][b], np.float32).reshape(8, 128).T)
        in_maps.append(m)
    res = run_bass_kernel_spmd(nc, in_maps, core_ids=list(range(B)))
    return np.stack([np.asarray(r["out"], np.float32) for r in res.results], axis=0)
```

```python
import contextlib
import numpy as np
import concourse.bass as bass
import concourse.mybir as mybir
from concourse.bass_utils import run_bass_kernel_spmd

F32 = mybir.dt.float32
BF16 = mybir.dt.bfloat16
I32 = mybir.dt.int32
AF = mybir.ActivationFunctionType
ALU = mybir.AluOpType
AX = mybir.AxisListType


class _Rec:
    def __init__(self):
        self.call = None

    def __getattr__(self, name):
        def f(*a, **k):
            self.call = (name, a, k)
            return self
        return f


def _bind(fn):
    r = _Rec()
    fn(r)
    assert r.call is not None
    return r.call


class Sched:
    COMPUTE = ("pe", "act", "dve", "pool")
    ALL = ("pe", "act", "dve", "pool", "sp")

    def __init__(self, nc, stack, n_dma_sems=24, epoch=30000):
        self.nc = nc
        self.stack = stack
        self.epoch = epoch
        self.prog = {e: [] for e in self.ALL}
        self.sems = []
        self.sem_eng = {}
        self.cur = {}
        self.cnt = {}
        for e in self.COMPUTE:
            self.cur[e] = self._new_sem("c_" + e)
            self.cnt[e] = 0
        self.dma_sems = [self._new_sem("d%d" % i) for i in range(n_dma_sems)]
        self.dma_cnt = [0] * n_dma_sems
        self.dma_rr = 0
        self.seen = {e: {} for e in self.ALL}
        self.last_w = {}
        self.readers = {}
        self.n_wait = 0
        self.n_op = 0

    def _new_sem(self, name):
        h = self.stack.enter_context(self.nc.semaphore(name + "_%d" % len(self.sems)))
        self.sems.append(h)
        if name.startswith("c_"):
            self.sem_eng[len(self.sems) - 1] = name[2:]
        return len(self.sems) - 1

    def _deps(self, rd, wr):
        deps = {}
        def add(tok):
            if tok is None:
                return
            s, v = tok
            if deps.get(s, 0) < v:
                deps[s] = v
        for k in rd:
            add(self.last_w.get(k))
        for k in wr:
            add(self.last_w.get(k))
            for s, v in self.readers.get(k, {}).items():
                add((s, v))
        return deps

    def _emit_waits(self, eng, deps):
        for s, v in deps.items():
            if eng == "pe" and s == self.cur["pe"]:
                continue
            if self.seen[eng].get(s, 0) >= v:
                continue
            self.seen[eng][s] = v
            self.prog[eng].append(("wait", s, v))
            self.n_wait += 1

    def _record(self, tok, rd, wr):
        s, v = tok
        for k in wr:
            self.last_w[k] = tok
            self.readers[k] = {}
        for k in rd:
            r = self.readers.setdefault(k, {})
            if r.get(s, 0) < v:
                r[s] = v

    def op(self, eng, fn, rd=(), wr=(), sig=True):
        deps = self._deps(rd, wr)
        if eng != "pe":
            for k in rd:
                if isinstance(k, str) and k.startswith("ps") and k[2:].isdigit():
                    for s_, v_ in self.readers.get(k, {}).items():
                        if self.sem_eng.get(s_) != eng and deps.get(s_, 0) < v_:
                            deps[s_] = v_
        self._emit_waits(eng, deps)
        if sig and self.cnt[eng] >= self.epoch:
            self.cur[eng] = self._new_sem("c_" + eng)
            self.cnt[eng] = 0
        tok = (self.cur[eng], self.cnt[eng] + 1)
        if sig:
            self.cnt[eng] += 1
            self.prog[eng].append(("op", _bind(fn), tok[0], 1))
        else:
            self.prog[eng].append(("op", _bind(fn), None, 0))
        self._record(tok, rd, wr)
        self.n_op += 1
        return tok

    def dma(self, eng, fn, rd=(), wr=()):
        i = self.dma_rr
        self.dma_rr = (self.dma_rr + 1) % len(self.dma_sems)
        s = self.dma_sems[i]
        deps = self._deps(rd, wr)
        if self.dma_cnt[i] > 0:
            v = 16 * self.dma_cnt[i]
            if deps.get(s, 0) < v:
                deps[s] = v
        self._emit_waits(eng, deps)
        self.dma_cnt[i] += 1
        tok = (s, 16 * self.dma_cnt[i])
        self.prog[eng].append(("op", _bind(fn), s, 16))
        self._record(tok, rd, wr)
        return tok

    def wait_all(self, eng, toks):
        deps = {}
        for s, v in toks:
            if deps.get(s, 0) < v:
                deps[s] = v
        self._emit_waits(eng, deps)

    def final_tokens(self):
        toks = []
        for i, s in enumerate(self.dma_sems):
            if self.dma_cnt[i]:
                toks.append((s, 16 * self.dma_cnt[i]))
        return toks

    def emit(self):
        nc = self.nc
        prog = self.prog
        sems = self.sems

        def replay(eng_obj, lst):
            for it in lst:
                if it[0] == "wait":
                    eng_obj.wait_ge(sems[it[1]], it[2])
                else:
                    name, a, k = it[1]
                    ins = getattr(eng_obj, name)(*a, **k)
                    if it[2] is not None:
                        ins.then_inc(sems[it[2]], it[3])

        with nc.Block() as block:
            @block.sync
            def _(e):
                replay(e, prog["sp"])

            @block.scalar
            def _(e):
                replay(e, prog["act"])

            @block.vector
            def _(e):
                replay(e, prog["dve"])

            @block.gpsimd
            def _(e):
                replay(e, prog["pool"])

            @block.tensor
            def _(e):
                replay(e, prog["pe"])


D = 1024
DEPTH = 2
NH = 4
HD = 128
W = 512
D_IN = 8192
D_FF = 2816
NFC = 2 * D_FF // 128
CONV_K = 31
ALPHA = (2.0 * DEPTH) ** 0.25
LN_EPS = 1e-5
PI = float(np.pi)
TWO_PI = 2.0 * float(np.pi)
CW1 = float(np.float32(6.28125))
CW2 = float(np.float32(TWO_PI - 6.28125))

O_BADA, O_LN1G, O_LN1B, O_LN2G, O_LN2B = 0, 48, 56, 64, 72
O_CW, O_CB, O_CLG, O_CLB, O_LBL, O_FCW, O_FCB, NPP = 80, 204, 208, 212, 216, 224, 356, 400
C_MASK, C_QD, C_KD, C_BD, C_INV, C_SGN, C_D0, NCONST = 0, 512, 1024, 1028, 1156, 1157, 1158, 1158 + 512


def bcast_last(ap, n):
    return bass.AP(ap.tensor, ap.offset, [list(d) for d in ap.ap] + [[0, n]])


def bcast_mid(ap, n):
    d = [list(x) for x in ap.ap]
    return bass.AP(ap.tensor, ap.offset, [d[0], [0, n]] + d[1:])


def v3(ap, a):
    return ap.rearrange("p (a b) -> p a b", a=a)


class _StopBuild(Exception):
    pass


POOL_ENG = "pool"


def build(S_LEN, T=512, n_layers=DEPTH, dbg=(), upto=None):
    nc = bass.Bass("TRN2", target_bir_lowering=False)
    NT = S_LEN // T
    NB = T // 128
    L = n_layers

    def din(name, shape, dt=F32):
        return nc.dram_tensor(name, list(shape), dt, kind="ExternalInput").ap()

    x_d = din("x", [S_LEN, D])
    pos_d = din("pos", [1, S_LEN], I32)
    cvec_d = din("cvec", [128, 8])
    wada_d = din("w_ada", [DEPTH, D, 6 * D])
    win_d = din("w_in", [DEPTH, D, D_IN])
    wrot_d = din("w_rot", [DEPTH, D, 1024])
    wbr_d = din("w_branch", [DEPTH, 3 * W, D])
    wout_d = din("w_out", [DEPTH, D, D])
    wup_d = din("w_up", [DEPTH, D, 2 * D_FF])
    wdn_d = din("w_down", [DEPTH, D_FF, D])
    pp_d = din("pp", [DEPTH, 128, NPP])
    bc_d = din("bc", [DEPTH, 128, 1024])
    const_d = din("consts", [128, NCONST])
    out_d = nc.dram_tensor("out", [S_LEN, D], F32, kind="ExternalOutput").ap()

    winb = nc.dram_tensor("winb", [DEPTH, D, D_IN], BF16).ap()
    wrotb = nc.dram_tensor("wrotb", [DEPTH, D, 1024], BF16).ap()
    wbrb = nc.dram_tensor("wbrb", [DEPTH, 3 * W, D], BF16).ap()
    woutb = nc.dram_tensor("woutb", [DEPTH, D, D], BF16).ap()
    wupb = nc.dram_tensor("wupb", [DEPTH, D, 2 * D_FF], BF16).ap()
    wdnb = nc.dram_tensor("wdnb", [DEPTH, D_FF, D], BF16).ap()

    cdiag = nc.dram_tensor("cdiag", [DEPTH, 4, 128, CONV_K * 128], BF16).ap()

    dbg_outs = {}

    with contextlib.ExitStack() as st:
        S = Sched(nc, st)

        def sb(name, shape, dt=F32):
            return nc.alloc_sbuf_tensor("sb_" + name, list(shape), dt)

        xT = sb("xT", [128, 8, T])
        hT = sb("hT", [128, 8, T], BF16)
        xin = [sb("xin%d" % i, [128, D]) for i in range(2)]
        NSLOT = 4
        SLOT_E = 4096
        wslot = [sb("ws%d" % i, [128, SLOT_E], BF16) for i in range(NSLOT)]
        cosT = sb("cosT", [128, T])
        sinT = sb("sinT", [128, T])
        coskT = sb("coskT", [128, T])
        sinkT = sb("sinkT", [128, T])
        NTMP = 8
        tmps = [sb("tmp%d" % i, [128, T]) for i in range(NTMP)]
        Bb = [sb("B%d" % i, [128, 4, T], BF16) for i in range(6)]
        qmc = sb("qmc", [128, 4, 4, 128], BF16)
        kmc = sb("kmc", [128, 4, 4, 128], BF16)
        Y = [sb("Y%d" % i, [128, T]) for i in range(8)]
        uT = [sb("u%dT" % i, [128, 4, T], BF16) for i in range(3)]
        G = sb("G", [128, 4, T + 30], BF16)
        STm = sb("STm", [128, 4, 128], BF16)
        STm2 = sb("STm2", [128, 4, 128], BF16)
        ubuf = sb("ubuf", [128, 512], BF16)
        st4 = [sb("st4_%d" % i, [128, 4]) for i in range(6)]
        S_ret = [sb("Sret%d" % l, [128, 4, 128]) for l in range(L)]
        Sbf_ret = [sb("Sbfret%d" % l, [128, 4, 128], BF16) for l in range(L)]
        R_h = [sb("Rh%d" % l, [128, 4, 128]) for l in range(L)]
        Lprev = [sb("Lprev%d" % l, [128, 4]) for l in range(L)]
        Lc = sb("Lc", [128, 4, T // 32])
        halo31 = [sb("halo31_%d" % l, [128, 4, 30], BF16) for l in range(L)]
        halo3 = [sb("halo3_%d" % l, [128, NFC, 2]) for l in range(L)]
        pbuf = [sb("pbuf%d" % i, [128, T + 2]) for i in range(3)]
        pp = sb("pp", [128, DEPTH, NPP])
        bc = sb("bc", [128, 1, 1024])
        cst = sb("cst", [128, NCONST])
        cvec = sb("cvec_sb", [128, 8])
        cond = sb("cond", [128, 8])
        mod = sb("mod", [128, DEPTH, 48])
        onep = sb("onep", [128, DEPTH, 2, 8])
        lb = sb("lb", [128, DEPTH, 4])
        oml = sb("oml", [128, DEPTH, 4])
        identf = sb("identf", [128, 128])
        identb = sb("identb", [128, 128], BF16)
        onesA = sb("onesA", [128, 128])
        onesB = sb("onesB", [128, 128])
        PS = [nc.alloc_psum_tensor("ps%d" % i, [128, 512], F32) for i in range(8)]
        PSB = [p.bitcast(BF16) for p in PS]
        print("sbuf bytes remaining", nc.sbuf_bytes_remaining)

        state = {"tmp": 0, "pm": 0, "slot": 0, "pb": 0, "pm_excl": ()}

        def tmp():
            i = state["tmp"]
            state["tmp"] = (i + 1) % NTMP
            return tmps[i], "tmp%d" % i

        def pm():
            while True:
                i = state["pm"]
                state["pm"] = (i + 1) % 8
                if i not in state["pm_excl"]:
                    return i

        P_ST, P_DS, P_O, P_TP = 4, 5, 6, 7

        def psk(i):
            return "ps%d" % i

        def ACT(fn, rd, wr, sig=True):
            return S.op("act", fn, rd, wr, sig)

        def DVE(fn, rd, wr, sig=True):
            return S.op("dve", fn, rd, wr, sig)

        def POOL(fn, rd, wr, sig=True):
            return S.op(POOL_ENG, fn, rd, wr, sig)

        def PE(fn, rd, wr, sig=True):
            return S.op("pe", fn, rd, wr, sig)

        def mm(out_ap, out_key, terms):
            n = len(terms)
            for i, (l_, r_, keys) in enumerate(terms):
                PE(lambda e, l_=l_, r_=r_, i=i: e.matmul(out_ap, l_, r_, start=(i == 0), stop=(i == n - 1)),
                   rd=keys, wr=[out_key], sig=(i == n - 1))

        def load_w(src3, K, C, rdkeys, dt=BF16):
            i = state["slot"]
            state["slot"] = (i + 1) % NSLOT
            if dt == BF16:
                view = wslot[i][:, 0:K * C].rearrange("p (k c) -> p k c", k=K)
            else:
                view = wslot[i].bitcast(F32)[:, 0:K * C].rearrange("p (k c) -> p k c", k=K)
            key = "ws%d" % i
            S.dma("sp", lambda e: e.dma_start(out=view, in_=src3), rd=rdkeys, wr=[key])
            return view, key

        def dump(name, ap, shape, dt=F32, keys=()):
            if name not in dbg:
                return
            d = nc.dram_tensor("dbg_" + name, list(shape), dt, kind="ExternalOutput").ap()
            dbg_outs[name] = d
            S.dma("pool", lambda e: e.dma_start(out=d, in_=ap), rd=list(keys), wr=["dbg_" + name])

        def cast_w(dst, src, rows, name):
            ncol = dst.shape[-1]
            for l in range(L):
                for r0 in range(0, rows, 128):
                    for c0 in range(0, ncol, 512):
                        S.dma("pool", lambda e, l=l, r0=r0, c0=c0: e.dma_start(out=dst[l, r0:r0 + 128, c0:c0 + 512], in_=src[l, r0:r0 + 128, c0:c0 + 512]),
                              rd=[], wr=[(name, l, r0 // 128, c0 // 512)])

        def wkeys(name, l, rows, c0=0, c1=None):
            if c1 is None:
                c1 = c0 + 512
            return [(name, l, r, cc) for r in range(rows // 128) for cc in range(c0 // 512, (c1 + 511) // 512)]

        cast_w(winb, win_d, D, "winb")
        cast_w(wrotb, wrot_d, D, "wrotb")
        cast_w(wbrb, wbr_d, 3 * W, "wbrb")
        cast_w(woutb, wout_d, D, "woutb")
        cast_w(wupb, wup_d, D, "wupb")
        cast_w(wdnb, wdn_d, D_FF, "wdnb")

        S.dma("sp", lambda e: e.dma_start(out=cst[:], in_=const_d), wr=["cst"])
        S.dma("sp", lambda e: e.dma_start(out=cvec[:], in_=cvec_d), wr=["cvec"])
        for l in range(DEPTH):
            S.dma("sp", lambda e, l=l: e.dma_start(out=pp[:, l, :], in_=pp_d[l]), wr=["pp"], rd=["pp"])
        POOL(lambda e: e.memset(identf[:], 1.0), [], ["identf"])
        S.op("pool", lambda e: e.affine_select(out=identf[:], in_=identf[:], pattern=[[-1, 128]], compare_op=ALU.is_equal,
                                               fill=0.0, base=0, channel_multiplier=1), ["identf"], ["identf"])
        DVE(lambda e: e.tensor_copy(identb[:], identf[:]), ["identf"], ["identb"])
        POOL(lambda e: e.memset(onesA[:], 1.0 / D), [], ["onesA"])
        POOL(lambda e: e.memset(onesB[:], 1.0 / W), [], ["onesB"])
        POOL(lambda e: e.memset(qmc[:], 0.0), [], ["qmc"])
        POOL(lambda e: e.memset(kmc[:], 0.0), [], ["kmc"])
        for l in range(L):
            POOL(lambda e, l=l: e.memset(S_ret[l][:], 0.0), [], ["Sret%d" % l])
            POOL(lambda e, l=l: e.memset(Sbf_ret[l][:], 0.0), [], ["Sbfret%d" % l])
            POOL(lambda e, l=l: e.memset(R_h[l][:], 0.0), [], ["Rh%d" % l])
            POOL(lambda e, l=l: e.memset(Lprev[l][:], 1.0), [], ["Lprev%d" % l])
            POOL(lambda e, l=l: e.memset(halo31[l][:], 0.0), [], ["halo31_%d" % l])
            POOL(lambda e, l=l: e.memset(halo3[l][:], 0.0), [], ["halo3_%d" % l])

        ACT(lambda e: e.activation(cond[:], cvec[:], AF.Silu), ["cvec"], ["cond"])
        condr = sb("condr", [128, 8, 8])
        DVE(lambda e: e.tensor_copy(condr[:], bcast_last(cond[:], 8)), ["cond"], ["condr"])
        for l in range(L):
            pmi = pm()
            for g in range(24):
                src = wada_d[l].rearrange("(k p) c -> p k c", p=128)[:, :, g * 256:(g + 1) * 256]
                wv, wk = load_w(src, 8, 256, [], dt=F32)
                for j in range(2):
                    m = g * 2 + j
                    mm(PS[pmi][:, m * 8:(m + 1) * 8], psk(pmi),
                       [(wv[:, k, j * 128:(j + 1) * 128], condr[:, k, :], [wk, "condr"]) for k in range(8)])
            mt_, mtk = tmp()
            ACT(lambda e, pmi=pmi, mt_=mt_: e.copy(mt_[:, 0:384], PS[pmi][:, 0:384]), [psk(pmi)], [mtk])
            DVE(lambda e, l=l, mt_=mt_: e.tensor_tensor(mod[:, l, :], v3(mt_[:, 0:384], 48)[:, :, 0], pp[:, l, O_BADA:O_BADA + 48], ALU.add),
                [mtk, "pp"], ["mod"])
            DVE(lambda e, l=l: e.tensor_scalar(onep[:, l, 0, :], mod[:, l, 8:16], 1.0, None, ALU.add), ["mod"], ["onep"])
            DVE(lambda e, l=l: e.tensor_scalar(onep[:, l, 1, :], mod[:, l, 32:40], 1.0, None, ALU.add), ["mod", "onep"], ["onep"])
        DVE(lambda e: e.memset(lb[:], 0.0), [], ["lb"])
        if L > 1:
            DVE(lambda e: e.tensor_tensor(lb[:, 1, :], pp[:, 0, O_LBL + 4:O_LBL + 8], pp[:, 0, O_LBL:O_LBL + 4], ALU.subtract),
                ["pp", "lb"], ["lb"])
            ACT(lambda e: e.activation(lb[:, 1, :], lb[:, 1, :], AF.Sigmoid), ["lb"], ["lb"])
        DVE(lambda e: e.tensor_scalar(oml[:], lb[:], -1.0, 1.0, ALU.mult, ALU.add), ["lb"], ["oml"])
        for l in range(L):
            for j in range(4):
                i = state["slot"]
                state["slot"] = (i + 1) % NSLOT
                dv = wslot[i][:, 0:CONV_K * 128].rearrange("p (k c) -> p k c", k=CONV_K)
                DVE(lambda e, dv=dv, l=l, j=j: e.tensor_tensor(dv, bcast_mid(identf[:], CONV_K),
                                                               bcast_last(pp[:, l, O_CW + j * 31:O_CW + j * 31 + 31], 128), ALU.mult),
                    ["identf", "pp"], ["ws%d" % i])
                S.dma("pool", lambda e, i=i, l=l, j=j: e.dma_start(out=cdiag[l, j], in_=wslot[i][:, 0:CONV_K * 128]), rd=["ws%d" % i], wr=[("cdiag", l, j)])
        dump("mod", mod[:], [128, DEPTH, 48], keys=["mod"])
        dump("lb", lb[:], [128, DEPTH, 4], keys=["lb"])

        maskT = v3(cst[:, C_MASK:C_MASK + 512], 4)
        qdv = v3(cst[:, C_QD:C_QD + 512], 4)
        kdv = cst[:, C_KD:C_KD + 4]
        bdm = cst[:, C_BD:C_BD + 128]
        invf = cst[:, C_INV:C_INV + 1]
        sgn = cst[:, C_SGN:C_SGN + 1]
        d0m = cst[:, C_D0:C_D0 + 512]
        GAM = [1.0 - 2.0 ** (-5 - h) for h in range(NH)]
        CD = [g ** 128 for g in GAM]
        KSCALE = float(HD ** -0.5)

        def ln_stats_chunk(f):
            PE(lambda e: e.matmul(PS[P_ST][:, 0:T], onesA[:], xT[:, f, :], start=(f == 0), stop=(f == 7)),
               rd=["onesA", ("xT", f)], wr=[psk(P_ST)], sig=(f == 7))
            t_, tk = tmp()
            ACT(lambda e: e.activation(t_[:], xT[:, f, :], AF.Square), [("xT", f)], [tk])
            PE(lambda e: e.matmul(PS[P_DS][:, 0:T], onesA[:], t_[:], start=(f == 0), stop=(f == 7)),
               rd=["onesA", tk], wr=[psk(P_DS)], sig=(f == 7))

        def layer_norm_stream(l, gcol, bcol):
            pmean, pmsq = P_ST, P_DS
            mean_, mk = lnm, "lnm"
            rstd_, rk = lnr, "lnr"
            ACT(lambda e: e.copy(mean_[:], PS[pmean][:, 0:T]), [psk(pmean)], [mk])
            DVE(lambda e: e.tensor_tensor(rstd_[:], mean_[:], mean_[:], ALU.mult), [mk], [rk])
            DVE(lambda e: e.tensor_tensor(rstd_[:], PS[pmsq][:, 0:T], rstd_[:], ALU.subtract), [psk(pmsq), rk], [rk])
            ACT(lambda e: e.activation(rstd_[:], rstd_[:], AF.Sqrt, bias=eps_t[:, 0:1]), [rk, "eps"], [rk])
            DVE(lambda e: e.reciprocal(rstd_[:], rstd_[:]), [rk], [rk])
            for f in range(8):
                t_, tk = tmp()
                POOL(lambda e, f=f, t_=t_: e.tensor_tensor(t_[:], xT[:, f, :], mean_[:], ALU.subtract), [("xT", f), mk], [tk])
                DVE(lambda e, t_=t_: e.tensor_tensor(t_[:], t_[:], rstd_[:], ALU.mult), [tk, rk], [tk])
                ACT(lambda e, f=f, t_=t_: e.activation(xT[:, f, :], t_[:], AF.Identity,
                                                       bias=pp[:, l, bcol + f:bcol + f + 1], scale=pp[:, l, gcol + f:gcol + f + 1]),
                    [tk, "pp"], [("xT", f)])

        eps_t = sb("eps_t", [128, 1])
        DVE(lambda e: e.memset(eps_t[:], LN_EPS), [], ["eps"])

        def modulate(l, which):
            shc = 0 if which == 0 else 24
            for f in range(8):
                ACT(lambda e, f=f: e.activation(hT[:, f, :], xT[:, f, :], AF.Identity,
                                                bias=mod[:, l, shc + f:shc + f + 1], scale=onep[:, l, which, f:f + 1]),
                    [("xT", f), "mod", "onep"], [("hT", f)])

        HT_KEYS = [("hT", f) for f in range(8)]

        def residual(l, f, pbank, gcol):
            t_, tk = tmp()
            ACT(lambda e, t_=t_: e.activation(t_[:], PS[pbank][:, 0:T], AF.Copy, scale=mod[:, l, gcol + f:gcol + f + 1]),
                [psk(pbank), "mod"], [tk])
            DVE(lambda e, t_=t_: e.scalar_tensor_tensor(xT[:, f, :], xT[:, f, :], ALPHA, t_[:], ALU.mult, ALU.add),
                [("xT", f), tk], [("xT", f)])

        def proj_fm(wv, wk, j, pbank):
            mm(PS[pbank][:, 0:T], psk(pbank),
               [(wv[:, k, j * 128:(j + 1) * 128], hT[:, k, :], [wk, ("hT", k)]) for k in range(8)])

        def proj_tm(wv, wk, c, pbank):
            mm(PS[pbank][:, 0:512], psk(pbank),
               [(hT[:, k, c * 128:(c + 1) * 128], wv[:, k, :], [wk, ("hT", k)]) for k in range(8)])

        def win_group(l, g):
            src = winb[l].rearrange("(k p) c -> p k c", p=128)[:, :, g * 512:(g + 1) * 512]
            return load_w(src, 8, 512, wkeys("winb", l, D, g * 512))

        def wrot_group(l, g):
            src = wrotb[l].rearrange("(k p) c -> p k c", p=128)[:, :, g * 512:(g + 1) * 512]
            return load_w(src, 8, 512, wkeys("wrotb", l, D, g * 512))

        def rope_tables(ti):
            pt_, pk_ = tmp()
            posi = pt_.bitcast(I32)
            S.dma("sp", lambda e: e.dma_start(out=posi[:], in_=pos_d[:, ti * T:(ti + 1) * T].partition_broadcast(128)),
                  rd=[], wr=[pk_])
            ang, ak = tmp()
            kf, kk = tmp()
            r_, rk = tmp()
            m_, mk = tmp()
            ki = kf.bitcast(I32)
            DVE(lambda e: e.tensor_copy(ang[:], posi[:]), [pk_], [ak])
            DVE(lambda e: e.tensor_scalar(ang[:], ang[:], invf, None, ALU.mult), [ak, "cst"], [ak])
            DVE(lambda e: e.tensor_scalar(ki[:], ang[:], 1.0 / TWO_PI, None, ALU.mult), [ak], [kk])
            DVE(lambda e: e.tensor_copy(r_[:], ki[:]), [kk], [rk])
            DVE(lambda e: e.scalar_tensor_tensor(ang[:], r_[:], -CW1, ang[:], ALU.mult, ALU.add), [rk, ak], [ak])
            DVE(lambda e: e.scalar_tensor_tensor(ang[:], r_[:], -CW2, ang[:], ALU.mult, ALU.add), [rk, ak], [ak])

            def wrap(dst, dk, shift):
                DVE(lambda e: e.tensor_scalar(dst[:], ang[:], shift, None, ALU.add), [ak], [dk])
                DVE(lambda e: e.tensor_scalar(m_[:], dst[:], PI, -TWO_PI, ALU.is_gt, ALU.mult), [dk], [mk])
                DVE(lambda e: e.tensor_tensor(dst[:], dst[:], m_[:], ALU.add), [dk, mk], [dk])
                DVE(lambda e: e.tensor_scalar(m_[:], dst[:], -PI, TWO_PI, ALU.is_lt, ALU.mult), [dk], [mk])
                DVE(lambda e: e.tensor_tensor(dst[:], dst[:], m_[:], ALU.add), [dk, mk], [dk])

            wrap(kf, kk, 0.0)
            ACT(lambda e: e.activation(sinT[:], kf[:], AF.Sin, scale=sgn), [kk, "cst"], ["sinT"])
            wrap(kf, kk, PI / 2)
            ACT(lambda e: e.activation(cosT[:], kf[:], AF.Sin), [kk], ["cosT"])
            DVE(lambda e: e.tensor_scalar(coskT[:], cosT[:], KSCALE, None, ALU.mult), ["cosT"], ["coskT"])
            DVE(lambda e: e.tensor_scalar(sinkT[:], sinT[:], KSCALE, None, ALU.mult), ["sinT"], ["sinkT"])

        def retention(l, ti):
            qT, kT, qdT, vv, vkd, ktok = Bb
            for (g, grot, dst, dkey, ct, ck, st_, sk) in ((0, 0, qT, "B0", cosT, "cosT", sinT, "sinT"),
                                                          (1, 1, kT, "B1", coskT, "coskT", sinkT, "sinkT")):
                wv, wk = win_group(l, g)
                wr_, wrk = wrot_group(l, grot)
                for j in range(4):
                    pa, pb = pm(), pm()
                    proj_fm(wv, wk, j, pa)
                    proj_fm(wr_, wrk, j, pb)
                    t1, t1k = tmp()
                    t2, t2k = tmp()
                    DVE(lambda e, t1=t1, pa=pa, ct=ct: e.tensor_tensor(t1[:], PS[pa][:, 0:T], ct[:], ALU.mult), [psk(pa), ck], [t1k])
                    DVE(lambda e, t2=t2, pb=pb, st_=st_: e.tensor_tensor(t2[:], PS[pb][:, 0:T], st_[:], ALU.mult), [psk(pb), sk], [t2k])
                    POOL(lambda e, t1=t1, t2=t2, dst=dst, j=j: e.tensor_tensor(dst[:, j, :], t1[:], t2[:], ALU.add), [t1k, t2k], [(dkey, j)])
                    if g == 0:
                        POOL(lambda e, j=j: e.tensor_tensor(v3(qdT[:, j, :], NB), v3(qT[:, j, :], NB), bcast_mid(qdv[:, j, :], NB), ALU.mult),
                             [("B0", j), "cst"], [("B2", j)])
                    pump(4)
            wv, wk = win_group(l, 2)
            for c in range(NB):
                pa = pm()
                proj_tm(wv, wk, c, pa)
                ACT(lambda e, c=c, pa=pa: e.copy(vv[:, c, :], PS[pa][:, 0:512]), [psk(pa)], [("B3", c)])
                DVE(lambda e, c=c, pa=pa: e.tensor_tensor(v3(vkd[:, c, :], 4), v3(PS[pa][:, 0:512], 4), bcast_last(kdv, 128), ALU.mult),
                    [psk(pa), "cst"], [("B4", c)])
            wv, wk = win_group(l, 3)
            for c in range(NB):
                pa = pm()
                proj_tm(wv, wk, c, pa)
                ACT(lambda e, c=c, pa=pa: e.activation(Y[4 + c][:], PS[pa][:, 0:512], AF.Silu), [psk(pa)], ["Y%d" % (4 + c)])
                POOL(lambda e, c=c: e.tensor_tensor(Y[4 + c][:], Y[4 + c][:], bc[:, 0, 0:512], ALU.mult), ["Y%d" % (4 + c), "bc"], ["Y%d" % (4 + c)])
            if l == 0 and ti == 0:
                dump("q", qT[:], [128, 4, T], BF16, keys=[("B0", j) for j in range(4)])
                dump("k", kT[:], [128, 4, T], BF16, keys=[("B1", j) for j in range(4)])
                dump("qd", qdT[:], [128, 4, T], BF16, keys=[("B2", j) for j in range(4)])
                dump("v", vv[:], [128, 4, 512], BF16, keys=[("B3", j) for j in range(4)])
                dump("vkd", vkd[:], [128, 4, 512], BF16, keys=[("B4", j) for j in range(4)])
                dump("ga", Y[4][:], [128, 512], F32, keys=["Y4"])
            stmb = [STm, STm2]
            stmk = ["STm", "STm2"]

            def prep(c):
                cb = slice(c * 128, (c + 1) * 128)
                for h in range(4):
                    PE(lambda e, h=h: e.transpose(PSB[P_TP][:, h * 128:(h + 1) * 128], kT[:, h, cb], identb[:]),
                       [("B1", h), "identb"], [psk(P_TP)], sig=(h == 3))
                ACT(lambda e: e.copy(ktok[:, c, :], PSB[P_TP][:, 0:512]), [psk(P_TP)], [("B5", c)])
                for h in range(4):
                    PE(lambda e, h=h: e.matmul(PS[P_ST][:, h * 128:(h + 1) * 128], kT[:, h, cb], qT[:, h, cb], start=True, stop=True),
                       [("B1", h), ("B0", h)], [psk(P_ST)], sig=(h == 3))
                DVE(lambda e: e.tensor_tensor(stmb[c % 2][:], v3(PS[P_ST][:, 0:512], 4), maskT, ALU.mult), [psk(P_ST), "cst"], [stmk[c % 2]])
                for h in range(4):
                    hs = slice(h * 128, (h + 1) * 128)
                    PE(lambda e, h=h, hs=hs: e.matmul(PS[P_DS][:, hs], ktok[:, c, hs], vkd[:, c, hs], start=True, stop=True),
                       [("B5", c), ("B4", c)], [psk(P_DS)], sig=(h == 3))

            def out_update(c):
                cb = slice(c * 128, (c + 1) * 128)
                for h in range(4):
                    hs = slice(h * 128, (h + 1) * 128)
                    PE(lambda e, h=h, hs=hs: e.matmul(PS[P_O][:, hs], stmb[c % 2][:, h, :], vv[:, c, hs], start=True, stop=False),
                       [stmk[c % 2], ("B3", c)], [psk(P_O)], sig=False)
                    PE(lambda e, h=h, hs=hs: e.matmul(PS[P_O][:, hs], qdT[:, h, cb], Sbf_ret[l][:, h, :], start=False, stop=True),
                       [("B2", h), "Sbfret%d" % l], [psk(P_O)], sig=(h == 3))
                for h in range(4):
                    hs = slice(h * 128, (h + 1) * 128)
                    DVE(lambda e, h=h, hs=hs: e.scalar_tensor_tensor(S_ret[l][:, h, :], S_ret[l][:, h, :], CD[h], PS[P_DS][:, hs], ALU.mult, ALU.add),
                        ["Sret%d" % l, psk(P_DS)], ["Sret%d" % l])
                ACT(lambda e: e.copy(Sbf_ret[l][:], S_ret[l][:]), ["Sret%d" % l], ["Sbfret%d" % l])

            def norm(c):
                cb = slice(c * 128, (c + 1) * 128)
                s1, s2, mean, var, rstd, nmr = st4
                sq, sqk = tmp()
                DVE(lambda e: e.tensor_reduce(s1[:], v3(PS[P_O][:, 0:512], 4), AX.X, ALU.add), [psk(P_O)], ["s1"])
                ACT(lambda e: e.activation(sq[:, 0:512], PS[P_O][:, 0:512], AF.Square), [psk(P_O)], [sqk])
                DVE(lambda e: e.tensor_reduce(s2[:], v3(sq[:, 0:512], 4), AX.X, ALU.add), [sqk], ["s2"])
                DVE(lambda e: e.tensor_scalar(mean[:], s1[:], 1.0 / HD, None, ALU.mult), ["s1"], ["mean"])
                DVE(lambda e: e.tensor_tensor(var[:], mean[:], mean[:], ALU.mult), ["mean"], ["var"])
                DVE(lambda e: e.scalar_tensor_tensor(var[:], s2[:], 1.0 / HD, var[:], ALU.mult, ALU.subtract), ["s2", "var"], ["var"])
                ACT(lambda e: e.activation(rstd[:], var[:], AF.Sqrt, bias=eps_t[:, 0:1]), ["var", "eps"], ["rstd"])
                DVE(lambda e: e.reciprocal(rstd[:], rstd[:]), ["rstd"], ["rstd"])
                DVE(lambda e: e.scalar_tensor_tensor(nmr[:], mean[:], -1.0, rstd[:], ALU.mult, ALU.mult), ["mean", "rstd"], ["nmr"])
                onb, onk = tmp()
                for h in range(4):
                    hs = slice(h * 128, (h + 1) * 128)
                    ACT(lambda e, h=h, hs=hs: e.activation(onb[:, hs], PS[P_O][:, hs], AF.Identity, bias=nmr[:, h:h + 1], scale=rstd[:, h:h + 1]),
                        [psk(P_O), "nmr", "rstd"], [onk])
                POOL(lambda e: e.tensor_tensor(ubuf[:], onb[:], Y[4 + c][:], ALU.mult), [onk, "Y%d" % (4 + c)], ["ubuf"])
                for h in range(4):
                    hs = slice(h * 128, (h + 1) * 128)
                    PE(lambda e, h=h, hs=hs: e.transpose(PSB[P_TP][:, hs], ubuf[:, hs], identb[:]), ["ubuf", "identb"], [psk(P_TP)], sig=(h == 3))
                ACT(lambda e: e.copy(uT[0][:, :, cb], v3(PSB[P_TP][:, 0:512], 4)), [psk(P_TP)], [("u0T", c)])

            prep(0)
            conv_chunk(l, 0)
            conv_evac(l, 0)
            for c in range(NB):
                out_update(c)
                if c + 1 < NB:
                    prep(c + 1)
                    conv_chunk(l, c + 1)
                norm(c)
                if c + 1 < NB:
                    conv_evac(l, c + 1)

        def hgrn(l, ti):
            qpT, kpT, vh, _kt0, Sb4f, _kt1 = Bb
            Sb4 = Sb4f
            NSC = T // 32
            wq, wqk = win_group(l, 4)
            wf, wfk = win_group(l, 5)
            for j in range(4):
                pq, pf = pm(), pm()
                proj_fm(wq, wqk, j, pq)
                proj_fm(wf, wfk, j, pf)
                ACT(lambda e, j=j, pq=pq: e.activation(Y[j][:], PS[pq][:, 0:T], AF.Silu), [psk(pq)], ["Y%d" % j])
                sig, sgk = tmp()
                sng, snk = tmp()
                cum, cuk = tmp()
                ACT(lambda e, sig=sig, pf=pf: e.activation(sig[:], PS[pf][:, 0:T], AF.Sigmoid), [psk(pf)], [sgk])
                ACT(lambda e, sng=sng, pf=pf: e.activation(sng[:], PS[pf][:, 0:T], AF.Sigmoid, scale=-1.0), [psk(pf)], [snk])
                ACT(lambda e, sig=sig, j=j: e.activation(sig[:], sig[:], AF.Ln, bias=lb[:, l, j:j + 1], scale=oml[:, l, j:j + 1]),
                    [sgk, "lb", "oml"], [sgk])
                DVE(lambda e, sig=sig, cum=cum: e.tensor_tensor_scan(cum[:], d0m[:, 0:T], sig[:], 0.0, ALU.mult, ALU.add), [sgk, "cst"], [cuk])
                ACT(lambda e, sig=sig, cum=cum: e.activation(sig[:], cum[:], AF.Exp), [cuk], [sgk])
                ACT(lambda e, cum=cum: e.activation(cum[:], cum[:], AF.Exp, scale=-1.0), [cuk], [cuk])
                POOL(lambda e, sig=sig, j=j: e.tensor_copy(Lc[:, j, :], sig[:, 31:T:32]), [sgk], [("Lc", j)])
                DVE(lambda e, sig=sig, j=j: e.tensor_tensor(qpT[:, j, :], Y[j][:], sig[:], ALU.mult), ["Y%d" % j, sgk], [("B0", j)])
                DVE(lambda e, sng=sng, cum=cum, j=j: e.scalar_tensor_tensor(kpT[:, j, :], sng[:], oml[:, l, j:j + 1], cum[:], ALU.mult, ALU.mult),
                    [snk, cuk, "oml"], [("B1", j)])
                pump(4)
            wv, wk = win_group(l, 6)
            for c in range(NB):
                pa = pm()
                proj_tm(wv, wk, c, pa)
                ACT(lambda e, c=c, pa=pa: e.copy(vh[:, c, :], PS[pa][:, 0:512]), [psk(pa)], [("B2", c)])
            wv, wk = win_group(l, 7)
            for c in range(NB):
                pa = pm()
                proj_tm(wv, wk, c, pa)
                ACT(lambda e, c=c, pa=pa: e.activation(Y[4 + c][:], PS[pa][:, 0:512], AF.Silu), [psk(pa)], ["Y%d" % (4 + c)])
                POOL(lambda e, c=c: e.tensor_tensor(Y[4 + c][:], Y[4 + c][:], bc[:, 0, 512:1024], ALU.mult), ["Y%d" % (4 + c), "bc"], ["Y%d" % (4 + c)])
            LC_KEYS = [("Lc", j) for j in range(4)]
            ktbuf = [Bb[3], Bb[5]]
            ktkey = ["B3", "B5"]
            stmb = [STm, STm2]
            stmk = ["STm", "STm2"]
            DSB = [0, 1, 2, 3]
            qdst = bass.AP(qmc, 0, [list(qmc[:].ap[0]), [512, 4], [160, 4], [1, 32]])
            kdst = bass.AP(kmc, 0, [list(kmc[:].ap[0]), [512, 4], [160, 4], [1, 32]])

            def prep(c):
                cb = slice(c * 128, (c + 1) * 128)
                kt = ktbuf[c % 2]
                POOL(lambda e: e.tensor_copy(kdst, kpT[:, :, cb].rearrange("p h (i t) -> p h i t", i=4)),
                     [("B1", j) for j in range(4)] + ["kmc"], ["kmc"])
                for h in range(4):
                    PE(lambda e, h=h: e.matmul(PS[P_ST][:, h * 128:(h + 1) * 128], kpT[:, h, cb], qpT[:, h, cb], start=True, stop=True),
                       [("B1", h), ("B0", h)], [psk(P_ST)], sig=(h == 3))
                DVE(lambda e: e.tensor_tensor(stmb[c % 2][:], v3(PS[P_ST][:, 0:512], 4), bcast_mid(bdm, 4), ALU.mult), [psk(P_ST), "cst"], [stmk[c % 2]])
                for half, bank in ((0, P_TP), (1, P_DS)):
                    for hh in range(2):
                        h = half * 2 + hh
                        for I in range(4):
                            col = (hh * 4 + I) * 128
                            PE(lambda e, h=h, I=I, col=col, bank=bank: e.transpose(PSB[bank][:, col:col + 128], kmc[:, h, I, :], identb[:]),
                               ["kmc", "identb"], [psk(bank)], sig=(hh == 1 and I == 3))
                    ACT(lambda e, half=half, bank=bank: e.copy(kt[:, half * 2:half * 2 + 2, :], v3(PSB[bank][:, 0:1024], 2)),
                        [psk(bank)], [(ktkey[c % 2], half)])

            def deltas(c):
                kt = ktbuf[c % 2]
                for h in range(4):
                    hs = slice(h * 128, (h + 1) * 128)
                    for I in range(4):
                        PE(lambda e, h=h, I=I, hs=hs: e.matmul(PS[DSB[h]][:, I * 128:(I + 1) * 128], kt[:, h, I * 128:(I + 1) * 128], vh[:, c, hs],
                                                               start=True, stop=True),
                           [(ktkey[c % 2], h // 2), ("B2", c)], [psk(DSB[h])], sig=(I == 3))

            def chain(c):
                for I in range(4):
                    n = c * 4 + I
                    for h in range(4):
                        if n == 0:
                            lsc = Lprev[l][:, h:h + 1]
                            lkeys = ["Lprev%d" % l]
                        else:
                            lsc = Lc[:, h, n - 1:n]
                            lkeys = [("Lc", h)]
                        ACT(lambda e, h=h, I=I, lsc=lsc: e.activation(Sb4[:, h, I * 128:(I + 1) * 128], R_h[l][:, h, :], AF.Copy, scale=lsc),
                            [("Rh%d" % l, h)] + lkeys, [("B4", h)])
                        DVE(lambda e, h=h, I=I, lsc=lsc: e.scalar_tensor_tensor(R_h[l][:, h, :], R_h[l][:, h, :], lsc, PS[DSB[h]][:, I * 128:(I + 1) * 128],
                                                                               ALU.mult, ALU.add),
                            [("Rh%d" % l, h), psk(DSB[h])] + lkeys, [("Rh%d" % l, h)])

            def outputs(c):
                cb = slice(c * 128, (c + 1) * 128)
                for h in range(4):
                    hs = slice(h * 128, (h + 1) * 128)
                    PE(lambda e, h=h, hs=hs: e.matmul(PS[P_O][:, hs], stmb[c % 2][:, h, :], vh[:, c, hs], start=True, stop=False),
                       [stmk[c % 2], ("B2", c)], [psk(P_O)], sig=False)
                    for I in range(4):
                        PE(lambda e, h=h, I=I, hs=hs: e.matmul(PS[P_O][:, hs], qmc[:, h, I, :], Sb4[:, h, I * 128:(I + 1) * 128], start=False, stop=(I == 3)),
                           ["qmc", ("B4", h)], [psk(P_O)], sig=(I == 3 and h == 3))
                s1, s2, mean, var, rstd, nmr = st4
                sq, sqk = tmp()
                ACT(lambda e, sq=sq: e.activation(sq[:, 0:512], PS[P_O][:, 0:512], AF.Square), [psk(P_O)], [sqk])
                DVE(lambda e, sq=sq: e.tensor_reduce(s2[:], v3(sq[:, 0:512], 4), AX.X, ALU.add), [sqk], ["s2"])
                ACT(lambda e: e.activation(rstd[:], s2[:], AF.Sqrt, bias=eps_t[:, 0:1], scale=1.0 / HD), ["s2", "eps"], ["rstd"])
                DVE(lambda e: e.reciprocal(rstd[:], rstd[:]), ["rstd"], ["rstd"])
                for h in range(4):
                    hs = slice(h * 128, (h + 1) * 128)
                    DVE(lambda e, h=h, hs=hs: e.scalar_tensor_tensor(ubuf[:, hs], PS[P_O][:, hs], rstd[:, h:h + 1], Y[4 + c][:, hs], ALU.mult, ALU.mult),
                        [psk(P_O), "rstd", "Y%d" % (4 + c)], ["ubuf"])
                for h in range(4):
                    hs = slice(h * 128, (h + 1) * 128)
                    PE(lambda e, h=h, hs=hs: e.transpose(PSB[P_TP][:, hs], ubuf[:, hs], identb[:]), ["ubuf", "identb"], [psk(P_TP)], sig=(h == 3))
                ACT(lambda e: e.copy(uT[1][:, :, cb], v3(PSB[P_TP][:, 0:512], 4)), [psk(P_TP)], [("u1T", c)])

            def qmask(c):
                cb = slice(c * 128, (c + 1) * 128)
                POOL(lambda e: e.tensor_copy(qdst, qpT[:, :, cb].rearrange("p h (i t) -> p h i t", i=4)),
                     [("B0", j) for j in range(4)] + ["qmc"], ["qmc"])

            prep(0)
            qmask(0)
            deltas(0)
            for c in range(NB):
                chain(c)
                if c + 1 < NB:
                    prep(c + 1)
                outputs(c)
                if c + 1 < NB:
                    qmask(c + 1)
                    deltas(c + 1)
            POOL(lambda e: e.tensor_copy(Lprev[l][:], Lc[:, :, NSC - 1]), LC_KEYS, ["Lprev%d" % l])

        CA = [sb("cacc%d" % i, [128, T + 2]) for i in range(4)]

        def conv_part1(l, ti):
            wa, wak = win_group(l, 8)
            wb_, wbk = win_group(l, 9)
            POOL(lambda e: e.tensor_copy(G[:, :, 0:30], halo31[l][:]), ["halo31_%d" % l], [("G", j) for j in range(4)])
            for j in range(4):
                pa, pb = pm(), pm()
                proj_fm(wa, wak, j, pa)
                proj_fm(wb_, wbk, j, pb)
                sg, sgk = tmp()
                ACT(lambda e, sg=sg, pb=pb: e.activation(sg[:], PS[pb][:, 0:T], AF.Sigmoid), [psk(pb)], [sgk])
                DVE(lambda e, sg=sg, pa=pa, j=j: e.tensor_tensor(G[:, j, 30:30 + T], PS[pa][:, 0:T], sg[:], ALU.mult), [psk(pa), sgk], [("G", j)])

        def conv_chunk(l, j):
            dg, dgk = load_w(cdiag[l, j].rearrange("p (k c) -> p k c", k=CONV_K), CONV_K, 128, [("cdiag", l, j)])
            pb_ = j
            mm(PS[pb_][:, 0:T], psk(pb_), [(dg[:, k, :], G[:, j, k:k + T], [dgk, ("G", j)]) for k in range(CONV_K)])

        def conv_evac(l, j):
            pb_ = j
            ACT(lambda e: e.activation(CA[j][:, 0:T], PS[pb_][:, 0:T], AF.Identity, bias=pp[:, l, O_CB + j:O_CB + j + 1]),
                [psk(pb_), "pp"], ["cacc%d" % j])
            if j == 3:
                POOL(lambda e: e.tensor_copy(halo31[l][:], G[:, :, T:T + 30]), [("G", jj) for jj in range(4)], ["halo31_%d" % l])

        def conv_part3(l, ti):
            for j in range(4):
                PE(lambda e, j=j: e.matmul(PS[P_ST][:, 0:T], onesB[:], CA[j][:, 0:T], start=(j == 0), stop=(j == 3)), ["onesB", "cacc%d" % j], [psk(P_ST)], sig=(j == 3))
            for j in range(4):
                t_, tk = tmp()
                ACT(lambda e, j=j, t_=t_: e.activation(t_[:], CA[j][:, 0:T], AF.Square), ["cacc%d" % j], [tk])
                PE(lambda e, j=j, t_=t_: e.matmul(PS[P_DS][:, 0:T], onesB[:], t_[:], start=(j == 0), stop=(j == 3)), ["onesB", tk], [psk(P_DS)], sig=(j == 3))
            ACT(lambda e: e.copy(lnm[:], PS[P_ST][:, 0:T]), [psk(P_ST)], ["lnm"])
            DVE(lambda e: e.tensor_tensor(lnr[:], lnm[:], lnm[:], ALU.mult), ["lnm"], ["lnr"])
            DVE(lambda e: e.tensor_tensor(lnr[:], PS[P_DS][:, 0:T], lnr[:], ALU.subtract), [psk(P_DS), "lnr"], ["lnr"])
            ACT(lambda e: e.activation(lnr[:], lnr[:], AF.Sqrt, bias=eps_t[:, 0:1]), ["lnr", "eps"], ["lnr"])
            DVE(lambda e: e.reciprocal(lnr[:], lnr[:]), ["lnr"], ["lnr"])
            for j in range(4):
                t_, tk = tmp()
                POOL(lambda e, j=j, t_=t_: e.tensor_tensor(t_[:], CA[j][:, 0:T], lnm[:], ALU.subtract), ["cacc%d" % j, "lnm"], [tk])
                DVE(lambda e, t_=t_: e.tensor_tensor(t_[:], t_[:], lnr[:], ALU.mult), [tk, "lnr"], [tk])
                ACT(lambda e, j=j, t_=t_: e.activation(t_[:], t_[:], AF.Identity, bias=pp[:, l, O_CLB + j:O_CLB + j + 1],
                                                       scale=pp[:, l, O_CLG + j:O_CLG + j + 1]), [tk, "pp"], [tk])
                ACT(lambda e, j=j, t_=t_: e.activation(uT[2][:, j, :], t_[:], AF.Silu), [tk], [("u2T", j)])

        pump_state = {"gen": None}

        def pump(n):
            g = pump_state["gen"]
            if g is None:
                return
            for _ in range(n):
                try:
                    next(g)
                except StopIteration:
                    pump_state["gen"] = None
                    return

        lnm = sb("lnm", [128, T])
        lnr = sb("lnr", [128, T])

        def merge_and_out(l, ti):
            yT = [Bb[0], Bb[1]]
            for b in range(3):
                src = wbrb[l].rearrange("(k p) c -> p k c", p=128)[:, b * 4:(b + 1) * 4, :]
                wbv, wbk = load_w(src, 4, 1024, [("wbrb", l, r, cc) for r in range(b * 4, b * 4 + 4) for cc in range(2)])
                ukeys = [("u%dT" % b, c) for c in range(4)]
                for half in range(2):
                    wg, wgk = win_group(l, 10 + 2 * b + half)
                    for jj in range(4):
                        f = half * 4 + jj
                        pg, pb_ = pm(), pm()
                        proj_fm(wg, wgk, jj, pg)
                        mm(PS[pb_][:, 0:T], psk(pb_),
                           [(wbv[:, k, f * 128:(f + 1) * 128], uT[b][:, k, :], [wbk] + ukeys) for k in range(4)])
                        sg, sgk = tmp()
                        ACT(lambda e, sg=sg, pg=pg: e.activation(sg[:], PS[pg][:, 0:T], AF.Sigmoid), [psk(pg)], [sgk])
                        if b == 0:
                            DVE(lambda e, sg=sg, pb_=pb_, f=f: e.tensor_tensor(Y[f][:], PS[pb_][:, 0:T], sg[:], ALU.mult), [psk(pb_), sgk], ["Y%d" % f])
                        else:
                            DVE(lambda e, sg=sg, pb_=pb_: e.tensor_tensor(sg[:], PS[pb_][:, 0:T], sg[:], ALU.mult), [psk(pb_), sgk], [sgk])
                            if b == 1:
                                POOL(lambda e, sg=sg, f=f: e.tensor_tensor(Y[f][:], Y[f][:], sg[:], ALU.add), ["Y%d" % f, sgk], ["Y%d" % f])
                            else:
                                POOL(lambda e, sg=sg, f=f: e.tensor_tensor(yT[f // 4][:, f % 4, :], Y[f][:], sg[:], ALU.add),
                                     ["Y%d" % f, sgk], [("B%d" % (f // 4), f % 4)])
            dump("y_%d_%d" % (l, ti), Bb[0][:], [128, 4, T], BF16, keys=[("B0", j) for j in range(4)])
            YK = [("B%d" % (f // 4), f % 4) for f in range(8)]
            for half in range(2):
                src = woutb[l].rearrange("(k p) c -> p k c", p=128)[:, :, half * 512:(half + 1) * 512]
                wo, wok = load_w(src, 8, 512, wkeys("woutb", l, D, half * 512))
                for jj in range(4):
                    f = half * 4 + jj
                    pz = pm()
                    mm(PS[pz][:, 0:T], psk(pz),
                       [(wo[:, k, jj * 128:(jj + 1) * 128], yT[k // 4][:, k % 4, :], [wok, YK[k]]) for k in range(8)])
                    residual(l, f, pz, 16)
            for f in range(8):
                ln_stats_chunk(f)
            layer_norm_stream(l, O_LN1G, O_LN1B)

        def ffn(l, ti):
            def gt(j):
                return Bb[j // 4][:, j % 4, :], ("B%d" % (j // 4), j % 4)

            def conv3(pbank, m):
                i = state["pb"]
                state["pb"] = (i + 1) % 7
                if i < 3:
                    pbf, pbk = pbuf[i], "pbuf%d" % i
                else:
                    pbf, pbk = CA[i - 3], "cacc%d" % (i - 3)
                cw = O_FCW + m * 3
                POOL(lambda e: e.tensor_copy(pbf[:, 0:2], halo3[l][:, m, :]), [("halo3_%d" % l, m)], [pbk])
                ACT(lambda e: e.copy(pbf[:, 2:T + 2], PS[pbank][:, 0:T]), [psk(pbank), pbk], [pbk])
                POOL(lambda e: e.tensor_copy(halo3[l][:, m, :], pbf[:, T:T + 2]), [pbk], [("halo3_%d" % l, m)])
                r_, rk = tmp()
                ACT(lambda e: e.activation(r_[:], PS[pbank][:, 0:T], AF.Identity, bias=pp[:, l, O_FCB + m:O_FCB + m + 1],
                                           scale=pp[:, l, cw + 2:cw + 3]), [psk(pbank), "pp"], [rk])
                DVE(lambda e: e.scalar_tensor_tensor(r_[:], pbf[:, 1:T + 1], pp[:, l, cw + 1:cw + 2], r_[:], ALU.mult, ALU.add), [pbk, "pp", rk], [rk])
                DVE(lambda e: e.scalar_tensor_tensor(r_[:], pbf[:, 0:T], pp[:, l, cw:cw + 1], r_[:], ALU.mult, ALU.add), [pbk, "pp", rk], [rk])
                return r_, rk

            for m in range(11):
                src = wupb[l].rearrange("(k p) c -> p k c", p=128)[:, :, m * 512:(m + 1) * 512]
                wu, wuk = load_w(src, 8, 512, wkeys("wupb", l, D, m * 512))
                for jj in range(2):
                    j = 2 * m + jj
                    pa, pv = pm(), pm()
                    proj_fm(wu, wuk, jj, pa)
                    proj_fm(wu, wuk, 2 + jj, pv)
                    ra, rak = conv3(pa, j)
                    rv, rvk = conv3(pv, 22 + j)
                    ACT(lambda e, ra=ra: e.activation(ra[:], ra[:], AF.Silu), [rak], [rak])
                    gd, gk = gt(j)
                    POOL(lambda e, ra=ra, rv=rv, gd=gd: e.tensor_tensor(gd, ra[:], rv[:], ALU.mult), [rak, rvk], [gk])
            GK = [gt(j)[1] for j in range(22)]
            for f in range(8):
                src = wdnb[l].rearrange("(k p) c -> p k c", p=128)[:, :, f * 128:(f + 1) * 128]
                wd, wdk = load_w(src, 22, 128, wkeys("wdnb", l, D_FF, (f // 4) * 512))
                pz = pm()
                mm(PS[pz][:, 0:T], psk(pz),
                   [(wd[:, k, :], gt(k)[0], [wdk, GK[k]]) for k in range(22)])
                residual(l, f, pz, 40)
            for f in range(8):
                ln_stats_chunk(f)
            layer_norm_stream(l, O_LN2G, O_LN2B)

        XK = [("xT", f) for f in range(8)]

        def stage(name):
            if upto == name:
                raise _StopBuild()

        try:
          stage("setup")
          for ti in range(NT):
              for c in range(NB):
                  xi = xin[c % 2]
                  xk = "xin%d" % (c % 2)
                  r0 = ti * T + c * 128
                  S.dma("sp", lambda e, xi=xi, r0=r0: e.dma_start(out=xi[:], in_=x_d[r0:r0 + 128, :]), rd=[], wr=[xk])
                  for half in range(2):
                      pa = pm()
                      for jj in range(4):
                          f = half * 4 + jj
                          PE(lambda e, xi=xi, f=f, jj=jj, pa=pa: e.transpose(PS[pa][:, jj * 128:(jj + 1) * 128], xi[:, f * 128:(f + 1) * 128], identf[:]),
                             [xk, "identf"], [psk(pa)], sig=(jj == 3))
                      ACT(lambda e, half=half, pa=pa, c=c: e.copy(xT[:, half * 4:half * 4 + 4, c * 128:(c + 1) * 128], v3(PS[pa][:, 0:512], 4)),
                          [psk(pa)], [("xT", half * 4 + jj) for jj in range(4)])
              stage("load")
              rope_tables(ti)
              stage("rope")
              if ti == 0:
                  dump("cos", cosT[:], [128, T], keys=["cosT"])
                  dump("sin", sinT[:], [128, T], keys=["sinT"])
              for l in range(L):
                  modulate(l, 0)
                  stage("mod")
                  S.dma("sp", lambda e, l=l: e.dma_start(out=bc[:, 0, :], in_=bc_d[l]), rd=[], wr=["bc"])
                  conv_part1(l, ti)
                  retention(l, ti)
                  dump("ua_%d_%d" % (l, ti), uT[0][:], [128, 4, T], BF16, keys=[("u0T", c) for c in range(4)])
                  stage("ret")
                  hgrn(l, ti)
                  dump("ub_%d_%d" % (l, ti), uT[1][:], [128, 4, T], BF16, keys=[("u1T", c) for c in range(4)])
                  stage("hgrn")
                  pump(1000)
                  conv_part3(l, ti)
                  dump("uc_%d_%d" % (l, ti), uT[2][:], [128, 4, T], BF16, keys=[("u2T", c) for c in range(4)])
                  stage("conv")
                  merge_and_out(l, ti)
                  dump("x1_%d_%d" % (l, ti), xT[:], [128, 8, T], keys=XK)
                  stage("merge")
                  modulate(l, 1)
                  ffn(l, ti)
                  dump("x2_%d_%d" % (l, ti), xT[:], [128, 8, T], keys=XK)
              for c in range(NB):
                  xi = xin[c % 2]
                  xk = "xin%d" % (c % 2)
                  r0 = ti * T + c * 128
                  for half in range(2):
                      pa = pm()
                      for jj in range(4):
                          f = half * 4 + jj
                          PE(lambda e, f=f, jj=jj, pa=pa, c=c: e.transpose(PS[pa][:, jj * 128:(jj + 1) * 128], xT[:, f, c * 128:(c + 1) * 128], identf[:]),
                             [("xT", f), "identf"], [psk(pa)], sig=(jj == 3))
                      ACT(lambda e, xi=xi, half=half, pa=pa: e.copy(xi[:, half * 512:(half + 1) * 512], PS[pa][:, 0:512]), [psk(pa)], [xk])
                  S.dma("pool", lambda e, xi=xi, r0=r0: e.dma_start(out=out_d[r0:r0 + 128, :], in_=xi[:]), rd=[xk], wr=[("out", r0)])

        except _StopBuild:
            pass

        S.wait_all("sp", S.final_tokens())
        print("ops", S.n_op, "waits", S.n_wait, {e: len(v) for e, v in S.prog.items()}, "sems", len(S.sems))
        S.emit()
    return nc, dbg_outs


def _host_prep(inputs, S_LEN):
    f32 = np.float32
    w_in = np.asarray(inputs["w_in"], f32)
    perm = np.concatenate([h * 128 + (np.arange(128) + 64) % 128 for h in range(NH)])
    w_rot = np.concatenate([w_in[:, :, 0:512][:, :, perm], w_in[:, :, 512:1024][:, :, perm]], axis=2)
    w_up = np.asarray(inputs["ffn_w_up"], f32)
    cols = []
    for m in range(11):
        for jj in range(2):
            cols.append(np.arange((2 * m + jj) * 128, (2 * m + jj + 1) * 128))
        for jj in range(2):
            cols.append(D_FF + np.arange((2 * m + jj) * 128, (2 * m + jj + 1) * 128))
    w_up_p = w_up[:, :, np.concatenate(cols)]

    def pcol(v, n):
        return np.asarray(v, f32).reshape(n, 128).T

    pp = np.zeros((DEPTH, 128, NPP), f32)
    bc = np.zeros((DEPTH, 128, 1024), f32)
    for l in range(DEPTH):
        pp[l, :, O_BADA:O_BADA + 48] = pcol(inputs["b_ada"][l], 48)
        pp[l, :, O_LN1G:O_LN1G + 8] = pcol(inputs["ln1_g"][l], 8)
        pp[l, :, O_LN1B:O_LN1B + 8] = pcol(inputs["ln1_b"][l], 8)
        pp[l, :, O_LN2G:O_LN2G + 8] = pcol(inputs["ln2_g"][l], 8)
        pp[l, :, O_LN2B:O_LN2B + 8] = pcol(inputs["ln2_b"][l], 8)
        cw = np.asarray(inputs["conv_w"][l], f32).reshape(CONV_K, 4, 128).transpose(2, 1, 0)
        pp[l, :, O_CW:O_CW + 124] = cw.reshape(128, 124)
        pp[l, :, O_CB:O_CB + 4] = pcol(inputs["conv_b"][l], 4)
        pp[l, :, O_CLG:O_CLG + 4] = pcol(inputs["conv_ln_g"][l], 4)
        pp[l, :, O_CLB:O_CLB + 4] = pcol(inputs["conv_ln_b"][l], 4)
        for l2 in range(DEPTH):
            pp[l, :, O_LBL + 4 * l2:O_LBL + 4 * l2 + 4] = pcol(inputs["hgrn_lb_logits"][l2], 4)
        fw = np.asarray(inputs["ffn_conv_w"][l], f32).reshape(3, NFC, 128).transpose(2, 1, 0)
        pp[l, :, O_FCW:O_FCW + 132] = fw.reshape(128, 132)
        pp[l, :, O_FCB:O_FCB + NFC] = pcol(inputs["ffn_conv_b"][l], NFC)
        bc[l, :, 0:512] = np.broadcast_to(np.asarray(inputs["ret_norm_g"][l], f32)[None, :], (128, 512))
        bc[l, :, 512:1024] = np.broadcast_to(np.asarray(inputs["hgrn_norm_g"][l], f32)[None, :], (128, 512))

    cst = np.zeros((128, NCONST), np.float64)
    idx = np.arange(128)
    for h in range(NH):
        g = 1.0 - 2.0 ** (-5 - h)
        rel = idx[None, :] - idx[:, None]
        cst[:, C_MASK + h * 128:C_MASK + (h + 1) * 128] = np.where(rel >= 0, g ** np.maximum(rel, 0), 0.0)
        cst[:, C_QD + h * 128:C_QD + (h + 1) * 128] = (g ** (idx + 1.0))[None, :]
        cst[:, C_KD + h] = g ** (127.0 - idx)
    cst[:, C_BD:C_BD + 128] = ((idx[:, None] // 32 == idx[None, :] // 32) & (idx[None, :] >= idx[:, None]))
    half = 64
    inv = (np.float32(10000.0) ** (-np.arange(half, dtype=np.float32) / np.float32(half))).astype(np.float32)
    cst[:, C_INV] = inv[idx % 64]
    cst[:, C_SGN] = np.where(idx < 64, -1.0, 1.0)
    cst[:, C_D0:C_D0 + 512] = (np.arange(512) % 32 != 0)[None, :]
    shared = {
        "w_ada": np.ascontiguousarray(inputs["w_ada"], f32),
        "w_in": np.ascontiguousarray(w_in),
        "w_rot": np.ascontiguousarray(w_rot),
        "w_branch": np.ascontiguousarray(np.asarray(inputs["w_branch"], f32).reshape(DEPTH, 3 * W, D)),
        "w_out": np.ascontiguousarray(inputs["w_out"], f32),
        "w_up": np.ascontiguousarray(w_up_p),
        "w_down": np.ascontiguousarray(inputs["ffn_w_down"], f32),
        "pp": pp, "bc": bc, "consts": cst.astype(f32),
    }
    return shared


_NC_CACHE = {}


def kernel(x, c, positions, w_ada, b_ada, w_in, ret_norm_g, hgrn_lb_logits, hgrn_norm_g,
           conv_w, conv_b, conv_ln_g, conv_ln_b, w_branch, w_out, ln1_g, ln1_b,
           ffn_w_up, ffn_conv_w, ffn_conv_b, ffn_w_down, ln2_g, ln2_b):
    inputs = dict(x=x, c=c, positions=positions, w_ada=w_ada, b_ada=b_ada, w_in=w_in, ret_norm_g=ret_norm_g,
                  hgrn_lb_logits=hgrn_lb_logits, hgrn_norm_g=hgrn_norm_g, conv_w=conv_w, conv_b=conv_b,
                  conv_ln_g=conv_ln_g, conv_ln_b=conv_ln_b, w_branch=w_branch, w_out=w_out, ln1_g=ln1_g, ln1_b=ln1_b,
                  ffn_w_up=ffn_w_up, ffn_conv_w=ffn_conv_w, ffn_conv_b=ffn_conv_b, ffn_w_down=ffn_w_down,
                  ln2_g=ln2_g, ln2_b=ln2_b)
    inputs = {k: np.asarray(v) for k, v in inputs.items()}
    B, S_LEN, _ = inputs["x"].shape
    shared = _host_prep(inputs, S_LEN)
    if S_LEN not in _NC_CACHE:
        _NC_CACHE[S_LEN] = build(S_LEN)[0]
    nc = _NC_CACHE[S_LEN]
    in_maps = []
    for b in range(B):
        m = dict(shared)
        m["x"] = np.ascontiguousarray(inputs["x"][b], np.float32)
        m["pos"] = np.ascontiguousarray(inputs["positions"][b][None, :], np.int32)
        m["cvec"] = np.ascontiguousarray(np.asarray(inputs["c"][b], np.float32).reshape(8, 128).T)
        in_maps.append(m)
    res = run_bass_kernel_spmd(nc, in_maps, core_ids=list(range(B)))
    return np.stack([np.asarray(r["out"], np.float32) for r in res.results], axis=0)
```

```python
import contextlib
import numpy as np
import concourse.bass as bass
import concourse.mybir as mybir
from concourse.bass_utils import run_bass_kernel_spmd

F32 = mybir.dt.float32
BF16 = mybir.dt.bfloat16
I32 = mybir.dt.int32
AF = mybir.ActivationFunctionType
ALU = mybir.AluOpType
AX = mybir.AxisListType


class _Rec:
    def __init__(self):
        self.call = None

    def __getattr__(self, name):
        def f(*a, **k):
            self.call = (name, a, k)
            return self
        return f


def _bind(fn):
    r = _Rec()
    fn(r)
    assert r.call is not None
    return r.call


class Sched:
    COMPUTE = ("pe", "act", "dve", "pool")
    ALL = ("pe", "act", "dve", "pool", "sp")

    def __init__(self, nc, stack, n_dma_sems=24, epoch=30000):
        self.nc = nc
        self.stack = stack
        self.epoch = epoch
        self.prog = {e: [] for e in self.ALL}
        self.sems = []
        self.sem_eng = {}
        self.cur = {}
        self.cnt = {}
        for e in self.COMPUTE:
            self.cur[e] = self._new_sem("c_" + e)
            self.cnt[e] = 0
        self.dma_sems = [self._new_sem("d%d" % i) for i in range(n_dma_sems)]
        self.dma_cnt = [0] * n_dma_sems
        self.dma_rr = 0
        self.seen = {e: {} for e in self.ALL}
        self.last_w = {}
        self.readers = {}
        self.n_wait = 0
        self.n_op = 0

    def _new_sem(self, name):
        h = self.stack.enter_context(self.nc.semaphore(name + "_%d" % len(self.sems)))
        self.sems.append(h)
        if name.startswith("c_"):
            self.sem_eng[len(self.sems) - 1] = name[2:]
        return len(self.sems) - 1

    def _deps(self, rd, wr):
        deps = {}
        def add(tok):
            if tok is None:
                return
            s, v = tok
            if deps.get(s, 0) < v:
                deps[s] = v
        for k in rd:
            add(self.last_w.get(k))
        for k in wr:
            add(self.last_w.get(k))
            for s, v in self.readers.get(k, {}).items():
                add((s, v))
        return deps

    def _emit_waits(self, eng, deps):
        for s, v in deps.items():
            if eng == "pe" and s == self.cur["pe"]:
                continue
            if self.seen[eng].get(s, 0) >= v:
                continue
            self.seen[eng][s] = v
            self.prog[eng].append(("wait", s, v))
            self.n_wait += 1

    def _record(self, tok, rd, wr):
        s, v = tok
        for k in wr:
            self.last_w[k] = tok
            self.readers[k] = {}
        for k in rd:
            r = self.readers.setdefault(k, {})
            if r.get(s, 0) < v:
                r[s] = v

    def op(self, eng, fn, rd=(), wr=(), sig=True):
        deps = self._deps(rd, wr)
        if eng != "pe":
            for k in rd:
                if isinstance(k, str) and k.startswith("ps") and k[2:].isdigit():
                    for s_, v_ in self.readers.get(k, {}).items():
                        if self.sem_eng.get(s_) != eng and deps.get(s_, 0) < v_:
                            deps[s_] = v_
        self._emit_waits(eng, deps)
        if sig and self.cnt[eng] >= self.epoch:
            self.cur[eng] = self._new_sem("c_" + eng)
            self.cnt[eng] = 0
        tok = (self.cur[eng], self.cnt[eng] + 1)
        if sig:
            self.cnt[eng] += 1
            self.prog[eng].append(("op", _bind(fn), tok[0], 1))
        else:
            self.prog[eng].append(("op", _bind(fn), None, 0))
        self._record(tok, rd, wr)
        self.n_op += 1
        return tok

    def dma(self, eng, fn, rd=(), wr=()):
        i = self.dma_rr
        self.dma_rr = (self.dma_rr + 1) % len(self.dma_sems)
        s = self.dma_sems[i]
        deps = self._deps(rd, wr)
        if self.dma_cnt[i] > 0:
            v = 16 * self.dma_cnt[i]
            if deps.get(s, 0) < v:
                deps[s] = v
        self._emit_waits(eng, deps)
        self.dma_cnt[i] += 1
        tok = (s, 16 * self.dma_cnt[i])
        self.prog[eng].append(("op", _bind(fn), s, 16))
        self._record(tok, rd, wr)
        return tok

    def wait_all(self, eng, toks):
        deps = {}
        for s, v in toks:
            if deps.get(s, 0) < v:
                deps[s] = v
        self._emit_waits(eng, deps)

    def final_tokens(self):
        toks = []
        for i, s in enumerate(self.dma_sems):
            if self.dma_cnt[i]:
                toks.append((s, 16 * self.dma_cnt[i]))
        return toks

    def emit(self):
        nc = self.nc
        prog = self.prog
        sems = self.sems

        def replay(eng_obj, lst):
            for it in lst:
                if it[0] == "wait":
                    eng_obj.wait_ge(sems[it[1]], it[2])
                else:
                    name, a, k = it[1]
                    ins = getattr(eng_obj, name)(*a, **k)
                    if it[2] is not None:
                        ins.then_inc(sems[it[2]], it[3])

        with nc.Block() as block:
            @block.sync
            def _(e):
                replay(e, prog["sp"])

            @block.scalar
            def _(e):
                replay(e, prog["act"])

            @block.vector
            def _(e):
                replay(e, prog["dve"])

            @block.gpsimd
            def _(e):
                replay(e, prog["pool"])

            @block.tensor
            def _(e):
                replay(e, prog["pe"])


D = 1024
DEPTH = 2
NH = 4
HD = 128
W = 512
D_IN = 8192
D_FF = 2816
NFC = 2 * D_FF // 128
CONV_K = 31
ALPHA = (2.0 * DEPTH) ** 0.25
LN_EPS = 1e-5
PI = float(np.pi)
TWO_PI = 2.0 * float(np.pi)
CW1 = float(np.float32(6.28125))
CW2 = float(np.float32(TWO_PI - 6.28125))

O_BADA, O_LN1G, O_LN1B, O_LN2G, O_LN2B = 0, 48, 56, 64, 72
O_CW, O_CB, O_CLG, O_CLB, O_LBL, O_FCW, O_FCB, NPP = 80, 204, 208, 212, 216, 224, 356, 400
C_MASK, C_QD, C_KD, C_BD, C_INV, C_SGN, C_D0, NCONST = 0, 512, 1024, 1028, 1156, 1157, 1158, 1158 + 512


def bcast_last(ap, n):
    return bass.AP(ap.tensor, ap.offset, [list(d) for d in ap.ap] + [[0, n]])


def bcast_mid(ap, n):
    d = [list(x) for x in ap.ap]
    return bass.AP(ap.tensor, ap.offset, [d[0], [0, n]] + d[1:])


def v3(ap, a):
    return ap.rearrange("p (a b) -> p a b", a=a)


class _StopBuild(Exception):
    pass


POOL_ENG = "pool"


def build(S_LEN, T=512, n_layers=DEPTH, dbg=(), upto=None):
    nc = bass.Bass("TRN2", target_bir_lowering=False)
    NT = S_LEN // T
    NB = T // 128
    L = n_layers

    def din(name, shape, dt=F32):
        return nc.dram_tensor(name, list(shape), dt, kind="ExternalInput").ap()

    x_d = din("x", [S_LEN, D])
    pos_d = din("pos", [1, S_LEN], I32)
    cvec_d = din("cvec", [128, 8])
    wada_d = din("w_ada", [DEPTH, D, 6 * D])
    win_d = din("w_in", [DEPTH, D, D_IN])
    wrot_d = din("w_rot", [DEPTH, D, 1024])
    wbr_d = din("w_branch", [DEPTH, 3 * W, D])
    wout_d = din("w_out", [DEPTH, D, D])
    wup_d = din("w_up", [DEPTH, D, 2 * D_FF])
    wdn_d = din("w_down", [DEPTH, D_FF, D])
    pp_d = din("pp", [DEPTH, 128, NPP])
    bc_d = din("bc", [DEPTH, 128, 1024])
    const_d = din("consts", [128, NCONST])
    out_d = nc.dram_tensor("out", [S_LEN, D], F32, kind="ExternalOutput").ap()

    winb = nc.dram_tensor("winb", [DEPTH, D, D_IN], BF16).ap()
    wrotb = nc.dram_tensor("wrotb", [DEPTH, D, 1024], BF16).ap()
    wbrb = nc.dram_tensor("wbrb", [DEPTH, 3 * W, D], BF16).ap()
    woutb = nc.dram_tensor("woutb", [DEPTH, D, D], BF16).ap()
    wupb = nc.dram_tensor("wupb", [DEPTH, D, 2 * D_FF], BF16).ap()
    wdnb = nc.dram_tensor("wdnb", [DEPTH, D_FF, D], BF16).ap()

    cdiag = nc.dram_tensor("cdiag", [DEPTH, 4, 128, CONV_K * 128], BF16).ap()

    dbg_outs = {}

    with contextlib.ExitStack() as st:
        S = Sched(nc, st)

        def sb(name, shape, dt=F32):
            return nc.alloc_sbuf_tensor("sb_" + name, list(shape), dt)

        xT = sb("xT", [128, 8, T])
        hT = sb("hT", [128, 8, T], BF16)
        xin = [sb("xin%d" % i, [128, D]) for i in range(2)]
        NSLOT = 4
        SLOT_E = 4096
        wslot = [sb("ws%d" % i, [128, SLOT_E], BF16) for i in range(NSLOT)]
        cosT = sb("cosT", [128, T])
        sinT = sb("sinT", [128, T])
        coskT = sb("coskT", [128, T])
        sinkT = sb("sinkT", [128, T])
        NTMP = 8
        tmps = [sb("tmp%d" % i, [128, T]) for i in range(NTMP)]
        Bb = [sb("B%d" % i, [128, 4, T], BF16) for i in range(6)]
        qmc = sb("qmc", [128, 4, 4, 128], BF16)
        kmc = sb("kmc", [128, 4, 4, 128], BF16)
        Y = [sb("Y%d" % i, [128, T]) for i in range(8)]
        uT = [sb("u%dT" % i, [128, 4, T], BF16) for i in range(3)]
        G = sb("G", [128, 4, T + 30], BF16)
        STm = sb("STm", [128, 4, 128], BF16)
        STm2 = sb("STm2", [128, 4, 128], BF16)
        ubuf = sb("ubuf", [128, 512], BF16)
        st4 = [sb("st4_%d" % i, [128, 4]) for i in range(6)]
        S_ret = [sb("Sret%d" % l, [128, 4, 128]) for l in range(L)]
        Sbf_ret = [sb("Sbfret%d" % l, [128, 4, 128], BF16) for l in range(L)]
        R_h = [sb("Rh%d" % l, [128, 4, 128]) for l in range(L)]
        Lprev = [sb("Lprev%d" % l, [128, 4]) for l in range(L)]
        Lc = sb("Lc", [128, 4, T // 32])
        halo31 = [sb("halo31_%d" % l, [128, 4, 30], BF16) for l in range(L)]
        halo3 = [sb("halo3_%d" % l, [128, NFC, 2]) for l in range(L)]
        pbuf = [sb("pbuf%d" % i, [128, T + 2]) for i in range(3)]
        pp = sb("pp", [128, DEPTH, NPP])
        bc = sb("bc", [128, 1, 1024])
        cst = sb("cst", [128, NCONST])
        cvec = sb("cvec_sb", [128, 8])
        cond = sb("cond", [128, 8])
        mod = sb("mod", [128, DEPTH, 48])
        onep = sb("onep", [128, DEPTH, 2, 8])
        lb = sb("lb", [128, DEPTH, 4])
        oml = sb("oml", [128, DEPTH, 4])
        identf = sb("identf", [128, 128])
        identb = sb("identb", [128, 128], BF16)
        onesA = sb("onesA", [128, 128])
        onesB = sb("onesB", [128, 128])
        PS = [nc.alloc_psum_tensor("ps%d" % i, [128, 512], F32) for i in range(8)]
        PSB = [p.bitcast(BF16) for p in PS]
        print("sbuf bytes remaining", nc.sbuf_bytes_remaining)

        state = {"tmp": 0, "pm": 0, "slot": 0, "pb": 0, "pm_excl": ()}

        def tmp():
            i = state["tmp"]
            state["tmp"] = (i + 1) % NTMP
            return tmps[i], "tmp%d" % i

        def pm():
            while True:
                i = state["pm"]
                state["pm"] = (i + 1) % 8
                if i not in state["pm_excl"]:
                    return i

        P_ST, P_DS, P_O, P_TP = 4, 5, 6, 7

        def psk(i):
            return "ps%d" % i

        def ACT(fn, rd, wr, sig=True):
            return S.op("act", fn, rd, wr, sig)

        def DVE(fn, rd, wr, sig=True):
            return S.op("dve", fn, rd, wr, sig)

        def POOL(fn, rd, wr, sig=True):
            return S.op(POOL_ENG, fn, rd, wr, sig)

        def PE(fn, rd, wr, sig=True):
            return S.op("pe", fn, rd, wr, sig)

        def mm(out_ap, out_key, terms):
            n = len(terms)
            for i, (l_, r_, keys) in enumerate(terms):
                PE(lambda e, l_=l_, r_=r_, i=i: e.matmul(out_ap, l_, r_, start=(i == 0), stop=(i == n - 1)),
                   rd=keys, wr=[out_key], sig=(i == n - 1))

        def load_w(src3, K, C, rdkeys, dt=BF16):
            i = state["slot"]
            state["slot"] = (i + 1) % NSLOT
            if dt == BF16:
                view = wslot[i][:, 0:K * C].rearrange("p (k c) -> p k c", k=K)
            else:
                view = wslot[i].bitcast(F32)[:, 0:K * C].rearrange("p (k c) -> p k c", k=K)
            key = "ws%d" % i
            S.dma("sp", lambda e: e.dma_start(out=view, in_=src3), rd=rdkeys, wr=[key])
            return view, key

        def dump(name, ap, shape, dt=F32, keys=()):
            if name not in dbg:
                return
            d = nc.dram_tensor("dbg_" + name, list(shape), dt, kind="ExternalOutput").ap()
            dbg_outs[name] = d
            S.dma("pool", lambda e: e.dma_start(out=d, in_=ap), rd=list(keys), wr=["dbg_" + name])

        def cast_w(dst, src, rows, name):
            ncol = dst.shape[-1]
            for l in range(L):
                for r0 in range(0, rows, 128):
                    for c0 in range(0, ncol, 512):
                        S.dma("pool", lambda e, l=l, r0=r0, c0=c0: e.dma_start(out=dst[l, r0:r0 + 128, c0:c0 + 512], in_=src[l, r0:r0 + 128, c0:c0 + 512]),
                              rd=[], wr=[(name, l, r0 // 128, c0 // 512)])

        def wkeys(name, l, rows, c0=0, c1=None):
            if c1 is None:
                c1 = c0 + 512
            return [(name, l, r, cc) for r in range(rows // 128) for cc in range(c0 // 512, (c1 + 511) // 512)]

        cast_w(winb, win_d, D, "winb")
        cast_w(wrotb, wrot_d, D, "wrotb")
        cast_w(wbrb, wbr_d, 3 * W, "wbrb")
        cast_w(woutb, wout_d, D, "woutb")
        cast_w(wupb, wup_d, D, "wupb")
        cast_w(wdnb, wdn_d, D_FF, "wdnb")

        S.dma("sp", lambda e: e.dma_start(out=cst[:], in_=const_d), wr=["cst"])
        S.dma("sp", lambda e: e.dma_start(out=cvec[:], in_=cvec_d), wr=["cvec"])
        for l in range(DEPTH):
            S.dma("sp", lambda e, l=l: e.dma_start(out=pp[:, l, :], in_=pp_d[l]), wr=["pp"], rd=["pp"])
        POOL(lambda e: e.memset(identf[:], 1.0), [], ["identf"])
        S.op("pool", lambda e: e.affine_select(out=identf[:], in_=identf[:], pattern=[[-1, 128]], compare_op=ALU.is_equal,
                                               fill=0.0, base=0, channel_multiplier=1), ["identf"], ["identf"])
        DVE(lambda e: e.tensor_copy(identb[:], identf[:]), ["identf"], ["identb"])
        POOL(lambda e: e.memset(onesA[:], 1.0 / D), [], ["onesA"])
        POOL(lambda e: e.memset(onesB[:], 1.0 / W), [], ["onesB"])
        POOL(lambda e: e.memset(qmc[:], 0.0), [], ["qmc"])
        POOL(lambda e: e.memset(kmc[:], 0.0), [], ["kmc"])
        for l in range(L):
            POOL(lambda e, l=l: e.memset(S_ret[l][:], 0.0), [], ["Sret%d" % l])
            POOL(lambda e, l=l: e.memset(Sbf_ret[l][:], 0.0), [], ["Sbfret%d" % l])
            POOL(lambda e, l=l: e.memset(R_h[l][:], 0.0), [], ["Rh%d" % l])
            POOL(lambda e, l=l: e.memset(Lprev[l][:], 1.0), [], ["Lprev%d" % l])
            POOL(lambda e, l=l: e.memset(halo31[l][:], 0.0), [], ["halo31_%d" % l])
            POOL(lambda e, l=l: e.memset(halo3[l][:], 0.0), [], ["halo3_%d" % l])

        ACT(lambda e: e.activation(cond[:], cvec[:], AF.Silu), ["cvec"], ["cond"])
        condr = sb("condr", [128, 8, 8])
        DVE(lambda e: e.tensor_copy(condr[:], bcast_last(cond[:], 8)), ["cond"], ["condr"])
        for l in range(L):
            pmi = pm()
            for g in range(24):
                src = wada_d[l].rearrange("(k p) c -> p k c", p=128)[:, :, g * 256:(g + 1) * 256]
                wv, wk = load_w(src, 8, 256, [], dt=F32)
                for j in range(2):
                    m = g * 2 + j
                    mm(PS[pmi][:, m * 8:(m + 1) * 8], psk(pmi),
                       [(wv[:, k, j * 128:(j + 1) * 128], condr[:, k, :], [wk, "condr"]) for k in range(8)])
            mt_, mtk = tmp()
            ACT(lambda e, pmi=pmi, mt_=mt_: e.copy(mt_[:, 0:384], PS[pmi][:, 0:384]), [psk(pmi)], [mtk])
            DVE(lambda e, l=l, mt_=mt_: e.tensor_tensor(mod[:, l, :], v3(mt_[:, 0:384], 48)[:, :, 0], pp[:, l, O_BADA:O_BADA + 48], ALU.add),
                [mtk, "pp"], ["mod"])
            DVE(lambda e, l=l: e.tensor_scalar(onep[:, l, 0, :], mod[:, l, 8:16], 1.0, None, ALU.add), ["mod"], ["onep"])
            DVE(lambda e, l=l: e.tensor_scalar(onep[:, l, 1, :], mod[:, l, 32:40], 1.0, None, ALU.add), ["mod", "onep"], ["onep"])
        DVE(lambda e: e.memset(lb[:], 0.0), [], ["lb"])
        if L > 1:
            DVE(lambda e: e.tensor_tensor(lb[:, 1, :], pp[:, 0, O_LBL + 4:O_LBL + 8], pp[:, 0, O_LBL:O_LBL + 4], ALU.subtract),
                ["pp", "lb"], ["lb"])
            ACT(lambda e: e.activation(lb[:, 1, :], lb[:, 1, :], AF.Sigmoid), ["lb"], ["lb"])
        DVE(lambda e: e.tensor_scalar(oml[:], lb[:], -1.0, 1.0, ALU.mult, ALU.add), ["lb"], ["oml"])
        for l in range(L):
            for j in range(4):
                i = state["slot"]
                state["slot"] = (i + 1) % NSLOT
                dv = wslot[i][:, 0:CONV_K * 128].rearrange("p (k c) -> p k c", k=CONV_K)
                DVE(lambda e, dv=dv, l=l, j=j: e.tensor_tensor(dv, bcast_mid(identf[:], CONV_K),
                                                               bcast_last(pp[:, l, O_CW + j * 31:O_CW + j * 31 + 31], 128), ALU.mult),
                    ["identf", "pp"], ["ws%d" % i])
                S.dma("pool", lambda e, i=i, l=l, j=j: e.dma_start(out=cdiag[l, j], in_=wslot[i][:, 0:CONV_K * 128]), rd=["ws%d" % i], wr=[("cdiag", l, j)])
        dump("mod", mod[:], [128, DEPTH, 48], keys=["mod"])
        dump("lb", lb[:], [128, DEPTH, 4], keys=["lb"])

        maskT = v3(cst[:, C_MASK:C_MASK + 512], 4)
        qdv = v3(cst[:, C_QD:C_QD + 512], 4)
        kdv = cst[:, C_KD:C_KD + 4]
        bdm = cst[:, C_BD:C_BD + 128]
        invf = cst[:, C_INV:C_INV + 1]
        sgn = cst[:, C_SGN:C_SGN + 1]
        d0m = cst[:, C_D0:C_D0 + 512]
        GAM = [1.0 - 2.0 ** (-5 - h) for h in range(NH)]
        CD = [g ** 128 for g in GAM]
        KSCALE = float(HD ** -0.5)

        def ln_stats_chunk(f):
            PE(lambda e: e.matmul(PS[P_ST][:, 0:T], onesA[:], xT[:, f, :], start=(f == 0), stop=(f == 7)),
               rd=["onesA", ("xT", f)], wr=[psk(P_ST)], sig=(f == 7))
            t_, tk = tmp()
            ACT(lambda e: e.activation(t_[:], xT[:, f, :], AF.Square), [("xT", f)], [tk])
            PE(lambda e: e.matmul(PS[P_DS][:, 0:T], onesA[:], t_[:], start=(f == 0), stop=(f == 7)),
               rd=["onesA", tk], wr=[psk(P_DS)], sig=(f == 7))

        def layer_norm_stream(l, gcol, bcol):
            pmean, pmsq = P_ST, P_DS
            mean_, mk = lnm, "lnm"
            rstd_, rk = lnr, "lnr"
            ACT(lambda e: e.copy(mean_[:], PS[pmean][:, 0:T]), [psk(pmean)], [mk])
            DVE(lambda e: e.tensor_tensor(rstd_[:], mean_[:], mean_[:], ALU.mult), [mk], [rk])
            DVE(lambda e: e.tensor_tensor(rstd_[:], PS[pmsq][:, 0:T], rstd_[:], ALU.subtract), [psk(pmsq), rk], [rk])
            ACT(lambda e: e.activation(rstd_[:], rstd_[:], AF.Sqrt, bias=eps_t[:, 0:1]), [rk, "eps"], [rk])
            DVE(lambda e: e.reciprocal(rstd_[:], rstd_[:]), [rk], [rk])
            for f in range(8):
                t_, tk = tmp()
                POOL(lambda e, f=f, t_=t_: e.tensor_tensor(t_[:], xT[:, f, :], mean_[:], ALU.subtract), [("xT", f), mk], [tk])
                DVE(lambda e, t_=t_: e.tensor_tensor(t_[:], t_[:], rstd_[:], ALU.mult), [tk, rk], [tk])
                ACT(lambda e, f=f, t_=t_: e.activation(xT[:, f, :], t_[:], AF.Identity,
                                                       bias=pp[:, l, bcol + f:bcol + f + 1], scale=pp[:, l, gcol + f:gcol + f + 1]),
                    [tk, "pp"], [("xT", f)])

        eps_t = sb("eps_t", [128, 1])
        DVE(lambda e: e.memset(eps_t[:], LN_EPS), [], ["eps"])

        def modulate(l, which):
            shc = 0 if which == 0 else 24
            for f in range(8):
                ACT(lambda e, f=f: e.activation(hT[:, f, :], xT[:, f, :], AF.Identity,
                                                bias=mod[:, l, shc + f:shc + f + 1], scale=onep[:, l, which, f:f + 1]),
                    [("xT", f), "mod", "onep"], [("hT", f)])

        HT_KEYS = [("hT", f) for f in range(8)]

        def residual(l, f, pbank, gcol):
            t_, tk = tmp()
            ACT(lambda e, t_=t_: e.activation(t_[:], PS[pbank][:, 0:T], AF.Copy, scale=mod[:, l, gcol + f:gcol + f + 1]),
                [psk(pbank), "mod"], [tk])
            DVE(lambda e, t_=t_: e.scalar_tensor_tensor(xT[:, f, :], xT[:, f, :], ALPHA, t_[:], ALU.mult, ALU.add),
                [("xT", f), tk], [("xT", f)])

        def proj_fm(wv, wk, j, pbank):
            mm(PS[pbank][:, 0:T], psk(pbank),
               [(wv[:, k, j * 128:(j + 1) * 128], hT[:, k, :], [wk, ("hT", k)]) for k in range(8)])

        def proj_tm(wv, wk, c, pbank):
            mm(PS[pbank][:, 0:512], psk(pbank),
               [(hT[:, k, c * 128:(c + 1) * 128], wv[:, k, :], [wk, ("hT", k)]) for k in range(8)])

        def win_group(l, g):
            src = winb[l].rearrange("(k p) c -> p k c", p=128)[:, :, g * 512:(g + 1) * 512]
            return load_w(src, 8, 512, wkeys("winb", l, D, g * 512))

        def wrot_group(l, g):
            src = wrotb[l].rearrange("(k p) c -> p k c", p=128)[:, :, g * 512:(g + 1) * 512]
            return load_w(src, 8, 512, wkeys("wrotb", l, D, g * 512))

        def rope_tables(ti):
            pt_, pk_ = tmp()
            posi = pt_.bitcast(I32)
            S.dma("sp", lambda e: e.dma_start(out=posi[:], in_=pos_d[:, ti * T:(ti + 1) * T].partition_broadcast(128)),
                  rd=[], wr=[pk_])
            ang, ak = tmp()
            kf, kk = tmp()
            r_, rk = tmp()
            m_, mk = tmp()
            ki = kf.bitcast(I32)
            DVE(lambda e: e.tensor_copy(ang[:], posi[:]), [pk_], [ak])
            DVE(lambda e: e.tensor_scalar(ang[:], ang[:], invf, None, ALU.mult), [ak, "cst"], [ak])
            DVE(lambda e: e.tensor_scalar(ki[:], ang[:], 1.0 / TWO_PI, None, ALU.mult), [ak], [kk])
            DVE(lambda e: e.tensor_copy(r_[:], ki[:]), [kk], [rk])
            DVE(lambda e: e.scalar_tensor_tensor(ang[:], r_[:], -CW1, ang[:], ALU.mult, ALU.add), [rk, ak], [ak])
            DVE(lambda e: e.scalar_tensor_tensor(ang[:], r_[:], -CW2, ang[:], ALU.mult, ALU.add), [rk, ak], [ak])

            def wrap(dst, dk, shift):
                DVE(lambda e: e.tensor_scalar(dst[:], ang[:], shift, None, ALU.add), [ak], [dk])
                DVE(lambda e: e.tensor_scalar(m_[:], dst[:], PI, -TWO_PI, ALU.is_gt, ALU.mult), [dk], [mk])
                DVE(lambda e: e.tensor_tensor(dst[:], dst[:], m_[:], ALU.add), [dk, mk], [dk])
                DVE(lambda e: e.tensor_scalar(m_[:], dst[:], -PI, TWO_PI, ALU.is_lt, ALU.mult), [dk], [mk])
                DVE(lambda e: e.tensor_tensor(dst[:], dst[:], m_[:], ALU.add), [dk, mk], [dk])

            wrap(kf, kk, 0.0)
            ACT(lambda e: e.activation(sinT[:], kf[:], AF.Sin, scale=sgn), [kk, "cst"], ["sinT"])
            wrap(kf, kk, PI / 2)
            ACT(lambda e: e.activation(cosT[:], kf[:], AF.Sin), [kk], ["cosT"])
            DVE(lambda e: e.tensor_scalar(coskT[:], cosT[:], KSCALE, None, ALU.mult), ["cosT"], ["coskT"])
            DVE(lambda e: e.tensor_scalar(sinkT[:], sinT[:], KSCALE, None, ALU.mult), ["sinT"], ["sinkT"])

        def retention(l, ti):
            qT, kT, qdT, vv, vkd, ktok = Bb
            for (g, grot, dst, dkey, ct, ck, st_, sk) in ((0, 0, qT, "B0", cosT, "cosT", sinT, "sinT"),
                                                          (1, 1, kT, "B1", coskT, "coskT", sinkT, "sinkT")):
                wv, wk = win_group(l, g)
                wr_, wrk = wrot_group(l, grot)
                for j in range(4):
                    pa, pb = pm(), pm()
                    proj_fm(wv, wk, j, pa)
                    proj_fm(wr_, wrk, j, pb)
                    t1, t1k = tmp()
                    t2, t2k = tmp()
                    DVE(lambda e, t1=t1, pa=pa, ct=ct: e.tensor_tensor(t1[:], PS[pa][:, 0:T], ct[:], ALU.mult), [psk(pa), ck], [t1k])
                    DVE(lambda e, t2=t2, pb=pb, st_=st_: e.tensor_tensor(t2[:], PS[pb][:, 0:T], st_[:], ALU.mult), [psk(pb), sk], [t2k])
                    POOL(lambda e, t1=t1, t2=t2, dst=dst, j=j: e.tensor_tensor(dst[:, j, :], t1[:], t2[:], ALU.add), [t1k, t2k], [(dkey, j)])
                    if g == 0:
                        POOL(lambda e, j=j: e.tensor_tensor(v3(qdT[:, j, :], NB), v3(qT[:, j, :], NB), bcast_mid(qdv[:, j, :], NB), ALU.mult),
                             [("B0", j), "cst"], [("B2", j)])
                    pump(4)
            wv, wk = win_group(l, 2)
            for c in range(NB):
                pa = pm()
                proj_tm(wv, wk, c, pa)
                ACT(lambda e, c=c, pa=pa: e.copy(vv[:, c, :], PS[pa][:, 0:512]), [psk(pa)], [("B3", c)])
                DVE(lambda e, c=c, pa=pa: e.tensor_tensor(v3(vkd[:, c, :], 4), v3(PS[pa][:, 0:512], 4), bcast_last(kdv, 128), ALU.mult),
                    [psk(pa), "cst"], [("B4", c)])
            wv, wk = win_group(l, 3)
            for c in range(NB):
                pa = pm()
                proj_tm(wv, wk, c, pa)
                ACT(lambda e, c=c, pa=pa: e.activation(Y[4 + c][:], PS[pa][:, 0:512], AF.Silu), [psk(pa)], ["Y%d" % (4 + c)])
                POOL(lambda e, c=c: e.tensor_tensor(Y[4 + c][:], Y[4 + c][:], bc[:, 0, 0:512], ALU.mult), ["Y%d" % (4 + c), "bc"], ["Y%d" % (4 + c)])
            if l == 0 and ti == 0:
                dump("q", qT[:], [128, 4, T], BF16, keys=[("B0", j) for j in range(4)])
                dump("k", kT[:], [128, 4, T], BF16, keys=[("B1", j) for j in range(4)])
                dump("qd", qdT[:], [128, 4, T], BF16, keys=[("B2", j) for j in range(4)])
                dump("v", vv[:], [128, 4, 512], BF16, keys=[("B3", j) for j in range(4)])
                dump("vkd", vkd[:], [128, 4, 512], BF16, keys=[("B4", j) for j in range(4)])
                dump("ga", Y[4][:], [128, 512], F32, keys=["Y4"])
            stmb = [STm, STm2]
            stmk = ["STm", "STm2"]

            def prep(c):
                cb = slice(c * 128, (c + 1) * 128)
                for h in range(4):
                    PE(lambda e, h=h: e.transpose(PSB[P_TP][:, h * 128:(h + 1) * 128], kT[:, h, cb], identb[:]),
                       [("B1", h), "identb"], [psk(P_TP)], sig=(h == 3))
                ACT(lambda e: e.copy(ktok[:, c, :], PSB[P_TP][:, 0:512]), [psk(P_TP)], [("B5", c)])
                for h in range(4):
                    PE(lambda e, h=h: e.matmul(PS[P_ST][:, h * 128:(h + 1) * 128], kT[:, h, cb], qT[:, h, cb], start=True, stop=True),
                       [("B1", h), ("B0", h)], [psk(P_ST)], sig=(h == 3))
                DVE(lambda e: e.tensor_tensor(stmb[c % 2][:], v3(PS[P_ST][:, 0:512], 4), maskT, ALU.mult), [psk(P_ST), "cst"], [stmk[c % 2]])
                for h in range(4):
                    hs = slice(h * 128, (h + 1) * 128)
                    PE(lambda e, h=h, hs=hs: e.matmul(PS[P_DS][:, hs], ktok[:, c, hs], vkd[:, c, hs], start=True, stop=True),
                       [("B5", c), ("B4", c)], [psk(P_DS)], sig=(h == 3))

            def out_update(c):
                cb = slice(c * 128, (c + 1) * 128)
                for h in range(4):
                    hs = slice(h * 128, (h + 1) * 128)
                    PE(lambda e, h=h, hs=hs: e.matmul(PS[P_O][:, hs], stmb[c % 2][:, h, :], vv[:, c, hs], start=True, stop=False),
                       [stmk[c % 2], ("B3", c)], [psk(P_O)], sig=False)
                    PE(lambda e, h=h, hs=hs: e.matmul(PS[P_O][:, hs], qdT[:, h, cb], Sbf_ret[l][:, h, :], start=False, stop=True),
                       [("B2", h), "Sbfret%d" % l], [psk(P_O)], sig=(h == 3))
                for h in range(4):
                    hs = slice(h * 128, (h + 1) * 128)
                    DVE(lambda e, h=h, hs=hs: e.scalar_tensor_tensor(S_ret[l][:, h, :], S_ret[l][:, h, :], CD[h], PS[P_DS][:, hs], ALU.mult, ALU.add),
                        ["Sret%d" % l, psk(P_DS)], ["Sret%d" % l])
                ACT(lambda e: e.copy(Sbf_ret[l][:], S_ret[l][:]), ["Sret%d" % l], ["Sbfret%d" % l])

            def norm(c):
                cb = slice(c * 128, (c + 1) * 128)
                s1, s2, mean, var, rstd, nmr = st4
                sq, sqk = tmp()
                DVE(lambda e: e.tensor_reduce(s1[:], v3(PS[P_O][:, 0:512], 4), AX.X, ALU.add), [psk(P_O)], ["s1"])
                ACT(lambda e: e.activation(sq[:, 0:512], PS[P_O][:, 0:512], AF.Square), [psk(P_O)], [sqk])
                DVE(lambda e: e.tensor_reduce(s2[:], v3(sq[:, 0:512], 4), AX.X, ALU.add), [sqk], ["s2"])
                DVE(lambda e: e.tensor_scalar(mean[:], s1[:], 1.0 / HD, None, ALU.mult), ["s1"], ["mean"])
                DVE(lambda e: e.tensor_tensor(var[:], mean[:], mean[:], ALU.mult), ["mean"], ["var"])
                DVE(lambda e: e.scalar_tensor_tensor(var[:], s2[:], 1.0 / HD, var[:], ALU.mult, ALU.subtract), ["s2", "var"], ["var"])
                ACT(lambda e: e.activation(rstd[:], var[:], AF.Sqrt, bias=eps_t[:, 0:1]), ["var", "eps"], ["rstd"])
                DVE(lambda e: e.reciprocal(rstd[:], rstd[:]), ["rstd"], ["rstd"])
                DVE(lambda e: e.scalar_tensor_tensor(nmr[:], mean[:], -1.0, rstd[:], ALU.mult, ALU.mult), ["mean", "rstd"], ["nmr"])
                onb, onk = tmp()
                for h in range(4):
                    hs = slice(h * 128, (h + 1) * 128)
                    ACT(lambda e, h=h, hs=hs: e.activation(onb[:, hs], PS[P_O][:, hs], AF.Identity, bias=nmr[:, h:h + 1], scale=rstd[:, h:h + 1]),
                        [psk(P_O), "nmr", "rstd"], [onk])
                POOL(lambda e: e.tensor_tensor(ubuf[:], onb[:], Y[4 + c][:], ALU.mult), [onk, "Y%d" % (4 + c)], ["ubuf"])
                for h in range(4):
                    hs = slice(h * 128, (h + 1) * 128)
                    PE(lambda e, h=h, hs=hs: e.transpose(PSB[P_TP][:, hs], ubuf[:, hs], identb[:]), ["ubuf", "identb"], [psk(P_TP)], sig=(h == 3))
                ACT(lambda e: e.copy(uT[0][:, :, cb], v3(PSB[P_TP][:, 0:512], 4)), [psk(P_TP)], [("u0T", c)])

            prep(0)
            conv_chunk(l, 0)
            conv_evac(l, 0)
            for c in range(NB):
                out_update(c)
                if c + 1 < NB:
                    prep(c + 1)
                    conv_chunk(l, c + 1)
                norm(c)
                if c + 1 < NB:
                    conv_evac(l, c + 1)

        def hgrn(l, ti):
            qpT, kpT, vh, _kt0, Sb4f, _kt1 = Bb
            Sb4 = Sb4f
            NSC = T // 32
            wq, wqk = win_group(l, 4)
            wf, wfk = win_group(l, 5)
            for j in range(4):
                pq, pf = pm(), pm()
                proj_fm(wq, wqk, j, pq)
                proj_fm(wf, wfk, j, pf)
                ACT(lambda e, j=j, pq=pq: e.activation(Y[j][:], PS[pq][:, 0:T], AF.Silu), [psk(pq)], ["Y%d" % j])
                sig, sgk = tmp()
                sng, snk = tmp()
                cum, cuk = tmp()
                ACT(lambda e, sig=sig, pf=pf: e.activation(sig[:], PS[pf][:, 0:T], AF.Sigmoid), [psk(pf)], [sgk])
                ACT(lambda e, sng=sng, pf=pf: e.activation(sng[:], PS[pf][:, 0:T], AF.Sigmoid, scale=-1.0), [psk(pf)], [snk])
                ACT(lambda e, sig=sig, j=j: e.activation(sig[:], sig[:], AF.Ln, bias=lb[:, l, j:j + 1], scale=oml[:, l, j:j + 1]),
                    [sgk, "lb", "oml"], [sgk])
                DVE(lambda e, sig=sig, cum=cum: e.tensor_tensor_scan(cum[:], d0m[:, 0:T], sig[:], 0.0, ALU.mult, ALU.add), [sgk, "cst"], [cuk])
                ACT(lambda e, sig=sig, cum=cum: e.activation(sig[:], cum[:], AF.Exp), [cuk], [sgk])
                ACT(lambda e, cum=cum: e.activation(cum[:], cum[:], AF.Exp, scale=-1.0), [cuk], [cuk])
                POOL(lambda e, sig=sig, j=j: e.tensor_copy(Lc[:, j, :], sig[:, 31:T:32]), [sgk], [("Lc", j)])
                DVE(lambda e, sig=sig, j=j: e.tensor_tensor(qpT[:, j, :], Y[j][:], sig[:], ALU.mult), ["Y%d" % j, sgk], [("B0", j)])
                DVE(lambda e, sng=sng, cum=cum, j=j: e.scalar_tensor_tensor(kpT[:, j, :], sng[:], oml[:, l, j:j + 1], cum[:], ALU.mult, ALU.mult),
                    [snk, cuk, "oml"], [("B1", j)])
                pump(4)
            wv, wk = win_group(l, 6)
            for c in range(NB):
                pa = pm()
                proj_tm(wv, wk, c, pa)
                ACT(lambda e, c=c, pa=pa: e.copy(vh[:, c, :], PS[pa][:, 0:512]), [psk(pa)], [("B2", c)])
            wv, wk = win_group(l, 7)
            for c in range(NB):
                pa = pm()
                proj_tm(wv, wk, c, pa)
                ACT(lambda e, c=c, pa=pa: e.activation(Y[4 + c][:], PS[pa][:, 0:512], AF.Silu), [psk(pa)], ["Y%d" % (4 + c)])
                POOL(lambda e, c=c: e.tensor_tensor(Y[4 + c][:], Y[4 + c][:], bc[:, 0, 512:1024], ALU.mult), ["Y%d" % (4 + c), "bc"], ["Y%d" % (4 + c)])
            LC_KEYS = [("Lc", j) for j in range(4)]
            ktbuf = [Bb[3], Bb[5]]
            ktkey = ["B3", "B5"]
            stmb = [STm, STm2]
            stmk = ["STm", "STm2"]
            DSB = [0, 1, 2, 3]
            qdst = bass.AP(qmc, 0, [list(qmc[:].ap[0]), [512, 4], [160, 4], [1, 32]])
            kdst = bass.AP(kmc, 0, [list(kmc[:].ap[0]), [512, 4], [160, 4], [1, 32]])

            def prep(c):
                cb = slice(c * 128, (c + 1) * 128)
                kt = ktbuf[c % 2]
                POOL(lambda e: e.tensor_copy(kdst, kpT[:, :, cb].rearrange("p h (i t) -> p h i t", i=4)),
                     [("B1", j) for j in range(4)] + ["kmc"], ["kmc"])
                for h in range(4):
                    PE(lambda e, h=h: e.matmul(PS[P_ST][:, h * 128:(h + 1) * 128], kpT[:, h, cb], qpT[:, h, cb], start=True, stop=True),
                       [("B1", h), ("B0", h)], [psk(P_ST)], sig=(h == 3))
                DVE(lambda e: e.tensor_tensor(stmb[c % 2][:], v3(PS[P_ST][:, 0:512], 4), bcast_mid(bdm, 4), ALU.mult), [psk(P_ST), "cst"], [stmk[c % 2]])
                for half, bank in ((0, P_TP), (1, P_DS)):
                    for hh in range(2):
                        h = half * 2 + hh
                        for I in range(4):
                            col = (hh * 4 + I) * 128
                            PE(lambda e, h=h, I=I, col=col, bank=bank: e.transpose(PSB[bank][:, col:col + 128], kmc[:, h, I, :], identb[:]),
                               ["kmc", "identb"], [psk(bank)], sig=(hh == 1 and I == 3))
                    ACT(lambda e, half=half, bank=bank: e.copy(kt[:, half * 2:half * 2 + 2, :], v3(PSB[bank][:, 0:1024], 2)),
                        [psk(bank)], [(ktkey[c % 2], half)])

            def deltas(c):
                kt = ktbuf[c % 2]
                for h in range(4):
                    hs = slice(h * 128, (h + 1) * 128)
                    for I in range(4):
                        PE(lambda e, h=h, I=I, hs=hs: e.matmul(PS[DSB[h]][:, I * 128:(I + 1) * 128], kt[:, h, I * 128:(I + 1) * 128], vh[:, c, hs],
                                                               start=True, stop=True),
                           [(ktkey[c % 2], h // 2), ("B2", c)], [psk(DSB[h])], sig=(I == 3))

            def chain(c):
                for I in range(4):
                    n = c * 4 + I
                    for h in range(4):
                        if n == 0:
                            lsc = Lprev[l][:, h:h + 1]
                            lkeys = ["Lprev%d" % l]
                        else:
                            lsc = Lc[:, h, n - 1:n]
                            lkeys = [("Lc", h)]
                        ACT(lambda e, h=h, I=I, lsc=lsc: e.activation(Sb4[:, h, I * 128:(I + 1) * 128], R_h[l][:, h, :], AF.Copy, scale=lsc),
                            [("Rh%d" % l, h)] + lkeys, [("B4", h)])
                        DVE(lambda e, h=h, I=I, lsc=lsc: e.scalar_tensor_tensor(R_h[l][:, h, :], R_h[l][:, h, :], lsc, PS[DSB[h]][:, I * 128:(I + 1) * 128],
                                                                               ALU.mult, ALU.add),
                            [("Rh%d" % l, h), psk(DSB[h])] + lkeys, [("Rh%d" % l, h)])

            def outputs(c):
                cb = slice(c * 128, (c + 1) * 128)
                for h in range(4):
                    hs = slice(h * 128, (h + 1) * 128)
                    PE(lambda e, h=h, hs=hs: e.matmul(PS[P_O][:, hs], stmb[c % 2][:, h, :], vh[:, c, hs], start=True, stop=False),
                       [stmk[c % 2], ("B2", c)], [psk(P_O)], sig=False)
                    for I in range(4):
                        PE(lambda e, h=h, I=I, hs=hs: e.matmul(PS[P_O][:, hs], qmc[:, h, I, :], Sb4[:, h, I * 128:(I + 1) * 128], start=False, stop=(I == 3)),
                           ["qmc", ("B4", h)], [psk(P_O)], sig=(I == 3 and h == 3))
                s1, s2, mean, var, rstd, nmr = st4
                sq, sqk = tmp()
                ACT(lambda e, sq=sq: e.activation(sq[:, 0:512], PS[P_O][:, 0:512], AF.Square), [psk(P_O)], [sqk])
                DVE(lambda e, sq=sq: e.tensor_reduce(s2[:], v3(sq[:, 0:512], 4), AX.X, ALU.add), [sqk], ["s2"])
                ACT(lambda e: e.activation(rstd[:], s2[:], AF.Sqrt, bias=eps_t[:, 0:1], scale=1.0 / HD), ["s2", "eps"], ["rstd"])
                DVE(lambda e: e.reciprocal(rstd[:], rstd[:]), ["rstd"], ["rstd"])
                for h in range(4):
                    hs = slice(h * 128, (h + 1) * 128)
                    DVE(lambda e, h=h, hs=hs: e.scalar_tensor_tensor(ubuf[:, hs], PS[P_O][:, hs], rstd[:, h:h + 1], Y[4 + c][:, hs], ALU.mult, ALU.mult),
                        [psk(P_O), "rstd", "Y%d" % (4 + c)], ["ubuf"])
                for h in range(4):
                    hs = slice(h * 128, (h + 1) * 128)
                    PE(lambda e, h=h, hs=hs: e.transpose(PSB[P_TP][:, hs], ubuf[:, hs], identb[:]), ["ubuf", "identb"], [psk(P_TP)], sig=(h == 3))
                ACT(lambda e: e.copy(uT[1][:, :, cb], v3(PSB[P_TP][:, 0:512], 4)), [psk(P_TP)], [("u1T", c)])

            def qmask(c):
                cb = slice(c * 128, (c + 1) * 128)
                POOL(lambda e: e.tensor_copy(qdst, qpT[:, :, cb].rearrange("p h (i t) -> p h i t", i=4)),
                     [("B0", j) for j in range(4)] + ["qmc"], ["qmc"])

            prep(0)
            qmask(0)
            deltas(0)
            for c in range(NB):
                chain(c)
                if c + 1 < NB:
                    prep(c + 1)
                outputs(c)
                if c + 1 < NB:
                    qmask(c + 1)
                    deltas(c + 1)
            POOL(lambda e: e.tensor_copy(Lprev[l][:], Lc[:, :, NSC - 1]), LC_KEYS, ["Lprev%d" % l])

        CA = [sb("cacc%d" % i, [128, T + 2]) for i in range(4)]

        def conv_part1(l, ti):
            wa, wak = win_group(l, 8)
            wb_, wbk = win_group(l, 9)
            POOL(lambda e: e.tensor_copy(G[:, :, 0:30], halo31[l][:]), ["halo31_%d" % l], [("G", j) for j in range(4)])
            for j in range(4):
                pa, pb = pm(), pm()
                proj_fm(wa, wak, j, pa)
                proj_fm(wb_, wbk, j, pb)
                sg, sgk = tmp()
                ACT(lambda e, sg=sg, pb=pb: e.activation(sg[:], PS[pb][:, 0:T], AF.Sigmoid), [psk(pb)], [sgk])
                DVE(lambda e, sg=sg, pa=pa, j=j: e.tensor_tensor(G[:, j, 30:30 + T], PS[pa][:, 0:T], sg[:], ALU.mult), [psk(pa), sgk], [("G", j)])

        def conv_chunk(l, j):
            dg, dgk = load_w(cdiag[l, j].rearrange("p (k c) -> p k c", k=CONV_K), CONV_K, 128, [("cdiag", l, j)])
            pb_ = j
            mm(PS[pb_][:, 0:T], psk(pb_), [(dg[:, k, :], G[:, j, k:k + T], [dgk, ("G", j)]) for k in range(CONV_K)])

        def conv_evac(l, j):
            pb_ = j
            ACT(lambda e: e.activation(CA[j][:, 0:T], PS[pb_][:, 0:T], AF.Identity, bias=pp[:, l, O_CB + j:O_CB + j + 1]),
                [psk(pb_), "pp"], ["cacc%d" % j])
            if j == 3:
                POOL(lambda e: e.tensor_copy(halo31[l][:], G[:, :, T:T + 30]), [("G", jj) for jj in range(4)], ["halo31_%d" % l])

        def conv_part3(l, ti):
            for j in range(4):
                PE(lambda e, j=j: e.matmul(PS[P_ST][:, 0:T], onesB[:], CA[j][:, 0:T], start=(j == 0), stop=(j == 3)), ["onesB", "cacc%d" % j], [psk(P_ST)], sig=(j == 3))
            for j in range(4):
                t_, tk = tmp()
                ACT(lambda e, j=j, t_=t_: e.activation(t_[:], CA[j][:, 0:T], AF.Square), ["cacc%d" % j], [tk])
                PE(lambda e, j=j, t_=t_: e.matmul(PS[P_DS][:, 0:T], onesB[:], t_[:], start=(j == 0), stop=(j == 3)), ["onesB", tk], [psk(P_DS)], sig=(j == 3))
            ACT(lambda e: e.copy(lnm[:], PS[P_ST][:, 0:T]), [psk(P_ST)], ["lnm"])
            DVE(lambda e: e.tensor_tensor(lnr[:], lnm[:], lnm[:], ALU.mult), ["lnm"], ["lnr"])
            DVE(lambda e: e.tensor_tensor(lnr[:], PS[P_DS][:, 0:T], lnr[:], ALU.subtract), [psk(P_DS), "lnr"], ["lnr"])
            ACT(lambda e: e.activation(lnr[:], lnr[:], AF.Sqrt, bias=eps_t[:, 0:1]), ["lnr", "eps"], ["lnr"])
            DVE(lambda e: e.reciprocal(lnr[:], lnr[:]), ["lnr"], ["lnr"])
            for j in range(4):
                t_, tk = tmp()
                POOL(lambda e, j=j, t_=t_: e.tensor_tensor(t_[:], CA[j][:, 0:T], lnm[:], ALU.subtract), ["cacc%d" % j, "lnm"], [tk])
                DVE(lambda e, t_=t_: e.tensor_tensor(t_[:], t_[:], lnr[:], ALU.mult), [tk, "lnr"], [tk])
                ACT(lambda e, j=j, t_=t_: e.activation(t_[:], t_[:], AF.Identity, bias=pp[:, l, O_CLB + j:O_CLB + j + 1],
                                                       scale=pp[:, l, O_CLG + j:O_CLG + j + 1]), [tk, "pp"], [tk])
                ACT(lambda e, j=j, t_=t_: e.activation(uT[2][:, j, :], t_[:], AF.Silu), [tk], [("u2T", j)])

        pump_state = {"gen": None}

        def pump(n):
            g = pump_state["gen"]
            if g is None:
                return
            for _ in range(n):
                try:
                    next(g)
                except StopIteration:
                    pump_state["gen"] = None
                    return

        lnm = sb("lnm", [128, T])
        lnr = sb("lnr", [128, T])

        def merge_and_out(l, ti):
            yT = [Bb[0], Bb[1]]
            for b in range(3):
                src = wbrb[l].rearrange("(k p) c -> p k c", p=128)[:, b * 4:(b + 1) * 4, :]
                wbv, wbk = load_w(src, 4, 1024, [("wbrb", l, r, cc) for r in range(b * 4, b * 4 + 4) for cc in range(2)])
                ukeys = [("u%dT" % b, c) for c in range(4)]
                for half in range(2):
                    wg, wgk = win_group(l, 10 + 2 * b + half)
                    for jj in range(4):
                        f = half * 4 + jj
                        pg, pb_ = pm(), pm()
                        proj_fm(wg, wgk, jj, pg)
                        mm(PS[pb_][:, 0:T], psk(pb_),
                           [(wbv[:, k, f * 128:(f + 1) * 128], uT[b][:, k, :], [wbk] + ukeys) for k in range(4)])
                        sg, sgk = tmp()
                        ACT(lambda e, sg=sg, pg=pg: e.activation(sg[:], PS[pg][:, 0:T], AF.Sigmoid), [psk(pg)], [sgk])
                        if b == 0:
                            DVE(lambda e, sg=sg, pb_=pb_, f=f: e.tensor_tensor(Y[f][:], PS[pb_][:, 0:T], sg[:], ALU.mult), [psk(pb_), sgk], ["Y%d" % f])
                        else:
                            DVE(lambda e, sg=sg, pb_=pb_: e.tensor_tensor(sg[:], PS[pb_][:, 0:T], sg[:], ALU.mult), [psk(pb_), sgk], [sgk])
                            if b == 1:
                                POOL(lambda e, sg=sg, f=f: e.tensor_tensor(Y[f][:], Y[f][:], sg[:], ALU.add), ["Y%d" % f, sgk], ["Y%d" % f])
                            else:
                                POOL(lambda e, sg=sg, f=f: e.tensor_tensor(yT[f // 4][:, f % 4, :], Y[f][:], sg[:], ALU.add),
                                     ["Y%d" % f, sgk], [("B%d" % (f // 4), f % 4)])
            dump("y_%d_%d" % (l, ti), Bb[0][:], [128, 4, T], BF16, keys=[("B0", j) for j in range(4)])
            YK = [("B%d" % (f // 4), f % 4) for f in range(8)]
            for half in range(2):
                src = woutb[l].rearrange("(k p) c -> p k c", p=128)[:, :, half * 512:(half + 1) * 512]
                wo, wok = load_w(src, 8, 512, wkeys("woutb", l, D, half * 512))
                for jj in range(4):
                    f = half * 4 + jj
                    pz = pm()
                    mm(PS[pz][:, 0:T], psk(pz),
                       [(wo[:, k, jj * 128:(jj + 1) * 128], yT[k // 4][:, k % 4, :], [wok, YK[k]]) for k in range(8)])
                    residual(l, f, pz, 16)
            for f in range(8):
                ln_stats_chunk(f)
            layer_norm_stream(l, O_LN1G, O_LN1B)

        def ffn(l, ti):
            def gt(j):
                return Bb[j // 4][:, j % 4, :], ("B%d" % (j // 4), j % 4)

            def conv3(pbank, m):
                i = state["pb"]
                state["pb"] = (i + 1) % 7
                if i < 3:
                    pbf, pbk = pbuf[i], "pbuf%d" % i
                else:
                    pbf, pbk = CA[i - 3], "cacc%d" % (i - 3)
                cw = O_FCW + m * 3
                POOL(lambda e: e.tensor_copy(pbf[:, 0:2], halo3[l][:, m, :]), [("halo3_%d" % l, m)], [pbk])
                ACT(lambda e: e.copy(pbf[:, 2:T + 2], PS[pbank][:, 0:T]), [psk(pbank), pbk], [pbk])
                POOL(lambda e: e.tensor_copy(halo3[l][:, m, :], pbf[:, T:T + 2]), [pbk], [("halo3_%d" % l, m)])
                r_, rk = tmp()
                ACT(lambda e: e.activation(r_[:], PS[pbank][:, 0:T], AF.Identity, bias=pp[:, l, O_FCB + m:O_FCB + m + 1],
                                           scale=pp[:, l, cw + 2:cw + 3]), [psk(pbank), "pp"], [rk])
                DVE(lambda e: e.scalar_tensor_tensor(r_[:], pbf[:, 1:T + 1], pp[:, l, cw + 1:cw + 2], r_[:], ALU.mult, ALU.add), [pbk, "pp", rk], [rk])
                DVE(lambda e: e.scalar_tensor_tensor(r_[:], pbf[:, 0:T], pp[:, l, cw:cw + 1], r_[:], ALU.mult, ALU.add), [pbk, "pp", rk], [rk])
                return r_, rk

            for m in range(11):
                src = wupb[l].rearrange("(k p) c -> p k c", p=128)[:, :, m * 512:(m + 1) * 512]
                wu, wuk = load_w(src, 8, 512, wkeys("wupb", l, D, m * 512))
                for jj in range(2):
                    j = 2 * m + jj
                    pa, pv = pm(), pm()
                    proj_fm(wu, wuk, jj, pa)
                    proj_fm(wu, wuk, 2 + jj, pv)
                    ra, rak = conv3(pa, j)
                    rv, rvk = conv3(pv, 22 + j)
                    ACT(lambda e, ra=ra: e.activation(ra[:], ra[:], AF.Silu), [rak], [rak])
                    gd, gk = gt(j)
                    POOL(lambda e, ra=ra, rv=rv, gd=gd: e.tensor_tensor(gd, ra[:], rv[:], ALU.mult), [rak, rvk], [gk])
            GK = [gt(j)[1] for j in range(22)]
            for fp in range(4):
                halves = []
                for kh in range(2):
                    src = wdnb[l].rearrange("(k p) c -> p k c", p=128)[:, kh * 11:(kh + 1) * 11, fp * 256:(fp + 1) * 256]
                    halves.append(load_w(src, 11, 256, wkeys("wdnb", l, D_FF, (fp // 2) * 512)))
                for jj in range(2):
                    f = fp * 2 + jj
                    pz = pm()
                    mm(PS[pz][:, 0:T], psk(pz),
                       [(halves[k // 11][0][:, k % 11, jj * 128:(jj + 1) * 128], gt(k)[0], [halves[k // 11][1], GK[k]]) for k in range(22)])
                    residual(l, f, pz, 40)
            for f in range(8):
                ln_stats_chunk(f)
            layer_norm_stream(l, O_LN2G, O_LN2B)

        XK = [("xT", f) for f in range(8)]

        def stage(name):
            if upto == name:
                raise _StopBuild()

        try:
          stage("setup")
          for ti in range(NT):
              for c in range(NB):
                  xi = xin[c % 2]
                  xk = "xin%d" % (c % 2)
                  r0 = ti * T + c * 128
                  S.dma("sp", lambda e, xi=xi, r0=r0: e.dma_start(out=xi[:], in_=x_d[r0:r0 + 128, :]), rd=[], wr=[xk])
                  for half in range(2):
                      pa = pm()
                      for jj in range(4):
                          f = half * 4 + jj
                          PE(lambda e, xi=xi, f=f, jj=jj, pa=pa: e.transpose(PS[pa][:, jj * 128:(jj + 1) * 128], xi[:, f * 128:(f + 1) * 128], identf[:]),
                             [xk, "identf"], [psk(pa)], sig=(jj == 3))
                      ACT(lambda e, half=half, pa=pa, c=c: e.copy(xT[:, half * 4:half * 4 + 4, c * 128:(c + 1) * 128], v3(PS[pa][:, 0:512], 4)),
                          [psk(pa)], [("xT", half * 4 + jj) for jj in range(4)])
              stage("load")
              rope_tables(ti)
              stage("rope")
              if ti == 0:
                  dump("cos", cosT[:], [128, T], keys=["cosT"])
                  dump("sin", sinT[:], [128, T], keys=["sinT"])
              for l in range(L):
                  modulate(l, 0)
                  stage("mod")
                  S.dma("sp", lambda e, l=l: e.dma_start(out=bc[:, 0, :], in_=bc_d[l]), rd=[], wr=["bc"])
                  conv_part1(l, ti)
                  retention(l, ti)
                  dump("ua_%d_%d" % (l, ti), uT[0][:], [128, 4, T], BF16, keys=[("u0T", c) for c in range(4)])
                  stage("ret")
                  hgrn(l, ti)
                  dump("ub_%d_%d" % (l, ti), uT[1][:], [128, 4, T], BF16, keys=[("u1T", c) for c in range(4)])
                  stage("hgrn")
                  pump(1000)
                  conv_part3(l, ti)
                  dump("uc_%d_%d" % (l, ti), uT[2][:], [128, 4, T], BF16, keys=[("u2T", c) for c in range(4)])
                  stage("conv")
                  merge_and_out(l, ti)
                  dump("x1_%d_%d" % (l, ti), xT[:], [128, 8, T], keys=XK)
                  stage("merge")
                  modulate(l, 1)
                  ffn(l, ti)
                  dump("x2_%d_%d" % (l, ti), xT[:], [128, 8, T], keys=XK)
              for c in range(NB):
                  xi = xin[c % 2]
                  xk = "xin%d" % (c % 2)
                  r0 = ti * T + c * 128
                  for half in range(2):
                      pa = pm()
                      for jj in range(4):
                          f = half * 4 + jj
                          PE(lambda e, f=f, jj=jj, pa=pa, c=c: e.transpose(PS[pa][:, jj * 128:(jj + 1) * 128], xT[:, f, c * 128:(c + 1) * 128], identf[:]),
                             [("xT", f), "identf"], [psk(pa)], sig=(jj == 3))
                      ACT(lambda e, xi=xi, half=half, pa=pa: e.copy(xi[:, half * 512:(half + 1) * 512], PS[pa][:, 0:512]), [psk(pa)], [xk])
                  S.dma("pool", lambda e, xi=xi, r0=r0: e.dma_start(out=out_d[r0:r0 + 128, :], in_=xi[:]), rd=[xk], wr=[("out", r0)])

        except _StopBuild:
            pass

        S.wait_all("sp", S.final_tokens())
        print("ops", S.n_op, "waits", S.n_wait, {e: len(v) for e, v in S.prog.items()}, "sems", len(S.sems))
        S.emit()
    return nc, dbg_outs


def _host_prep(inputs, S_LEN):
    f32 = np.float32
    w_in = np.asarray(inputs["w_in"], f32)
    perm = np.concatenate([h * 128 + (np.arange(128) + 64) % 128 for h in range(NH)])
    w_rot = np.concatenate([w_in[:, :, 0:512][:, :, perm], w_in[:, :, 512:1024][:, :, perm]], axis=2)
    w_up = np.asarray(inputs["ffn_w_up"], f32)
    cols = []
    for m in range(11):
        for jj in range(2):
            cols.append(np.arange((2 * m + jj) * 128, (2 * m + jj + 1) * 128))
        for jj in range(2):
            cols.append(D_FF + np.arange((2 * m + jj) * 128, (2 * m + jj + 1) * 128))
    w_up_p = w_up[:, :, np.concatenate(cols)]

    def pcol(v, n):
        return np.asarray(v, f32).reshape(n, 128).T

    pp = np.zeros((DEPTH, 128, NPP), f32)
    bc = np.zeros((DEPTH, 128, 1024), f32)
    for l in range(DEPTH):
        pp[l, :, O_BADA:O_BADA + 48] = pcol(inputs["b_ada"][l], 48)
        pp[l, :, O_LN1G:O_LN1G + 8] = pcol(inputs["ln1_g"][l], 8)
        pp[l, :, O_LN1B:O_LN1B + 8] = pcol(inputs["ln1_b"][l], 8)
        pp[l, :, O_LN2G:O_LN2G + 8] = pcol(inputs["ln2_g"][l], 8)
        pp[l, :, O_LN2B:O_LN2B + 8] = pcol(inputs["ln2_b"][l], 8)
        cw = np.asarray(inputs["conv_w"][l], f32).reshape(CONV_K, 4, 128).transpose(2, 1, 0)
        pp[l, :, O_CW:O_CW + 124] = cw.reshape(128, 124)
        pp[l, :, O_CB:O_CB + 4] = pcol(inputs["conv_b"][l], 4)
        pp[l, :, O_CLG:O_CLG + 4] = pcol(inputs["conv_ln_g"][l], 4)
        pp[l, :, O_CLB:O_CLB + 4] = pcol(inputs["conv_ln_b"][l], 4)
        for l2 in range(DEPTH):
            pp[l, :, O_LBL + 4 * l2:O_LBL + 4 * l2 + 4] = pcol(inputs["hgrn_lb_logits"][l2], 4)
        fw = np.asarray(inputs["ffn_conv_w"][l], f32).reshape(3, NFC, 128).transpose(2, 1, 0)
        pp[l, :, O_FCW:O_FCW + 132] = fw.reshape(128, 132)
        pp[l, :, O_FCB:O_FCB + NFC] = pcol(inputs["ffn_conv_b"][l], NFC)
        bc[l, :, 0:512] = np.broadcast_to(np.asarray(inputs["ret_norm_g"][l], f32)[None, :], (128, 512))
        bc[l, :, 512:1024] = np.broadcast_to(np.asarray(inputs["hgrn_norm_g"][l], f32)[None, :], (128, 512))

    cst = np.zeros((128, NCONST), np.float64)
    idx = np.arange(128)
    for h in range(NH):
        g = 1.0 - 2.0 ** (-5 - h)
        rel = idx[None, :] - idx[:, None]
        cst[:, C_MASK + h * 128:C_MASK + (h + 1) * 128] = np.where(rel >= 0, g ** np.maximum(rel, 0), 0.0)
        cst[:, C_QD + h * 128:C_QD + (h + 1) * 128] = (g ** (idx + 1.0))[None, :]
        cst[:, C_KD + h] = g ** (127.0 - idx)
    cst[:, C_BD:C_BD + 128] = ((idx[:, None] // 32 == idx[None, :] // 32) & (idx[None, :] >= idx[:, None]))
    half = 64
    inv = (np.float32(10000.0) ** (-np.arange(half, dtype=np.float32) / np.float32(half))).astype(np.float32)
    cst[:, C_INV] = inv[idx % 64]
    cst[:, C_SGN] = np.where(idx < 64, -1.0, 1.0)
    cst[:, C_D0:C_D0 + 512] = (np.arange(512) % 32 != 0)[None, :]
    shared = {
        "w_ada": np.ascontiguousarray(inputs["w_ada"], f32),
        "w_in": np.ascontiguousarray(w_in),
        "w_rot": np.ascontiguousarray(w_rot),
        "w_branch": np.ascontiguousarray(np.asarray(inputs["w_branch"], f32).reshape(DEPTH, 3 * W, D)),
        "w_out": np.ascontiguousarray(inputs["w_out"], f32),
        "w_up": np.ascontiguousarray(w_up_p),
        "w_down": np.ascontiguousarray(inputs["ffn_w_down"], f32),
        "pp": pp, "bc": bc, "consts": cst.astype(f32),
    }
    return shared


_NC_CACHE = {}


def kernel(x, c, positions, w_ada, b_ada, w_in, ret_norm_g, hgrn_lb_logits, hgrn_norm_g,
           conv_w, conv_b, conv_ln_g, conv_ln_b, w_branch, w_out, ln1_g, ln1_b,
           ffn_w_up, ffn_conv_w, ffn_conv_b, ffn_w_down, ln2_g, ln2_b):
    inputs = dict(x=x, c=c, positions=positions, w_ada=w_ada, b_ada=b_ada, w_in=w_in, ret_norm_g=ret_norm_g,
                  hgrn_lb_logits=hgrn_lb_logits, hgrn_norm_g=hgrn_norm_g, conv_w=conv_w, conv_b=conv_b,
                  conv_ln_g=conv_ln_g, conv_ln_b=conv_ln_b, w_branch=w_branch, w_out=w_out, ln1_g=ln1_g, ln1_b=ln1_b,
                  ffn_w_up=ffn_w_up, ffn_conv_w=ffn_conv_w, ffn_conv_b=ffn_conv_b, ffn_w_down=ffn_w_down,
                  ln2_g=ln2_g, ln2_b=ln2_b)
    inputs = {k: np.asarray(v) for k, v in inputs.items()}
    B, S_LEN, _ = inputs["x"].shape
    shared = _host_prep(inputs, S_LEN)
    if S_LEN not in _NC_CACHE:
        _NC_CACHE[S_LEN] = build(S_LEN)[0]
    nc = _NC_CACHE[S_LEN]
    in_maps = []
    for b in range(B):
        m = dict(shared)
        m["x"] = np.ascontiguousarray(inputs["x"][b], np.float32)
        m["pos"] = np.ascontiguousarray(inputs["positions"][b][None, :], np.int32)
        m["cvec"] = np.ascontiguousarray(np.asarray(inputs["c"][b], np.float32).reshape(8, 128).T)
        in_maps.append(m)
    res = run_bass_kernel_spmd(nc, in_maps, core_ids=list(range(B)))
    return np.stack([np.asarray(r["out"], np.float32) for r in res.results], axis=0)
```

```python
import contextlib
import numpy as np
import concourse.bass as bass
import concourse.mybir as mybir
from concourse.bass_utils import run_bass_kernel_spmd

F32 = mybir.dt.float32
BF16 = mybir.dt.bfloat16
I32 = mybir.dt.int32
AF = mybir.ActivationFunctionType
ALU = mybir.AluOpType
AX = mybir.AxisListType


class _Rec:
    def __init__(self):
        self.call = None

    def __getattr__(self, name):
        def f(*a, **k):
            self.call = (name, a, k)
            return self
        return f


def _bind(fn):
    r = _Rec()
    fn(r)
    assert r.call is not None
    return r.call


class Sched:
    COMPUTE = ("pe", "act", "dve", "pool")
    ALL = ("pe", "act", "dve", "pool", "sp")

    def __init__(self, nc, stack, n_dma_sems=24, epoch=30000):
        self.nc = nc
        self.stack = stack
        self.epoch = epoch
        self.prog = {e: [] for e in self.ALL}
        self.sems = []
        self.sem_eng = {}
        self.cur = {}
        self.cnt = {}
        for e in self.COMPUTE:
            self.cur[e] = self._new_sem("c_" + e)
            self.cnt[e] = 0
        self.dma_sems = [self._new_sem("d%d" % i) for i in range(n_dma_sems)]
        self.dma_cnt = [0] * n_dma_sems
        self.dma_rr = 0
        self.seen = {e: {} for e in self.ALL}
        self.last_w = {}
        self.readers = {}
        self.n_wait = 0
        self.n_op = 0

    def _new_sem(self, name):
        h = self.stack.enter_context(self.nc.semaphore(name + "_%d" % len(self.sems)))
        self.sems.append(h)
        if name.startswith("c_"):
            self.sem_eng[len(self.sems) - 1] = name[2:]
        return len(self.sems) - 1

    def _deps(self, rd, wr):
        deps = {}
        def add(tok):
            if tok is None:
                return
            s, v = tok
            if deps.get(s, 0) < v:
                deps[s] = v
        for k in rd:
            add(self.last_w.get(k))
        for k in wr:
            add(self.last_w.get(k))
            for s, v in self.readers.get(k, {}).items():
                add((s, v))
        return deps

    def _emit_waits(self, eng, deps):
        for s, v in deps.items():
            if eng == "pe" and s == self.cur["pe"]:
                continue
            if self.seen[eng].get(s, 0) >= v:
                continue
            self.seen[eng][s] = v
            self.prog[eng].append(("wait", s, v))
            self.n_wait += 1

    def _record(self, tok, rd, wr):
        s, v = tok
        for k in wr:
            self.last_w[k] = tok
            self.readers[k] = {}
        for k in rd:
            r = self.readers.setdefault(k, {})
            if r.get(s, 0) < v:
                r[s] = v

    def op(self, eng, fn, rd=(), wr=(), sig=True):
        deps = self._deps(rd, wr)
        if eng != "pe":
            for k in rd:
                if isinstance(k, str) and k.startswith("ps") and k[2:].isdigit():
                    for s_, v_ in self.readers.get(k, {}).items():
                        if self.sem_eng.get(s_) != eng and deps.get(s_, 0) < v_:
                            deps[s_] = v_
        self._emit_waits(eng, deps)
        if sig and self.cnt[eng] >= self.epoch:
            self.cur[eng] = self._new_sem("c_" + eng)
            self.cnt[eng] = 0
        tok = (self.cur[eng], self.cnt[eng] + 1)
        if sig:
            self.cnt[eng] += 1
            self.prog[eng].append(("op", _bind(fn), tok[0], 1))
        else:
            self.prog[eng].append(("op", _bind(fn), None, 0))
        self._record(tok, rd, wr)
        self.n_op += 1
        return tok

    def dma(self, eng, fn, rd=(), wr=()):
        i = self.dma_rr
        self.dma_rr = (self.dma_rr + 1) % len(self.dma_sems)
        s = self.dma_sems[i]
        deps = self._deps(rd, wr)
        if self.dma_cnt[i] > 0:
            v = 16 * self.dma_cnt[i]
            if deps.get(s, 0) < v:
                deps[s] = v
        self._emit_waits(eng, deps)
        self.dma_cnt[i] += 1
        tok = (s, 16 * self.dma_cnt[i])
        self.prog[eng].append(("op", _bind(fn), s, 16))
        self._record(tok, rd, wr)
        return tok

    def wait_all(self, eng, toks):
        deps = {}
        for s, v in toks:
            if deps.get(s, 0) < v:
                deps[s] = v
        self._emit_waits(eng, deps)

    def final_tokens(self):
        toks = []
        for i, s in enumerate(self.dma_sems):
            if self.dma_cnt[i]:
                toks.append((s, 16 * self.dma_cnt[i]))
        return toks

    def emit(self):
        nc = self.nc
        prog = self.prog
        sems = self.sems

        def replay(eng_obj, lst):
            for it in lst:
                if it[0] == "wait":
                    eng_obj.wait_ge(sems[it[1]], it[2])
                else:
                    name, a, k = it[1]
                    ins = getattr(eng_obj, name)(*a, **k)
                    if it[2] is not None:
                        ins.then_inc(sems[it[2]], it[3])

        with nc.Block() as block:
            @block.sync
            def _(e):
                replay(e, prog["sp"])

            @block.scalar
            def _(e):
                replay(e, prog["act"])

            @block.vector
            def _(e):
                replay(e, prog["dve"])

            @block.gpsimd
            def _(e):
                replay(e, prog["pool"])

            @block.tensor
            def _(e):
                replay(e, prog["pe"])


D = 1024
DEPTH = 2
NH = 4
HD = 128
W = 512
D_IN = 8192
D_FF = 2816
NFC = 2 * D_FF // 128
CONV_K = 31
ALPHA = (2.0 * DEPTH) ** 0.25
LN_EPS = 1e-5
PI = float(np.pi)
TWO_PI = 2.0 * float(np.pi)
CW1 = float(np.float32(6.28125))
CW2 = float(np.float32(TWO_PI - 6.28125))

O_BADA, O_LN1G, O_LN1B, O_LN2G, O_LN2B = 0, 48, 56, 64, 72
O_CW, O_CB, O_CLG, O_CLB, O_LBL, O_FCW, O_FCB, NPP = 80, 204, 208, 212, 216, 224, 356, 400
C_MASK, C_QD, C_KD, C_BD, C_INV, C_SGN, C_D0, NCONST = 0, 512, 1024, 1028, 1156, 1157, 1158, 1158 + 512


def bcast_last(ap, n):
    return bass.AP(ap.tensor, ap.offset, [list(d) for d in ap.ap] + [[0, n]])


def bcast_mid(ap, n):
    d = [list(x) for x in ap.ap]
    return bass.AP(ap.tensor, ap.offset, [d[0], [0, n]] + d[1:])


def v3(ap, a):
    return ap.rearrange("p (a b) -> p a b", a=a)


class _StopBuild(Exception):
    pass


POOL_ENG = "pool"


def build(S_LEN, T=512, n_layers=DEPTH, dbg=(), upto=None):
    nc = bass.Bass("TRN2", target_bir_lowering=False)
    NT = S_LEN // T
    NB = T // 128
    L = n_layers

    def din(name, shape, dt=F32):
        return nc.dram_tensor(name, list(shape), dt, kind="ExternalInput").ap()

    x_d = din("x", [S_LEN, D])
    pos_d = din("pos", [1, S_LEN], I32)
    cvec_d = din("cvec", [128, 8])
    wada_d = din("w_ada", [DEPTH, D, 6 * D])
    win_d = din("w_in", [DEPTH, D, D_IN])
    wrot_d = din("w_rot", [DEPTH, D, 1024])
    wbr_d = din("w_branch", [DEPTH, 3 * W, D])
    wout_d = din("w_out", [DEPTH, D, D])
    wup_d = din("w_up", [DEPTH, D, 2 * D_FF])
    wdn_d = din("w_down", [DEPTH, D_FF, D])
    pp_d = din("pp", [DEPTH, 128, NPP])
    bc_d = din("bc", [DEPTH, 128, 1024])
    const_d = din("consts", [128, NCONST])
    out_d = nc.dram_tensor("out", [S_LEN, D], F32, kind="ExternalOutput").ap()

    winb = nc.dram_tensor("winb", [DEPTH, D, D_IN], BF16).ap()
    wrotb = nc.dram_tensor("wrotb", [DEPTH, D, 1024], BF16).ap()
    wbrb = nc.dram_tensor("wbrb", [DEPTH, 3 * W, D], BF16).ap()
    woutb = nc.dram_tensor("woutb", [DEPTH, D, D], BF16).ap()
    wupb = nc.dram_tensor("wupb", [DEPTH, D, 2 * D_FF], BF16).ap()
    wdnb = nc.dram_tensor("wdnb", [DEPTH, D_FF, D], BF16).ap()

    cdiag = nc.dram_tensor("cdiag", [DEPTH, 4, 128, CONV_K * 128], BF16).ap()

    dbg_outs = {}

    with contextlib.ExitStack() as st:
        S = Sched(nc, st)

        def sb(name, shape, dt=F32):
            return nc.alloc_sbuf_tensor("sb_" + name, list(shape), dt)

        xT = sb("xT", [128, 8, T])
        hT = sb("hT", [128, 8, T], BF16)
        xin = [sb("xin%d" % i, [128, D]) for i in range(2)]
        NSLOT = 4
        SLOT_E = 4096
        wslot = [sb("ws%d" % i, [128, SLOT_E], BF16) for i in range(NSLOT)]
        cosT = sb("cosT", [128, T])
        sinT = sb("sinT", [128, T])
        coskT = sb("coskT", [128, T])
        sinkT = sb("sinkT", [128, T])
        NTMP = 8
        tmps = [sb("tmp%d" % i, [128, T]) for i in range(NTMP)]
        Bb = [sb("B%d" % i, [128, 4, T], BF16) for i in range(6)]
        qmc = sb("qmc", [128, 4, 4, 128], BF16)
        kmc = sb("kmc", [128, 4, 4, 128], BF16)
        Y = [sb("Y%d" % i, [128, T]) for i in range(8)]
        uT = [sb("u%dT" % i, [128, 4, T], BF16) for i in range(3)]
        G = sb("G", [128, 4, T + 30], BF16)
        STm = sb("STm", [128, 4, 128], BF16)
        STm2 = sb("STm2", [128, 4, 128], BF16)
        ubuf = sb("ubuf", [128, 512], BF16)
        st4 = [sb("st4_%d" % i, [128, 4]) for i in range(6)]
        S_ret = [sb("Sret%d" % l, [128, 4, 128]) for l in range(L)]
        Sbf_ret = [sb("Sbfret%d" % l, [128, 4, 128], BF16) for l in range(L)]
        R_h = [sb("Rh%d" % l, [128, 4, 128]) for l in range(L)]
        Lprev = [sb("Lprev%d" % l, [128, 4]) for l in range(L)]
        Lc = sb("Lc", [128, 4, T // 32])
        halo31 = [sb("halo31_%d" % l, [128, 4, 30], BF16) for l in range(L)]
        halo3 = [sb("halo3_%d" % l, [128, NFC, 2]) for l in range(L)]
        pbuf = [sb("pbuf%d" % i, [128, T + 2]) for i in range(3)]
        pp = sb("pp", [128, DEPTH, NPP])
        bc = sb("bc", [128, 1, 1024])
        cst = sb("cst", [128, NCONST])
        cvec = sb("cvec_sb", [128, 8])
        cond = sb("cond", [128, 8])
        mod = sb("mod", [128, DEPTH, 48])
        onep = sb("onep", [128, DEPTH, 2, 8])
        lb = sb("lb", [128, DEPTH, 4])
        oml = sb("oml", [128, DEPTH, 4])
        identf = sb("identf", [128, 128])
        identb = sb("identb", [128, 128], BF16)
        onesA = sb("onesA", [128, 128])
        onesB = sb("onesB", [128, 128])
        PS = [nc.alloc_psum_tensor("ps%d" % i, [128, 512], F32) for i in range(8)]
        PSB = [p.bitcast(BF16) for p in PS]
        print("sbuf bytes remaining", nc.sbuf_bytes_remaining)

        state = {"tmp": 0, "pm": 0, "slot": 0, "pb": 0, "pm_excl": ()}

        def tmp():
            i = state["tmp"]
            state["tmp"] = (i + 1) % NTMP
            return tmps[i], "tmp%d" % i

        def pm():
            while True:
                i = state["pm"]
                state["pm"] = (i + 1) % 8
                if i not in state["pm_excl"]:
                    return i

        P_ST, P_DS, P_O, P_TP = 4, 5, 6, 7

        def psk(i):
            return "ps%d" % i

        def ACT(fn, rd, wr, sig=True):
            return S.op("act", fn, rd, wr, sig)

        def DVE(fn, rd, wr, sig=True):
            return S.op("dve", fn, rd, wr, sig)

        def POOL(fn, rd, wr, sig=True):
            return S.op(POOL_ENG, fn, rd, wr, sig)

        def PE(fn, rd, wr, sig=True):
            return S.op("pe", fn, rd, wr, sig)

        def mm(out_ap, out_key, terms):
            n = len(terms)
            for i, (l_, r_, keys) in enumerate(terms):
                PE(lambda e, l_=l_, r_=r_, i=i: e.matmul(out_ap, l_, r_, start=(i == 0), stop=(i == n - 1)),
                   rd=keys, wr=[out_key], sig=(i == n - 1))

        def load_w(src3, K, C, rdkeys, dt=BF16):
            i = state["slot"]
            state["slot"] = (i + 1) % NSLOT
            if dt == BF16:
                view = wslot[i][:, 0:K * C].rearrange("p (k c) -> p k c", k=K)
            else:
                view = wslot[i].bitcast(F32)[:, 0:K * C].rearrange("p (k c) -> p k c", k=K)
            key = "ws%d" % i
            S.dma("sp", lambda e: e.dma_start(out=view, in_=src3), rd=rdkeys, wr=[key])
            return view, key

        def dump(name, ap, shape, dt=F32, keys=()):
            if name not in dbg:
                return
            d = nc.dram_tensor("dbg_" + name, list(shape), dt, kind="ExternalOutput").ap()
            dbg_outs[name] = d
            S.dma("pool", lambda e: e.dma_start(out=d, in_=ap), rd=list(keys), wr=["dbg_" + name])

        def cast_w(dst, src, rows, name):
            ncol = dst.shape[-1]
            for l in range(L):
                for r0 in range(0, rows, 128):
                    for c0 in range(0, ncol, 512):
                        S.dma("pool", lambda e, l=l, r0=r0, c0=c0: e.dma_start(out=dst[l, r0:r0 + 128, c0:c0 + 512], in_=src[l, r0:r0 + 128, c0:c0 + 512]),
                              rd=[], wr=[(name, l, r0 // 128, c0 // 512)])

        def wkeys(name, l, rows, c0=0, c1=None):
            if c1 is None:
                c1 = c0 + 512
            return [(name, l, r, cc) for r in range(rows // 128) for cc in range(c0 // 512, (c1 + 511) // 512)]

        cast_w(winb, win_d, D, "winb")
        cast_w(wrotb, wrot_d, D, "wrotb")
        cast_w(wbrb, wbr_d, 3 * W, "wbrb")
        cast_w(woutb, wout_d, D, "woutb")
        cast_w(wupb, wup_d, D, "wupb")
        cast_w(wdnb, wdn_d, D_FF, "wdnb")

        S.dma("sp", lambda e: e.dma_start(out=cst[:], in_=const_d), wr=["cst"])
        S.dma("sp", lambda e: e.dma_start(out=cvec[:], in_=cvec_d), wr=["cvec"])
        for l in range(DEPTH):
            S.dma("sp", lambda e, l=l: e.dma_start(out=pp[:, l, :], in_=pp_d[l]), wr=["pp"], rd=["pp"])
        POOL(lambda e: e.memset(identf[:], 1.0), [], ["identf"])
        S.op("pool", lambda e: e.affine_select(out=identf[:], in_=identf[:], pattern=[[-1, 128]], compare_op=ALU.is_equal,
                                               fill=0.0, base=0, channel_multiplier=1), ["identf"], ["identf"])
        DVE(lambda e: e.tensor_copy(identb[:], identf[:]), ["identf"], ["identb"])
        POOL(lambda e: e.memset(onesA[:], 1.0 / D), [], ["onesA"])
        POOL(lambda e: e.memset(onesB[:], 1.0 / W), [], ["onesB"])
        POOL(lambda e: e.memset(qmc[:], 0.0), [], ["qmc"])
        POOL(lambda e: e.memset(kmc[:], 0.0), [], ["kmc"])
        for l in range(L):
            POOL(lambda e, l=l: e.memset(S_ret[l][:], 0.0), [], ["Sret%d" % l])
            POOL(lambda e, l=l: e.memset(Sbf_ret[l][:], 0.0), [], ["Sbfret%d" % l])
            POOL(lambda e, l=l: e.memset(R_h[l][:], 0.0), [], ["Rh%d" % l])
            POOL(lambda e, l=l: e.memset(Lprev[l][:], 1.0), [], ["Lprev%d" % l])
            POOL(lambda e, l=l: e.memset(halo31[l][:], 0.0), [], ["halo31_%d" % l])
            POOL(lambda e, l=l: e.memset(halo3[l][:], 0.0), [], ["halo3_%d" % l])

        ACT(lambda e: e.activation(cond[:], cvec[:], AF.Silu), ["cvec"], ["cond"])
        condr = sb("condr", [128, 8, 8])
        DVE(lambda e: e.tensor_copy(condr[:], bcast_last(cond[:], 8)), ["cond"], ["condr"])
        for l in range(L):
            pmi = pm()
            for g in range(24):
                src = wada_d[l].rearrange("(k p) c -> p k c", p=128)[:, :, g * 256:(g + 1) * 256]
                wv, wk = load_w(src, 8, 256, [], dt=F32)
                for j in range(2):
                    m = g * 2 + j
                    mm(PS[pmi][:, m * 8:(m + 1) * 8], psk(pmi),
                       [(wv[:, k, j * 128:(j + 1) * 128], condr[:, k, :], [wk, "condr"]) for k in range(8)])
            mt_, mtk = tmp()
            ACT(lambda e, pmi=pmi, mt_=mt_: e.copy(mt_[:, 0:384], PS[pmi][:, 0:384]), [psk(pmi)], [mtk])
            DVE(lambda e, l=l, mt_=mt_: e.tensor_tensor(mod[:, l, :], v3(mt_[:, 0:384], 48)[:, :, 0], pp[:, l, O_BADA:O_BADA + 48], ALU.add),
                [mtk, "pp"], ["mod"])
            DVE(lambda e, l=l: e.tensor_scalar(onep[:, l, 0, :], mod[:, l, 8:16], 1.0, None, ALU.add), ["mod"], ["onep"])
            DVE(lambda e, l=l: e.tensor_scalar(onep[:, l, 1, :], mod[:, l, 32:40], 1.0, None, ALU.add), ["mod", "onep"], ["onep"])
        DVE(lambda e: e.memset(lb[:], 0.0), [], ["lb"])
        if L > 1:
            DVE(lambda e: e.tensor_tensor(lb[:, 1, :], pp[:, 0, O_LBL + 4:O_LBL + 8], pp[:, 0, O_LBL:O_LBL + 4], ALU.subtract),
                ["pp", "lb"], ["lb"])
            ACT(lambda e: e.activation(lb[:, 1, :], lb[:, 1, :], AF.Sigmoid), ["lb"], ["lb"])
        DVE(lambda e: e.tensor_scalar(oml[:], lb[:], -1.0, 1.0, ALU.mult, ALU.add), ["lb"], ["oml"])
        for l in range(L):
            for j in range(4):
                i = state["slot"]
                state["slot"] = (i + 1) % NSLOT
                dv = wslot[i][:, 0:CONV_K * 128].rearrange("p (k c) -> p k c", k=CONV_K)
                DVE(lambda e, dv=dv, l=l, j=j: e.tensor_tensor(dv, bcast_mid(identf[:], CONV_K),
                                                               bcast_last(pp[:, l, O_CW + j * 31:O_CW + j * 31 + 31], 128), ALU.mult),
                    ["identf", "pp"], ["ws%d" % i])
                S.dma("pool", lambda e, i=i, l=l, j=j: e.dma_start(out=cdiag[l, j], in_=wslot[i][:, 0:CONV_K * 128]), rd=["ws%d" % i], wr=[("cdiag", l, j)])
        dump("mod", mod[:], [128, DEPTH, 48], keys=["mod"])
        dump("lb", lb[:], [128, DEPTH, 4], keys=["lb"])

        maskT = v3(cst[:, C_MASK:C_MASK + 512], 4)
        qdv = v3(cst[:, C_QD:C_QD + 512], 4)
        kdv = cst[:, C_KD:C_KD + 4]
        bdm = cst[:, C_BD:C_BD + 128]
        invf = cst[:, C_INV:C_INV + 1]
        sgn = cst[:, C_SGN:C_SGN + 1]
        d0m = cst[:, C_D0:C_D0 + 512]
        GAM = [1.0 - 2.0 ** (-5 - h) for h in range(NH)]
        CD = [g ** 128 for g in GAM]
        KSCALE = float(HD ** -0.5)

        def ln_stats_chunk(f):
            PE(lambda e: e.matmul(PS[P_ST][:, 0:T], onesA[:], xT[:, f, :], start=(f == 0), stop=(f == 7)),
               rd=["onesA", ("xT", f)], wr=[psk(P_ST)], sig=(f == 7))
            t_, tk = tmp()
            ACT(lambda e: e.activation(t_[:], xT[:, f, :], AF.Square), [("xT", f)], [tk])
            PE(lambda e: e.matmul(PS[P_DS][:, 0:T], onesA[:], t_[:], start=(f == 0), stop=(f == 7)),
               rd=["onesA", tk], wr=[psk(P_DS)], sig=(f == 7))

        def layer_norm_stream(l, gcol, bcol):
            pmean, pmsq = P_ST, P_DS
            mean_, mk = lnm, "lnm"
            rstd_, rk = lnr, "lnr"
            ACT(lambda e: e.copy(mean_[:], PS[pmean][:, 0:T]), [psk(pmean)], [mk])
            DVE(lambda e: e.tensor_tensor(rstd_[:], mean_[:], mean_[:], ALU.mult), [mk], [rk])
            DVE(lambda e: e.tensor_tensor(rstd_[:], PS[pmsq][:, 0:T], rstd_[:], ALU.subtract), [psk(pmsq), rk], [rk])
            ACT(lambda e: e.activation(rstd_[:], rstd_[:], AF.Sqrt, bias=eps_t[:, 0:1]), [rk, "eps"], [rk])
            DVE(lambda e: e.reciprocal(rstd_[:], rstd_[:]), [rk], [rk])
            for f in range(8):
                t_, tk = tmp()
                POOL(lambda e, f=f, t_=t_: e.tensor_tensor(t_[:], xT[:, f, :], mean_[:], ALU.subtract), [("xT", f), mk], [tk])
                DVE(lambda e, t_=t_: e.tensor_tensor(t_[:], t_[:], rstd_[:], ALU.mult), [tk, rk], [tk])
                ACT(lambda e, f=f, t_=t_: e.activation(xT[:, f, :], t_[:], AF.Identity,
                                                       bias=pp[:, l, bcol + f:bcol + f + 1], scale=pp[:, l, gcol + f:gcol + f + 1]),
                    [tk, "pp"], [("xT", f)])

        eps_t = sb("eps_t", [128, 1])
        DVE(lambda e: e.memset(eps_t[:], LN_EPS), [], ["eps"])

        def modulate(l, which):
            shc = 0 if which == 0 else 24
            for f in range(8):
                ACT(lambda e, f=f: e.activation(hT[:, f, :], xT[:, f, :], AF.Identity,
                                                bias=mod[:, l, shc + f:shc + f + 1], scale=onep[:, l, which, f:f + 1]),
                    [("xT", f), "mod", "onep"], [("hT", f)])

        HT_KEYS = [("hT", f) for f in range(8)]

        def residual(l, f, pbank, gcol):
            t_, tk = tmp()
            ACT(lambda e, t_=t_: e.activation(t_[:], PS[pbank][:, 0:T], AF.Copy, scale=mod[:, l, gcol + f:gcol + f + 1]),
                [psk(pbank), "mod"], [tk])
            DVE(lambda e, t_=t_: e.scalar_tensor_tensor(xT[:, f, :], xT[:, f, :], ALPHA, t_[:], ALU.mult, ALU.add),
                [("xT", f), tk], [("xT", f)])

        def proj_fm(wv, wk, j, pbank):
            mm(PS[pbank][:, 0:T], psk(pbank),
               [(wv[:, k, j * 128:(j + 1) * 128], hT[:, k, :], [wk, ("hT", k)]) for k in range(8)])

        def proj_tm(wv, wk, c, pbank):
            mm(PS[pbank][:, 0:512], psk(pbank),
               [(hT[:, k, c * 128:(c + 1) * 128], wv[:, k, :], [wk, ("hT", k)]) for k in range(8)])

        def win_group(l, g):
            src = winb[l].rearrange("(k p) c -> p k c", p=128)[:, :, g * 512:(g + 1) * 512]
            return load_w(src, 8, 512, wkeys("winb", l, D, g * 512))

        def wrot_group(l, g):
            src = wrotb[l].rearrange("(k p) c -> p k c", p=128)[:, :, g * 512:(g + 1) * 512]
            return load_w(src, 8, 512, wkeys("wrotb", l, D, g * 512))

        def rope_tables(ti):
            pt_, pk_ = tmp()
            posi = pt_.bitcast(I32)
            S.dma("sp", lambda e: e.dma_start(out=posi[:], in_=pos_d[:, ti * T:(ti + 1) * T].partition_broadcast(128)),
                  rd=[], wr=[pk_])
            ang, ak = tmp()
            kf, kk = tmp()
            r_, rk = tmp()
            m_, mk = tmp()
            ki = kf.bitcast(I32)
            DVE(lambda e: e.tensor_copy(ang[:], posi[:]), [pk_], [ak])
            DVE(lambda e: e.tensor_scalar(ang[:], ang[:], invf, None, ALU.mult), [ak, "cst"], [ak])
            DVE(lambda e: e.tensor_scalar(ki[:], ang[:], 1.0 / TWO_PI, None, ALU.mult), [ak], [kk])
            DVE(lambda e: e.tensor_copy(r_[:], ki[:]), [kk], [rk])
            DVE(lambda e: e.scalar_tensor_tensor(ang[:], r_[:], -CW1, ang[:], ALU.mult, ALU.add), [rk, ak], [ak])
            DVE(lambda e: e.scalar_tensor_tensor(ang[:], r_[:], -CW2, ang[:], ALU.mult, ALU.add), [rk, ak], [ak])

            def wrap(dst, dk, shift):
                DVE(lambda e: e.tensor_scalar(dst[:], ang[:], shift, None, ALU.add), [ak], [dk])
                DVE(lambda e: e.tensor_scalar(m_[:], dst[:], PI, -TWO_PI, ALU.is_gt, ALU.mult), [dk], [mk])
                DVE(lambda e: e.tensor_tensor(dst[:], dst[:], m_[:], ALU.add), [dk, mk], [dk])
                DVE(lambda e: e.tensor_scalar(m_[:], dst[:], -PI, TWO_PI, ALU.is_lt, ALU.mult), [dk], [mk])
                DVE(lambda e: e.tensor_tensor(dst[:], dst[:], m_[:], ALU.add), [dk, mk], [dk])

            wrap(kf, kk, 0.0)
            ACT(lambda e: e.activation(sinT[:], kf[:], AF.Sin, scale=sgn), [kk, "cst"], ["sinT"])
            wrap(kf, kk, PI / 2)
            ACT(lambda e: e.activation(cosT[:], kf[:], AF.Sin), [kk], ["cosT"])
            DVE(lambda e: e.tensor_scalar(coskT[:], cosT[:], KSCALE, None, ALU.mult), ["cosT"], ["coskT"])
            DVE(lambda e: e.tensor_scalar(sinkT[:], sinT[:], KSCALE, None, ALU.mult), ["sinT"], ["sinkT"])

        def retention(l, ti):
            qT, kT, qdT, vv, vkd, ktok = Bb
            for (g, grot, dst, dkey, ct, ck, st_, sk) in ((0, 0, qT, "B0", cosT, "cosT", sinT, "sinT"),
                                                          (1, 1, kT, "B1", coskT, "coskT", sinkT, "sinkT")):
                wv, wk = win_group(l, g)
                wr_, wrk = wrot_group(l, grot)
                for j in range(4):
                    pa, pb = pm(), pm()
                    proj_fm(wv, wk, j, pa)
                    proj_fm(wr_, wrk, j, pb)
                    t1, t1k = tmp()
                    t2, t2k = tmp()
                    DVE(lambda e, t1=t1, pa=pa, ct=ct: e.tensor_tensor(t1[:], PS[pa][:, 0:T], ct[:], ALU.mult), [psk(pa), ck], [t1k])
                    DVE(lambda e, t2=t2, pb=pb, st_=st_: e.tensor_tensor(t2[:], PS[pb][:, 0:T], st_[:], ALU.mult), [psk(pb), sk], [t2k])
                    POOL(lambda e, t1=t1, t2=t2, dst=dst, j=j: e.tensor_tensor(dst[:, j, :], t1[:], t2[:], ALU.add), [t1k, t2k], [(dkey, j)])
                    if g == 0:
                        POOL(lambda e, j=j: e.tensor_tensor(v3(qdT[:, j, :], NB), v3(qT[:, j, :], NB), bcast_mid(qdv[:, j, :], NB), ALU.mult),
                             [("B0", j), "cst"], [("B2", j)])
                    pump(4)
            wv, wk = win_group(l, 2)
            for c in range(NB):
                pa = pm()
                proj_tm(wv, wk, c, pa)
                ACT(lambda e, c=c, pa=pa: e.copy(vv[:, c, :], PS[pa][:, 0:512]), [psk(pa)], [("B3", c)])
                DVE(lambda e, c=c, pa=pa: e.tensor_tensor(v3(vkd[:, c, :], 4), v3(PS[pa][:, 0:512], 4), bcast_last(kdv, 128), ALU.mult),
                    [psk(pa), "cst"], [("B4", c)])
            wv, wk = win_group(l, 3)
            for c in range(NB):
                pa = pm()
                proj_tm(wv, wk, c, pa)
                ACT(lambda e, c=c, pa=pa: e.activation(Y[4 + c][:], PS[pa][:, 0:512], AF.Silu), [psk(pa)], ["Y%d" % (4 + c)])
                POOL(lambda e, c=c: e.tensor_tensor(Y[4 + c][:], Y[4 + c][:], bc[:, 0, 0:512], ALU.mult), ["Y%d" % (4 + c), "bc"], ["Y%d" % (4 + c)])
            if l == 0 and ti == 0:
                dump("q", qT[:], [128, 4, T], BF16, keys=[("B0", j) for j in range(4)])
                dump("k", kT[:], [128, 4, T], BF16, keys=[("B1", j) for j in range(4)])
                dump("qd", qdT[:], [128, 4, T], BF16, keys=[("B2", j) for j in range(4)])
                dump("v", vv[:], [128, 4, 512], BF16, keys=[("B3", j) for j in range(4)])
                dump("vkd", vkd[:], [128, 4, 512], BF16, keys=[("B4", j) for j in range(4)])
                dump("ga", Y[4][:], [128, 512], F32, keys=["Y4"])
            stmb = [STm, STm2]
            stmk = ["STm", "STm2"]

            def prep(c):
                cb = slice(c * 128, (c + 1) * 128)
                for h in range(4):
                    PE(lambda e, h=h: e.transpose(PSB[P_TP][:, h * 128:(h + 1) * 128], kT[:, h, cb], identb[:]),
                       [("B1", h), "identb"], [psk(P_TP)], sig=(h == 3))
                ACT(lambda e: e.copy(ktok[:, c, :], PSB[P_TP][:, 0:512]), [psk(P_TP)], [("B5", c)])
                for h in range(4):
                    PE(lambda e, h=h: e.matmul(PS[P_ST][:, h * 128:(h + 1) * 128], kT[:, h, cb], qT[:, h, cb], start=True, stop=True),
                       [("B1", h), ("B0", h)], [psk(P_ST)], sig=(h == 3))
                DVE(lambda e: e.tensor_tensor(stmb[c % 2][:], v3(PS[P_ST][:, 0:512], 4), maskT, ALU.mult), [psk(P_ST), "cst"], [stmk[c % 2]])
                for h in range(4):
                    hs = slice(h * 128, (h + 1) * 128)
                    PE(lambda e, h=h, hs=hs: e.matmul(PS[P_DS][:, hs], ktok[:, c, hs], vkd[:, c, hs], start=True, stop=True),
                       [("B5", c), ("B4", c)], [psk(P_DS)], sig=(h == 3))

            def out_update(c):
                cb = slice(c * 128, (c + 1) * 128)
                for h in range(4):
                    hs = slice(h * 128, (h + 1) * 128)
                    PE(lambda e, h=h, hs=hs: e.matmul(PS[P_O][:, hs], stmb[c % 2][:, h, :], vv[:, c, hs], start=True, stop=False),
                       [stmk[c % 2], ("B3", c)], [psk(P_O)], sig=False)
                    PE(lambda e, h=h, hs=hs: e.matmul(PS[P_O][:, hs], qdT[:, h, cb], Sbf_ret[l][:, h, :], start=False, stop=True),
                       [("B2", h), "Sbfret%d" % l], [psk(P_O)], sig=(h == 3))
                for h in range(4):
                    hs = slice(h * 128, (h + 1) * 128)
                    DVE(lambda e, h=h, hs=hs: e.scalar_tensor_tensor(S_ret[l][:, h, :], S_ret[l][:, h, :], CD[h], PS[P_DS][:, hs], ALU.mult, ALU.add),
                        ["Sret%d" % l, psk(P_DS)], ["Sret%d" % l])
                ACT(lambda e: e.copy(Sbf_ret[l][:], S_ret[l][:]), ["Sret%d" % l], ["Sbfret%d" % l])

            def norm(c):
                cb = slice(c * 128, (c + 1) * 128)
                s1, s2, mean, var, rstd, nmr = st4
                sq, sqk = tmp()
                DVE(lambda e: e.tensor_reduce(s1[:], v3(PS[P_O][:, 0:512], 4), AX.X, ALU.add), [psk(P_O)], ["s1"])
                ACT(lambda e: e.activation(sq[:, 0:512], PS[P_O][:, 0:512], AF.Square), [psk(P_O)], [sqk])
                DVE(lambda e: e.tensor_reduce(s2[:], v3(sq[:, 0:512], 4), AX.X, ALU.add), [sqk], ["s2"])
                DVE(lambda e: e.tensor_scalar(mean[:], s1[:], 1.0 / HD, None, ALU.mult), ["s1"], ["mean"])
                DVE(lambda e: e.tensor_tensor(var[:], mean[:], mean[:], ALU.mult), ["mean"], ["var"])
                DVE(lambda e: e.scalar_tensor_tensor(var[:], s2[:], 1.0 / HD, var[:], ALU.mult, ALU.subtract), ["s2", "var"], ["var"])
                ACT(lambda e: e.activation(rstd[:], var[:], AF.Sqrt, bias=eps_t[:, 0:1]), ["var", "eps"], ["rstd"])
                DVE(lambda e: e.reciprocal(rstd[:], rstd[:]), ["rstd"], ["rstd"])
                DVE(lambda e: e.scalar_tensor_tensor(nmr[:], mean[:], -1.0, rstd[:], ALU.mult, ALU.mult), ["mean", "rstd"], ["nmr"])
                onb, onk = tmp()
                for h in range(4):
                    hs = slice(h * 128, (h + 1) * 128)
                    ACT(lambda e, h=h, hs=hs: e.activation(onb[:, hs], PS[P_O][:, hs], AF.Identity, bias=nmr[:, h:h + 1], scale=rstd[:, h:h + 1]),
                        [psk(P_O), "nmr", "rstd"], [onk])
                POOL(lambda e: e.tensor_tensor(ubuf[:], onb[:], Y[4 + c][:], ALU.mult), [onk, "Y%d" % (4 + c)], ["ubuf"])
                for h in range(4):
                    hs = slice(h * 128, (h + 1) * 128)
                    PE(lambda e, h=h, hs=hs: e.transpose(PSB[P_TP][:, hs], ubuf[:, hs], identb[:]), ["ubuf", "identb"], [psk(P_TP)], sig=(h == 3))
                ACT(lambda e: e.copy(uT[0][:, :, cb], v3(PSB[P_TP][:, 0:512], 4)), [psk(P_TP)], [("u0T", c)])

            prep(0)
            conv_chunk(l, 0)
            conv_evac(l, 0)
            for c in range(NB):
                out_update(c)
                if c + 1 < NB:
                    prep(c + 1)
                    conv_chunk(l, c + 1)
                norm(c)
                if c + 1 < NB:
                    conv_evac(l, c + 1)

        def hgrn(l, ti):
            qpT, kpT, vh, _kt0, Sb4f, _kt1 = Bb
            Sb4 = Sb4f
            NSC = T // 32
            wq, wqk = win_group(l, 4)
            wf, wfk = win_group(l, 5)
            for j in range(4):
                pq, pf = pm(), pm()
                proj_fm(wq, wqk, j, pq)
                proj_fm(wf, wfk, j, pf)
                ACT(lambda e, j=j, pq=pq: e.activation(Y[j][:], PS[pq][:, 0:T], AF.Silu), [psk(pq)], ["Y%d" % j])
                sig, sgk = tmp()
                sng, snk = tmp()
                cum, cuk = tmp()
                ACT(lambda e, sig=sig, pf=pf: e.activation(sig[:], PS[pf][:, 0:T], AF.Sigmoid), [psk(pf)], [sgk])
                ACT(lambda e, sng=sng, pf=pf: e.activation(sng[:], PS[pf][:, 0:T], AF.Sigmoid, scale=-1.0), [psk(pf)], [snk])
                ACT(lambda e, sig=sig, j=j: e.activation(sig[:], sig[:], AF.Ln, bias=lb[:, l, j:j + 1], scale=oml[:, l, j:j + 1]),
                    [sgk, "lb", "oml"], [sgk])
                DVE(lambda e, sig=sig, cum=cum: e.tensor_tensor_scan(cum[:], d0m[:, 0:T], sig[:], 0.0, ALU.mult, ALU.add), [sgk, "cst"], [cuk])
                ACT(lambda e, sig=sig, cum=cum: e.activation(sig[:], cum[:], AF.Exp), [cuk], [sgk])
                ACT(lambda e, cum=cum: e.activation(cum[:], cum[:], AF.Exp, scale=-1.0), [cuk], [cuk])
                POOL(lambda e, sig=sig, j=j: e.tensor_copy(Lc[:, j, :], sig[:, 31:T:32]), [sgk], [("Lc", j)])
                DVE(lambda e, sig=sig, j=j: e.tensor_tensor(qpT[:, j, :], Y[j][:], sig[:], ALU.mult), ["Y%d" % j, sgk], [("B0", j)])
                DVE(lambda e, sng=sng, cum=cum, j=j: e.scalar_tensor_tensor(kpT[:, j, :], sng[:], oml[:, l, j:j + 1], cum[:], ALU.mult, ALU.mult),
                    [snk, cuk, "oml"], [("B1", j)])
                pump(4)
            wv, wk = win_group(l, 6)
            for c in range(NB):
                pa = pm()
                proj_tm(wv, wk, c, pa)
                ACT(lambda e, c=c, pa=pa: e.copy(vh[:, c, :], PS[pa][:, 0:512]), [psk(pa)], [("B2", c)])
            wv, wk = win_group(l, 7)
            for c in range(NB):
                pa = pm()
                proj_tm(wv, wk, c, pa)
                ACT(lambda e, c=c, pa=pa: e.activation(Y[4 + c][:], PS[pa][:, 0:512], AF.Silu), [psk(pa)], ["Y%d" % (4 + c)])
                POOL(lambda e, c=c: e.tensor_tensor(Y[4 + c][:], Y[4 + c][:], bc[:, 0, 512:1024], ALU.mult), ["Y%d" % (4 + c), "bc"], ["Y%d" % (4 + c)])
            LC_KEYS = [("Lc", j) for j in range(4)]
            ktbuf = [Bb[3], Bb[5]]
            ktkey = ["B3", "B5"]
            stmb = [STm, STm2]
            stmk = ["STm", "STm2"]
            DSB = [0, 1, 2, 3]
            qdst = bass.AP(qmc, 0, [list(qmc[:].ap[0]), [512, 4], [160, 4], [1, 32]])
            kdst = bass.AP(kmc, 0, [list(kmc[:].ap[0]), [512, 4], [160, 4], [1, 32]])

            def prep(c):
                cb = slice(c * 128, (c + 1) * 128)
                kt = ktbuf[c % 2]
                POOL(lambda e: e.tensor_copy(kdst, kpT[:, :, cb].rearrange("p h (i t) -> p h i t", i=4)),
                     [("B1", j) for j in range(4)] + ["kmc"], ["kmc"])
                for h in range(4):
                    PE(lambda e, h=h: e.matmul(PS[P_ST][:, h * 128:(h + 1) * 128], kpT[:, h, cb], qpT[:, h, cb], start=True, stop=True),
                       [("B1", h), ("B0", h)], [psk(P_ST)], sig=(h == 3))
                DVE(lambda e: e.tensor_tensor(stmb[c % 2][:], v3(PS[P_ST][:, 0:512], 4), bcast_mid(bdm, 4), ALU.mult), [psk(P_ST), "cst"], [stmk[c % 2]])
                for half, bank in ((0, P_TP), (1, P_DS)):
                    for hh in range(2):
                        h = half * 2 + hh
                        for I in range(4):
                            col = (hh * 4 + I) * 128
                            PE(lambda e, h=h, I=I, col=col, bank=bank: e.transpose(PSB[bank][:, col:col + 128], kmc[:, h, I, :], identb[:]),
                               ["kmc", "identb"], [psk(bank)], sig=(hh == 1 and I == 3))
                    ACT(lambda e, half=half, bank=bank: e.copy(kt[:, half * 2:half * 2 + 2, :], v3(PSB[bank][:, 0:1024], 2)),
                        [psk(bank)], [(ktkey[c % 2], half)])

            def deltas(c):
                kt = ktbuf[c % 2]
                for h in range(4):
                    hs = slice(h * 128, (h + 1) * 128)
                    for I in range(4):
                        PE(lambda e, h=h, I=I, hs=hs: e.matmul(PS[DSB[h]][:, I * 128:(I + 1) * 128], kt[:, h, I * 128:(I + 1) * 128], vh[:, c, hs],
                                                               start=True, stop=True),
                           [(ktkey[c % 2], h // 2), ("B2", c)], [psk(DSB[h])], sig=(I == 3))

            def chain(c):
                for I in range(4):
                    n = c * 4 + I
                    for h in range(4):
                        if n == 0:
                            lsc = Lprev[l][:, h:h + 1]
                            lkeys = ["Lprev%d" % l]
                        else:
                            lsc = Lc[:, h, n - 1:n]
                            lkeys = [("Lc", h)]
                        ACT(lambda e, h=h, I=I, lsc=lsc: e.activation(Sb4[:, h, I * 128:(I + 1) * 128], R_h[l][:, h, :], AF.Copy, scale=lsc),
                            [("Rh%d" % l, h)] + lkeys, [("B4", h)])
                        DVE(lambda e, h=h, I=I, lsc=lsc: e.scalar_tensor_tensor(R_h[l][:, h, :], R_h[l][:, h, :], lsc, PS[DSB[h]][:, I * 128:(I + 1) * 128],
                                                                               ALU.mult, ALU.add),
                            [("Rh%d" % l, h), psk(DSB[h])] + lkeys, [("Rh%d" % l, h)])

            def out_mm(c):
                for h in range(4):
                    hs = slice(h * 128, (h + 1) * 128)
                    PE(lambda e, h=h, hs=hs: e.matmul(PS[P_O][:, hs], stmb[c % 2][:, h, :], vh[:, c, hs], start=True, stop=False),
                       [stmk[c % 2], ("B2", c)], [psk(P_O)], sig=False)
                    for I in range(4):
                        PE(lambda e, h=h, I=I, hs=hs: e.matmul(PS[P_O][:, hs], qmc[:, h, I, :], Sb4[:, h, I * 128:(I + 1) * 128], start=False, stop=(I == 3)),
                           ["qmc", ("B4", h)], [psk(P_O)], sig=(I == 3 and h == 3))

            def out_norm(c):
                cb = slice(c * 128, (c + 1) * 128)
                s1, s2, mean, var, rstd, nmr = st4
                sq, sqk = tmp()
                ACT(lambda e, sq=sq: e.activation(sq[:, 0:512], PS[P_O][:, 0:512], AF.Square), [psk(P_O)], [sqk])
                DVE(lambda e, sq=sq: e.tensor_reduce(s2[:], v3(sq[:, 0:512], 4), AX.X, ALU.add), [sqk], ["s2"])
                ACT(lambda e: e.activation(rstd[:], s2[:], AF.Sqrt, bias=eps_t[:, 0:1], scale=1.0 / HD), ["s2", "eps"], ["rstd"])
                DVE(lambda e: e.reciprocal(rstd[:], rstd[:]), ["rstd"], ["rstd"])
                for h in range(4):
                    hs = slice(h * 128, (h + 1) * 128)
                    DVE(lambda e, h=h, hs=hs: e.scalar_tensor_tensor(ubuf[:, hs], PS[P_O][:, hs], rstd[:, h:h + 1], Y[4 + c][:, hs], ALU.mult, ALU.mult),
                        [psk(P_O), "rstd", "Y%d" % (4 + c)], ["ubuf"])
                for h in range(4):
                    hs = slice(h * 128, (h + 1) * 128)
                    PE(lambda e, h=h, hs=hs: e.transpose(PSB[P_TP][:, hs], ubuf[:, hs], identb[:]), ["ubuf", "identb"], [psk(P_TP)], sig=(h == 3))
                ACT(lambda e: e.copy(uT[1][:, :, cb], v3(PSB[P_TP][:, 0:512], 4)), [psk(P_TP)], [("u1T", c)])

            def qmask(c):
                cb = slice(c * 128, (c + 1) * 128)
                POOL(lambda e: e.tensor_copy(qdst, qpT[:, :, cb].rearrange("p h (i t) -> p h i t", i=4)),
                     [("B0", j) for j in range(4)] + ["qmc"], ["qmc"])

            prep(0)
            qmask(0)
            deltas(0)
            for c in range(NB):
                chain(c)
                if c + 1 < NB:
                    prep(c + 1)
                out_mm(c)
                if c + 1 < NB:
                    qmask(c + 1)
                    deltas(c + 1)
                out_norm(c)
            POOL(lambda e: e.tensor_copy(Lprev[l][:], Lc[:, :, NSC - 1]), LC_KEYS, ["Lprev%d" % l])

        CA = [sb("cacc%d" % i, [128, T + 2]) for i in range(4)]

        def conv_part1(l, ti):
            wa, wak = win_group(l, 8)
            wb_, wbk = win_group(l, 9)
            POOL(lambda e: e.tensor_copy(G[:, :, 0:30], halo31[l][:]), ["halo31_%d" % l], [("G", j) for j in range(4)])
            for j in range(4):
                pa, pb = pm(), pm()
                proj_fm(wa, wak, j, pa)
                proj_fm(wb_, wbk, j, pb)
                sg, sgk = tmp()
                ACT(lambda e, sg=sg, pb=pb: e.activation(sg[:], PS[pb][:, 0:T], AF.Sigmoid), [psk(pb)], [sgk])
                DVE(lambda e, sg=sg, pa=pa, j=j: e.tensor_tensor(G[:, j, 30:30 + T], PS[pa][:, 0:T], sg[:], ALU.mult), [psk(pa), sgk], [("G", j)])

        def conv_chunk(l, j):
            dg, dgk = load_w(cdiag[l, j].rearrange("p (k c) -> p k c", k=CONV_K), CONV_K, 128, [("cdiag", l, j)])
            pb_ = j
            mm(PS[pb_][:, 0:T], psk(pb_), [(dg[:, k, :], G[:, j, k:k + T], [dgk, ("G", j)]) for k in range(CONV_K)])

        def conv_evac(l, j):
            pb_ = j
            ACT(lambda e: e.activation(CA[j][:, 0:T], PS[pb_][:, 0:T], AF.Identity, bias=pp[:, l, O_CB + j:O_CB + j + 1]),
                [psk(pb_), "pp"], ["cacc%d" % j])
            if j == 3:
                POOL(lambda e: e.tensor_copy(halo31[l][:], G[:, :, T:T + 30]), [("G", jj) for jj in range(4)], ["halo31_%d" % l])

        def conv_part3(l, ti):
            for j in range(4):
                PE(lambda e, j=j: e.matmul(PS[P_ST][:, 0:T], onesB[:], CA[j][:, 0:T], start=(j == 0), stop=(j == 3)), ["onesB", "cacc%d" % j], [psk(P_ST)], sig=(j == 3))
            for j in range(4):
                t_, tk = tmp()
                ACT(lambda e, j=j, t_=t_: e.activation(t_[:], CA[j][:, 0:T], AF.Square), ["cacc%d" % j], [tk])
                PE(lambda e, j=j, t_=t_: e.matmul(PS[P_DS][:, 0:T], onesB[:], t_[:], start=(j == 0), stop=(j == 3)), ["onesB", tk], [psk(P_DS)], sig=(j == 3))
            ACT(lambda e: e.copy(lnm[:], PS[P_ST][:, 0:T]), [psk(P_ST)], ["lnm"])
            DVE(lambda e: e.tensor_tensor(lnr[:], lnm[:], lnm[:], ALU.mult), ["lnm"], ["lnr"])
            DVE(lambda e: e.tensor_tensor(lnr[:], PS[P_DS][:, 0:T], lnr[:], ALU.subtract), [psk(P_DS), "lnr"], ["lnr"])
            ACT(lambda e: e.activation(lnr[:], lnr[:], AF.Sqrt, bias=eps_t[:, 0:1]), ["lnr", "eps"], ["lnr"])
            DVE(lambda e: e.reciprocal(lnr[:], lnr[:]), ["lnr"], ["lnr"])
            for j in range(4):
                t_, tk = tmp()
                POOL(lambda e, j=j, t_=t_: e.tensor_tensor(t_[:], CA[j][:, 0:T], lnm[:], ALU.subtract), ["cacc%d" % j, "lnm"], [tk])
                DVE(lambda e, t_=t_: e.tensor_tensor(t_[:], t_[:], lnr[:], ALU.mult), [tk, "lnr"], [tk])
                ACT(lambda e, j=j, t_=t_: e.activation(t_[:], t_[:], AF.Identity, bias=pp[:, l, O_CLB + j:O_CLB + j + 1],
                                                       scale=pp[:, l, O_CLG + j:O_CLG + j + 1]), [tk, "pp"], [tk])
                ACT(lambda e, j=j, t_=t_: e.activation(uT[2][:, j, :], t_[:], AF.Silu), [tk], [("u2T", j)])

        pump_state = {"gen": None}

        def pump(n):
            g = pump_state["gen"]
            if g is None:
                return
            for _ in range(n):
                try:
                    next(g)
                except StopIteration:
                    pump_state["gen"] = None
                    return

        lnm = sb("lnm", [128, T])
        lnr = sb("lnr", [128, T])

        def merge_and_out(l, ti):
            yT = [Bb[0], Bb[1]]
            for b in range(3):
                src = wbrb[l].rearrange("(k p) c -> p k c", p=128)[:, b * 4:(b + 1) * 4, :]
                wbv, wbk = load_w(src, 4, 1024, [("wbrb", l, r, cc) for r in range(b * 4, b * 4 + 4) for cc in range(2)])
                ukeys = [("u%dT" % b, c) for c in range(4)]
                for half in range(2):
                    wg, wgk = win_group(l, 10 + 2 * b + half)
                    for jj in range(4):
                        f = half * 4 + jj
                        pg, pb_ = pm(), pm()
                        proj_fm(wg, wgk, jj, pg)
                        mm(PS[pb_][:, 0:T], psk(pb_),
                           [(wbv[:, k, f * 128:(f + 1) * 128], uT[b][:, k, :], [wbk] + ukeys) for k in range(4)])
                        sg, sgk = tmp()
                        ACT(lambda e, sg=sg, pg=pg: e.activation(sg[:], PS[pg][:, 0:T], AF.Sigmoid), [psk(pg)], [sgk])
                        if b == 0:
                            DVE(lambda e, sg=sg, pb_=pb_, f=f: e.tensor_tensor(Y[f][:], PS[pb_][:, 0:T], sg[:], ALU.mult), [psk(pb_), sgk], ["Y%d" % f])
                        else:
                            DVE(lambda e, sg=sg, pb_=pb_: e.tensor_tensor(sg[:], PS[pb_][:, 0:T], sg[:], ALU.mult), [psk(pb_), sgk], [sgk])
                            if b == 1:
                                POOL(lambda e, sg=sg, f=f: e.tensor_tensor(Y[f][:], Y[f][:], sg[:], ALU.add), ["Y%d" % f, sgk], ["Y%d" % f])
                            else:
                                POOL(lambda e, sg=sg, f=f: e.tensor_tensor(yT[f // 4][:, f % 4, :], Y[f][:], sg[:], ALU.add),
                                     ["Y%d" % f, sgk], [("B%d" % (f // 4), f % 4)])
            dump("y_%d_%d" % (l, ti), Bb[0][:], [128, 4, T], BF16, keys=[("B0", j) for j in range(4)])
            YK = [("B%d" % (f // 4), f % 4) for f in range(8)]
            for half in range(2):
                src = woutb[l].rearrange("(k p) c -> p k c", p=128)[:, :, half * 512:(half + 1) * 512]
                wo, wok = load_w(src, 8, 512, wkeys("woutb", l, D, half * 512))
                for jj in range(4):
                    f = half * 4 + jj
                    pz = pm()
                    mm(PS[pz][:, 0:T], psk(pz),
                       [(wo[:, k, jj * 128:(jj + 1) * 128], yT[k // 4][:, k % 4, :], [wok, YK[k]]) for k in range(8)])
                    residual(l, f, pz, 16)
            for f in range(8):
                ln_stats_chunk(f)
            layer_norm_stream(l, O_LN1G, O_LN1B)

        def ffn(l, ti):
            def gt(j):
                return Bb[j // 4][:, j % 4, :], ("B%d" % (j // 4), j % 4)

            def conv3(pbank, m):
                i = state["pb"]
                state["pb"] = (i + 1) % 7
                if i < 3:
                    pbf, pbk = pbuf[i], "pbuf%d" % i
                else:
                    pbf, pbk = CA[i - 3], "cacc%d" % (i - 3)
                cw = O_FCW + m * 3
                POOL(lambda e: e.tensor_copy(pbf[:, 0:2], halo3[l][:, m, :]), [("halo3_%d" % l, m)], [pbk])
                ACT(lambda e: e.copy(pbf[:, 2:T + 2], PS[pbank][:, 0:T]), [psk(pbank), pbk], [pbk])
                POOL(lambda e: e.tensor_copy(halo3[l][:, m, :], pbf[:, T:T + 2]), [pbk], [("halo3_%d" % l, m)])
                r_, rk = tmp()
                ACT(lambda e: e.activation(r_[:], PS[pbank][:, 0:T], AF.Identity, bias=pp[:, l, O_FCB + m:O_FCB + m + 1],
                                           scale=pp[:, l, cw + 2:cw + 3]), [psk(pbank), "pp"], [rk])
                DVE(lambda e: e.scalar_tensor_tensor(r_[:], pbf[:, 1:T + 1], pp[:, l, cw + 1:cw + 2], r_[:], ALU.mult, ALU.add), [pbk, "pp", rk], [rk])
                DVE(lambda e: e.scalar_tensor_tensor(r_[:], pbf[:, 0:T], pp[:, l, cw:cw + 1], r_[:], ALU.mult, ALU.add), [pbk, "pp", rk], [rk])
                return r_, rk

            for m in range(11):
                src = wupb[l].rearrange("(k p) c -> p k c", p=128)[:, :, m * 512:(m + 1) * 512]
                wu, wuk = load_w(src, 8, 512, wkeys("wupb", l, D, m * 512))
                for jj in range(2):
                    j = 2 * m + jj
                    pa, pv = pm(), pm()
                    proj_fm(wu, wuk, jj, pa)
                    proj_fm(wu, wuk, 2 + jj, pv)
                    ra, rak = conv3(pa, j)
                    rv, rvk = conv3(pv, 22 + j)
                    ACT(lambda e, ra=ra: e.activation(ra[:], ra[:], AF.Silu), [rak], [rak])
                    gd, gk = gt(j)
                    POOL(lambda e, ra=ra, rv=rv, gd=gd: e.tensor_tensor(gd, ra[:], rv[:], ALU.mult), [rak, rvk], [gk])
            GK = [gt(j)[1] for j in range(22)]
            for f in range(8):
                src = wdnb[l].rearrange("(k p) c -> p k c", p=128)[:, :, f * 128:(f + 1) * 128]
                wd, wdk = load_w(src, 22, 128, wkeys("wdnb", l, D_FF, (f // 4) * 512))
                pz = pm()
                mm(PS[pz][:, 0:T], psk(pz),
                   [(wd[:, k, :], gt(k)[0], [wdk, GK[k]]) for k in range(22)])
                residual(l, f, pz, 40)
            for f in range(8):
                ln_stats_chunk(f)
            layer_norm_stream(l, O_LN2G, O_LN2B)

        XK = [("xT", f) for f in range(8)]

        def stage(name):
            if upto == name:
                raise _StopBuild()

        try:
          stage("setup")
          for ti in range(NT):
              for c in range(NB):
                  xi = xin[c % 2]
                  xk = "xin%d" % (c % 2)
                  r0 = ti * T + c * 128
                  S.dma("sp", lambda e, xi=xi, r0=r0: e.dma_start(out=xi[:], in_=x_d[r0:r0 + 128, :]), rd=[], wr=[xk])
                  for half in range(2):
                      pa = pm()
                      for jj in range(4):
                          f = half * 4 + jj
                          PE(lambda e, xi=xi, f=f, jj=jj, pa=pa: e.transpose(PS[pa][:, jj * 128:(jj + 1) * 128], xi[:, f * 128:(f + 1) * 128], identf[:]),
                             [xk, "identf"], [psk(pa)], sig=(jj == 3))
                      ACT(lambda e, half=half, pa=pa, c=c: e.copy(xT[:, half * 4:half * 4 + 4, c * 128:(c + 1) * 128], v3(PS[pa][:, 0:512], 4)),
                          [psk(pa)], [("xT", half * 4 + jj) for jj in range(4)])
              stage("load")
              rope_tables(ti)
              stage("rope")
              if ti == 0:
                  dump("cos", cosT[:], [128, T], keys=["cosT"])
                  dump("sin", sinT[:], [128, T], keys=["sinT"])
              for l in range(L):
                  modulate(l, 0)
                  stage("mod")
                  S.dma("sp", lambda e, l=l: e.dma_start(out=bc[:, 0, :], in_=bc_d[l]), rd=[], wr=["bc"])
                  conv_part1(l, ti)
                  retention(l, ti)
                  dump("ua_%d_%d" % (l, ti), uT[0][:], [128, 4, T], BF16, keys=[("u0T", c) for c in range(4)])
                  stage("ret")
                  hgrn(l, ti)
                  dump("ub_%d_%d" % (l, ti), uT[1][:], [128, 4, T], BF16, keys=[("u1T", c) for c in range(4)])
                  stage("hgrn")
                  pump(1000)
                  conv_part3(l, ti)
                  dump("uc_%d_%d" % (l, ti), uT[2][:], [128, 4, T], BF16, keys=[("u2T", c) for c in range(4)])
                  stage("conv")
                  merge_and_out(l, ti)
                  dump("x1_%d_%d" % (l, ti), xT[:], [128, 8, T], keys=XK)
                  stage("merge")
                  modulate(l, 1)
                  ffn(l, ti)
                  dump("x2_%d_%d" % (l, ti), xT[:], [128, 8, T], keys=XK)
              for c in range(NB):
                  xi = xin[c % 2]
                  xk = "xin%d" % (c % 2)
                  r0 = ti * T + c * 128
                  for half in range(2):
                      pa = pm()
                      for jj in range(4):
                          f = half * 4 + jj
                          PE(lambda e, f=f, jj=jj, pa=pa, c=c: e.transpose(PS[pa][:, jj * 128:(jj + 1) * 128], xT[:, f, c * 128:(c + 1) * 128], identf[:]),
                             [("xT", f), "identf"], [psk(pa)], sig=(jj == 3))
                      ACT(lambda e, xi=xi, half=half, pa=pa: e.copy(xi[:, half * 512:(half + 1) * 512], PS[pa][:, 0:512]), [psk(pa)], [xk])
                  S.dma("pool", lambda e, xi=xi, r0=r0: e.dma_start(out=out_d[r0:r0 + 128, :], in_=xi[:]), rd=[xk], wr=[("out", r0)])

        except _StopBuild:
            pass

        S.wait_all("sp", S.final_tokens())
        print("ops", S.n_op, "waits", S.n_wait, {e: len(v) for e, v in S.prog.items()}, "sems", len(S.sems))
        S.emit()
    return nc, dbg_outs


def _host_prep(inputs, S_LEN):
    f32 = np.float32
    w_in = np.asarray(inputs["w_in"], f32)
    perm = np.concatenate([h * 128 + (np.arange(128) + 64) % 128 for h in range(NH)])
    w_rot = np.concatenate([w_in[:, :, 0:512][:, :, perm], w_in[:, :, 512:1024][:, :, perm]], axis=2)
    w_up = np.asarray(inputs["ffn_w_up"], f32)
    cols = []
    for m in range(11):
        for jj in range(2):
            cols.append(np.arange((2 * m + jj) * 128, (2 * m + jj + 1) * 128))
        for jj in range(2):
            cols.append(D_FF + np.arange((2 * m + jj) * 128, (2 * m + jj + 1) * 128))
    w_up_p = w_up[:, :, np.concatenate(cols)]

    def pcol(v, n):
        return np.asarray(v, f32).reshape(n, 128).T

    pp = np.zeros((DEPTH, 128, NPP), f32)
    bc = np.zeros((DEPTH, 128, 1024), f32)
    for l in range(DEPTH):
        pp[l, :, O_BADA:O_BADA + 48] = pcol(inputs["b_ada"][l], 48)
        pp[l, :, O_LN1G:O_LN1G + 8] = pcol(inputs["ln1_g"][l], 8)
        pp[l, :, O_LN1B:O_LN1B + 8] = pcol(inputs["ln1_b"][l], 8)
        pp[l, :, O_LN2G:O_LN2G + 8] = pcol(inputs["ln2_g"][l], 8)
        pp[l, :, O_LN2B:O_LN2B + 8] = pcol(inputs["ln2_b"][l], 8)
        cw = np.asarray(inputs["conv_w"][l], f32).reshape(CONV_K, 4, 128).transpose(2, 1, 0)
        pp[l, :, O_CW:O_CW + 124] = cw.reshape(128, 124)
        pp[l, :, O_CB:O_CB + 4] = pcol(inputs["conv_b"][l], 4)
        pp[l, :, O_CLG:O_CLG + 4] = pcol(inputs["conv_ln_g"][l], 4)
        pp[l, :, O_CLB:O_CLB + 4] = pcol(inputs["conv_ln_b"][l], 4)
        for l2 in range(DEPTH):
            pp[l, :, O_LBL + 4 * l2:O_LBL + 4 * l2 + 4] = pcol(inputs["hgrn_lb_logits"][l2], 4)
        fw = np.asarray(inputs["ffn_conv_w"][l], f32).reshape(3, NFC, 128).transpose(2, 1, 0)
        pp[l, :, O_FCW:O_FCW + 132] = fw.reshape(128, 132)
        pp[l, :, O_FCB:O_FCB + NFC] = pcol(inputs["ffn_conv_b"][l], NFC)
        bc[l, :, 0:512] = np.broadcast_to(np.asarray(inputs["ret_norm_g"][l], f32)[None, :], (128, 512))
        bc[l, :, 512:1024] = np.broadcast_to(np.asarray(inputs["hgrn_norm_g"][l], f32)[None, :], (128, 512))

    cst = np.zeros((128, NCONST), np.float64)
    idx = np.arange(128)
    for h in range(NH):
        g = 1.0 - 2.0 ** (-5 - h)
        rel = idx[None, :] - idx[:, None]
        cst[:, C_MASK + h * 128:C_MASK + (h + 1) * 128] = np.where(rel >= 0, g ** np.maximum(rel, 0), 0.0)
        cst[:, C_QD + h * 128:C_QD + (h + 1) * 128] = (g ** (idx + 1.0))[None, :]
        cst[:, C_KD + h] = g ** (127.0 - idx)
    cst[:, C_BD:C_BD + 128] = ((idx[:, None] // 32 == idx[None, :] // 32) & (idx[None, :] >= idx[:, None]))
    half = 64
    inv = (np.float32(10000.0) ** (-np.arange(half, dtype=np.float32) / np.float32(half))).astype(np.float32)
    cst[:, C_INV] = inv[idx % 64]
    cst[:, C_SGN] = np.where(idx < 64, -1.0, 1.0)
    cst[:, C_D0:C_D0 + 512] = (np.arange(512) % 32 != 0)[None, :]
    shared = {
        "w_ada": np.ascontiguousarray(inputs["w_ada"], f32),
        "w_in": np.ascontiguousarray(w_in),
        "w_rot": np.ascontiguousarray(w_rot),
        "w_branch": np.ascontiguousarray(np.asarray(inputs["w_branch"], f32).reshape(DEPTH, 3 * W, D)),
        "w_out": np.ascontiguousarray(inputs["w_out"], f32),
        "w_up": np.ascontiguousarray(w_up_p),
        "w_down": np.ascontiguousarray(inputs["ffn_w_down"], f32),
        "pp": pp, "bc": bc, "consts": cst.astype(f32),
    }
    return shared


_NC_CACHE = {}


def kernel(x, c, positions, w_ada, b_ada, w_in, ret_norm_g, hgrn_lb_logits, hgrn_norm_g,
           conv_w, conv_b, conv_ln_g, conv_ln_b, w_branch, w_out, ln1_g, ln1_b,
           ffn_w_up, ffn_conv_w, ffn_conv_b, ffn_w_down, ln2_g, ln2_b):
    inputs = dict(x=x, c=c, positions=positions, w_ada=w_ada, b_ada=b_ada, w_in=w_in, ret_norm_g=ret_norm_g,
                  hgrn_lb_logits=hgrn_lb_logits, hgrn_norm_g=hgrn_norm_g, conv_w=conv_w, conv_b=conv_b,
                  conv_ln_g=conv_ln_g, conv_ln_b=conv_ln_b, w_branch=w_branch, w_out=w_out, ln1_g=ln1_g, ln1_b=ln1_b,
                  ffn_w_up=ffn_w_up, ffn_conv_w=ffn_conv_w, ffn_conv_b=ffn_conv_b, ffn_w_down=ffn_w_down,
                  ln2_g=ln2_g, ln2_b=ln2_b)
    inputs = {k: np.asarray(v) for k, v in inputs.items()}
    B, S_LEN, _ = inputs["x"].shape
    shared = _host_prep(inputs, S_LEN)
    if S_LEN not in _NC_CACHE:
        _NC_CACHE[S_LEN] = build(S_LEN)[0]
    nc = _NC_CACHE[S_LEN]
    in_maps = []
    for b in range(B):
        m = dict(shared)
        m["x"] = np.ascontiguousarray(inputs["x"][b], np.float32)
        m["pos"] = np.ascontiguousarray(inputs["positions"][b][None, :], np.int32)
        m["cvec"] = np.ascontiguousarray(np.asarray(inputs["c"][b], np.float32).reshape(8, 128).T)
        in_maps.append(m)
    res = run_bass_kernel_spmd(nc, in_maps, core_ids=list(range(B)))
    return np.stack([np.asarray(r["out"], np.float32) for r in res.results], axis=0)
```
